# Optimizing a Trainium2 kernel written in Bass

```python
import math
import jax, jax.numpy as jnp
from jax import lax
import numpy as np

D_MODEL = 1024
BATCH = 4
SEQ = 8192
DEPTH = 2
DEC_BATCH = 8
DEC_SEQ = 64
PAST_LEN = 2048

CHUNK = 64
Q_BLOCK = 128
HEAD_DIM = 64
EPS = 1e-6
NEG_INF = -1e30
SCALE = HEAD_DIM ** -0.5

A_HEADS = D_MODEL // 256
A_WIDTH = A_HEADS * 2 * HEAD_DIM
B_HEADS = D_MODEL // 128
B_WIDTH = B_HEADS * HEAD_DIM
B_PREV_CHUNKS = 8
B_WIN = B_PREV_CHUNKS * CHUNK
B_BAND = B_WIN + CHUNK
B_REL_CLIP = 128
AB_SIZES = (A_WIDTH, A_WIDTH, A_WIDTH, B_WIDTH, B_WIDTH, B_WIDTH)

C_HEADS = D_MODEL // 128
C_WIDTH = C_HEADS * HEAD_DIM
D_WIDTH = D_MODEL // 2
D_BLOCKS = 8
D_BLOCK_DIM = D_WIDTH // D_BLOCKS
D_CONV = 4
RG_C = 8.0
CD_SIZES = (C_WIDTH, C_WIDTH, C_WIDTH, C_HEADS, D_WIDTH, D_WIDTH)

MIX_WIDTH = A_WIDTH + B_WIDTH
FFN_HIDDEN = ((8 * D_MODEL // 3 + 255) // 256) * 256

kernel_name = 'hybrid_stream_encoder_step'

F32 = jnp.float32


def lambda_init(layer):
    return 0.8 - 0.6 * math.exp(-0.3 * layer)


def alibi_slopes(n):
    return jnp.asarray([2.0 ** (-8.0 * (h + 1) / n) for h in range(n)], dtype=F32)


def split_cols(x, sizes):
    cuts = [int(c) for c in np.cumsum(sizes)[:-1]]
    return jnp.split(x, cuts, axis=-1)


def rmsnorm(x, g):
    xf = x.astype(F32)
    y = xf * lax.rsqrt(jnp.mean(xf * xf, axis=-1, keepdims=True) + EPS) * g.astype(F32)
    return y.astype(x.dtype)


def swiglu(h, w1, w3, w2):
    return (jax.nn.silu(h @ w1) * (h @ w3)) @ w2


def sweep_query_blocks(fn, q_arrays, q_pos):
    nb = q_pos.shape[0] // Q_BLOCK
    def split(a):
        return jnp.moveaxis(a.reshape((a.shape[0], nb, Q_BLOCK) + a.shape[2:]), 1, 0)
    blocks = tuple(split(a) for a in q_arrays) + (q_pos.reshape(nb, Q_BLOCK),)
    out = lax.map(lambda args: fn(*args), blocks)
    out = jnp.moveaxis(out, 0, 1)
    return out.reshape((out.shape[0], nb * Q_BLOCK) + out.shape[3:])


def diff_attn_core(q, k, v, q_pos, k_pos, lam):
    dist = jnp.abs(q_pos[:, None] - k_pos[None, :]).astype(F32)
    bias = -alibi_slopes(A_HEADS)[:, None, None] * dist
    mask = (k_pos[None, :] // CHUNK) <= (q_pos[:, None] // CHUNK)
    s = jnp.einsum('bqhcd,bkhcd->bchqk', q, k).astype(F32) * SCALE + bias
    p = jax.nn.softmax(jnp.where(mask, s, NEG_INF), axis=-1)
    w = p[:, 0] - lam * p[:, 1]
    return jnp.einsum('bhqk,bkhe->bqhe', w.astype(v.dtype), v)


def rel_position_bias(table, rel):
    idx = jnp.clip(rel, -B_REL_CLIP, B_REL_CLIP) + B_REL_CLIP
    return table.astype(F32)[:, idx]


def band_core(q, k, v, bias, valid):
    s = jnp.einsum('bqhd,bkhd->bhqk', q, k).astype(F32) * SCALE + bias
    p = jax.nn.softmax(jnp.where(valid, s, NEG_INF), axis=-1)
    return jnp.einsum('bhqk,bkhd->bqhd', p.astype(v.dtype), v)


def band_attn_prompt(q, k, v, rel_table):
    nb, S, H, d = q.shape
    nc = S // CHUNK
    pad = ((0, 0), (B_WIN, 0), (0, 0), (0, 0))
    kp, vp = jnp.pad(k, pad), jnp.pad(v, pad)
    rel = jnp.arange(CHUNK)[:, None] + B_WIN - jnp.arange(B_BAND)[None, :]
    bias = rel_position_bias(rel_table, rel)
    qc = jnp.moveaxis(q.reshape(nb, nc, CHUNK, H, d), 1, 0)
    def one_chunk(args):
        c, qb = args
        start = c * CHUNK
        kb = lax.dynamic_slice_in_dim(kp, start, B_BAND, axis=1)
        vb = lax.dynamic_slice_in_dim(vp, start, B_BAND, axis=1)
        valid = (start - B_WIN + jnp.arange(B_BAND)) >= 0
        return band_core(qb, kb, vb, bias, valid)
    out = lax.map(one_chunk, (jnp.arange(nc), qc))
    return jnp.moveaxis(out, 0, 1).reshape(nb, S, H, d)


def fox_core(q, k, v, f_q, f_k, q_pos, k_pos):
    s = jnp.einsum('bqhd,bkhd->bhqk', q, k).astype(F32) * SCALE
    s = s + jnp.swapaxes(f_q, 1, 2)[..., :, None] - jnp.swapaxes(f_k, 1, 2)[..., None, :]
    mask = k_pos[None, :] <= q_pos[:, None]
    p = jax.nn.softmax(jnp.where(mask, s, NEG_INF), axis=-1)
    return jnp.einsum('bhqk,bkhd->bqhd', p.astype(v.dtype), v)


def causal_conv(u, buf, w, b):
    up = jnp.concatenate([buf.astype(u.dtype), u], axis=1)
    y = lax.conv_general_dilated(up, w[:, None, :].astype(u.dtype), window_strides=(1,), padding='VALID',
                                 dimension_numbers=('NWC', 'WIO', 'NWC'), feature_group_count=u.shape[-1])
    return y + b.astype(u.dtype), up[:, up.shape[1] - (D_CONV - 1):]


def rg_lru(u, h0, w_a, b_a, w_x, b_x, lam):
    nb, T, C = u.shape
    ub = u.reshape(nb, T, D_BLOCKS, D_BLOCK_DIM)
    r = jax.nn.sigmoid(jnp.einsum('btni,nij->btnj', ub, w_a) + b_a).reshape(nb, T, C)
    i = jax.nn.sigmoid(jnp.einsum('btni,nij->btnj', ub, w_x) + b_x).reshape(nb, T, C)
    log_a = -RG_C * r.astype(F32) * jax.nn.softplus(-lam.astype(F32))
    a = jnp.exp(log_a)
    b = jnp.sqrt(-jnp.expm1(2.0 * log_a)) * (i * u).astype(F32)
    b = b.at[:, 0].add(a[:, 0] * h0.astype(F32))
    def combine(left, right):
        a_l, b_l = left
        a_r, b_r = right
        return a_l * a_r, a_r * b_l + b_r
    _, h = lax.associative_scan(combine, (a, b), axis=1)
    return h, h[:, -1]


def ab_mixer(h, w_in, w_out, a_lambda, a_subln_g, b_rel_bias, layer, past):
    nb, T, _ = h.shape
    aq, ak, av, bq, bk, bv = split_cols(h @ w_in, AB_SIZES)
    aq = aq.reshape(nb, T, A_HEADS, 2, HEAD_DIM)
    ak = ak.reshape(nb, T, A_HEADS, 2 * HEAD_DIM)
    av = av.reshape(nb, T, A_HEADS, 2 * HEAD_DIM)
    bq = bq.reshape(nb, T, B_HEADS, HEAD_DIM)
    bk = bk.reshape(nb, T, B_HEADS, HEAD_DIM)
    bv = bv.reshape(nb, T, B_HEADS, HEAD_DIM)
    lam_init = lambda_init(layer)
    lq1, lk1, lq2, lk2 = a_lambda.astype(F32)
    lam = jnp.exp(jnp.sum(lq1 * lk1)) - jnp.exp(jnp.sum(lq2 * lk2)) + lam_init
    if past is None:
        pos = jnp.arange(T, dtype=jnp.int32)
        k_all = ak.reshape(nb, T, A_HEADS, 2, HEAD_DIM)
        o_a = sweep_query_blocks(lambda qb, pb: diff_attn_core(qb, k_all, av, pb, pos, lam), (aq,), pos)
        o_b = band_attn_prompt(bq, bk, bv, b_rel_bias)
        keep = min(B_WIN, T)
        new_state = (ak, av, bk[:, T - keep:], bv[:, T - keep:])
    else:
        cak, cav, cbk, cbv = past
        P, Lb = cak.shape[1], cbk.shape[1]
        q_pos = P + jnp.arange(T, dtype=jnp.int32)
        k_pos = jnp.arange(P + T, dtype=jnp.int32)
        k_all = jnp.concatenate([cak.astype(ak.dtype), ak], axis=1).reshape(nb, P + T, A_HEADS, 2, HEAD_DIM)
        v_all = jnp.concatenate([cav.astype(av.dtype), av], axis=1)
        o_a = diff_attn_core(aq, k_all, v_all, q_pos, k_pos, lam)
        kb = jnp.concatenate([cbk.astype(bk.dtype), bk], axis=1)
        vb = jnp.concatenate([cbv.astype(bv.dtype), bv], axis=1)
        rel = jnp.arange(T)[:, None] + Lb - jnp.arange(Lb + T)[None, :]
        o_b = band_core(bq, kb, vb, rel_position_bias(b_rel_bias, rel), jnp.ones((Lb + T,), dtype=bool))
        new_state = (ak, av, kb[:, T:], vb[:, T:])
    o_a = (rmsnorm(o_a, a_subln_g) * (1.0 - lam_init)).reshape(nb, T, A_WIDTH)
    o_b = o_b.reshape(nb, T, B_WIDTH)
    return jnp.concatenate([o_a, o_b], axis=-1) @ w_out, new_state


def cd_mixer(h, w_in, w_out, c_f_bias, d_conv_w, d_conv_b, d_w_a, d_b_a, d_w_x, d_b_x, d_lambda, past):
    nb, T, _ = h.shape
    cq, ck, cv, cf, dx, dg = split_cols(h @ w_in, CD_SIZES)
    cq = cq.reshape(nb, T, C_HEADS, HEAD_DIM)
    ck = ck.reshape(nb, T, C_HEADS, HEAD_DIM)
    cv = cv.reshape(nb, T, C_HEADS, HEAD_DIM)
    logf = jax.nn.log_sigmoid(cf.astype(F32) + c_f_bias.astype(F32))
    if past is None:
        pos = jnp.arange(T, dtype=jnp.int32)
        F = jnp.cumsum(logf, axis=1)
        o_c = sweep_query_blocks(lambda qb, fb, pb: fox_core(qb, ck, cv, fb, F, pb, pos), (cq, F), pos)
        conv_buf = jnp.zeros((nb, D_CONV - 1, D_WIDTH), dx.dtype)
        h0 = jnp.zeros((nb, D_WIDTH), F32)
    else:
        cck, ccv, cclogf, conv_buf, h0 = past
        P = cck.shape[1]
        F = jnp.cumsum(jnp.concatenate([cclogf.astype(F32), logf], axis=1), axis=1)
        k_all = jnp.concatenate([cck.astype(ck.dtype), ck], axis=1)
        v_all = jnp.concatenate([ccv.astype(cv.dtype), cv], axis=1)
        o_c = fox_core(cq, k_all, v_all, F[:, P:], F, P + jnp.arange(T, dtype=jnp.int32),
                       jnp.arange(P + T, dtype=jnp.int32))
    u, new_buf = causal_conv(dx, conv_buf, d_conv_w, d_conv_b)
    hseq, h_last = rg_lru(u, h0, d_w_a, d_b_a, d_w_x, d_b_x, d_lambda)
    o_d = hseq.astype(dx.dtype) * jax.nn.gelu(dg)
    out = jnp.concatenate([o_c.reshape(nb, T, C_WIDTH), o_d], axis=-1) @ w_out
    return out, (ck, cv, logf, new_buf, h_last.astype(dx.dtype))


def setup_inputs(seed: int = 0) -> dict:
    key = jax.random.key(seed)
    ks = iter(jax.random.split(key, 40))
    def nrm(shape, scale=1.0):
        return jax.random.normal(next(ks), shape, F32) * scale
    b_buf = min(B_WIN, PAST_LEN)
    x_prompt = nrm((BATCH, SEQ, D_MODEL))
    x_sample = nrm((DEC_BATCH, DEC_SEQ, D_MODEL))
    cache_a_k = nrm((DEC_BATCH, PAST_LEN, A_HEADS, 2 * HEAD_DIM))
    cache_a_v = nrm((DEC_BATCH, PAST_LEN, A_HEADS, 2 * HEAD_DIM))
    cache_b_k = nrm((DEC_BATCH, b_buf, B_HEADS, HEAD_DIM))
    cache_b_v = nrm((DEC_BATCH, b_buf, B_HEADS, HEAD_DIM))
    cache_c_k = nrm((DEC_BATCH, PAST_LEN, C_HEADS, HEAD_DIM))
    cache_c_v = nrm((DEC_BATCH, PAST_LEN, C_HEADS, HEAD_DIM))
    cache_c_logf = jax.nn.log_sigmoid(3.0 + nrm((DEC_BATCH, PAST_LEN, C_HEADS)))
    state_d_conv = nrm((DEC_BATCH, D_CONV - 1, D_WIDTH))
    state_d_h = nrm((DEC_BATCH, D_WIDTH), 0.5)
    norm_mix_g = 1.0 + nrm((DEPTH, D_MODEL), 0.02)
    norm_ffn_g = 1.0 + nrm((DEPTH, D_MODEL), 0.02)
    ab_w_in = nrm((D_MODEL, sum(AB_SIZES)), D_MODEL ** -0.5)
    ab_w_out = nrm((MIX_WIDTH, D_MODEL), MIX_WIDTH ** -0.5)
    a_lambda = nrm((4, HEAD_DIM), 0.1)
    a_subln_g = 1.0 + nrm((2 * HEAD_DIM,), 0.02)
    b_rel_bias = nrm((B_HEADS, 2 * B_REL_CLIP + 1), 0.2)
    cd_w_in = nrm((D_MODEL, sum(CD_SIZES)), D_MODEL ** -0.5)
    cd_w_out = nrm((MIX_WIDTH, D_MODEL), MIX_WIDTH ** -0.5)
    c_f_bias = jnp.linspace(1.0, 6.0, C_HEADS, dtype=F32) + nrm((C_HEADS,), 0.1)
    d_conv_w = nrm((D_CONV, D_WIDTH), D_CONV ** -0.5)
    d_conv_b = nrm((D_WIDTH,), 0.01)
    d_w_a = nrm((D_BLOCKS, D_BLOCK_DIM, D_BLOCK_DIM), D_BLOCK_DIM ** -0.5)
    d_b_a = nrm((D_BLOCKS, D_BLOCK_DIM), 0.01)
    d_w_x = nrm((D_BLOCKS, D_BLOCK_DIM, D_BLOCK_DIM), D_BLOCK_DIM ** -0.5)
    d_b_x = nrm((D_BLOCKS, D_BLOCK_DIM), 0.01)
    a_c = jax.random.uniform(next(ks), (D_WIDTH,), F32, 0.9, 0.999) ** (1.0 / RG_C)
    d_lambda = jnp.log(a_c) - jnp.log1p(-a_c)
    ffn_w1 = nrm((DEPTH, D_MODEL, FFN_HIDDEN), D_MODEL ** -0.5)
    ffn_w3 = nrm((DEPTH, D_MODEL, FFN_HIDDEN), D_MODEL ** -0.5)
    ffn_w2 = nrm((DEPTH, FFN_HIDDEN, D_MODEL), FFN_HIDDEN ** -0.5)
    final_g = 1.0 + nrm((D_MODEL,), 0.02)
    return {'x_prompt': x_prompt, 'x_sample': x_sample,
            'cache_a_k': cache_a_k, 'cache_a_v': cache_a_v, 'cache_b_k': cache_b_k, 'cache_b_v': cache_b_v,
            'cache_c_k': cache_c_k, 'cache_c_v': cache_c_v, 'cache_c_logf': cache_c_logf,
            'state_d_conv': state_d_conv, 'state_d_h': state_d_h,
            'norm_mix_g': norm_mix_g, 'norm_ffn_g': norm_ffn_g,
            'ab_w_in': ab_w_in, 'ab_w_out': ab_w_out, 'a_lambda': a_lambda, 'a_subln_g': a_subln_g,
            'b_rel_bias': b_rel_bias, 'cd_w_in': cd_w_in, 'cd_w_out': cd_w_out, 'c_f_bias': c_f_bias,
            'd_conv_w': d_conv_w, 'd_conv_b': d_conv_b, 'd_w_a': d_w_a, 'd_b_a': d_b_a,
            'd_w_x': d_w_x, 'd_b_x': d_b_x, 'd_lambda': d_lambda,
            'ffn_w1': ffn_w1, 'ffn_w3': ffn_w3, 'ffn_w2': ffn_w2, 'final_g': final_g}


def reference(x_prompt, x_sample, cache_a_k, cache_a_v, cache_b_k, cache_b_v, cache_c_k, cache_c_v,
              cache_c_logf, state_d_conv, state_d_h, norm_mix_g, norm_ffn_g, ab_w_in, ab_w_out, a_lambda,
              a_subln_g, b_rel_bias, cd_w_in, cd_w_out, c_f_bias, d_conv_w, d_conv_b, d_w_a, d_b_a,
              d_w_x, d_b_x, d_lambda, ffn_w1, ffn_w3, ffn_w2, final_g):
    yp, ys = x_prompt, x_sample
    for layer in range(DEPTH):
        hp = rmsnorm(yp, norm_mix_g[layer])
        hs = rmsnorm(ys, norm_mix_g[layer])
        if layer % 2 == 0:
            mp, (pa_k, pa_v, pb_k, pb_v) = ab_mixer(hp, ab_w_in, ab_w_out, a_lambda, a_subln_g, b_rel_bias,
                                                    layer, None)
            ms, (sa_k, sa_v, sb_k, sb_v) = ab_mixer(hs, ab_w_in, ab_w_out, a_lambda, a_subln_g, b_rel_bias,
                                                    layer, (cache_a_k, cache_a_v, cache_b_k, cache_b_v))
        else:
            mp, (pc_k, pc_v, pc_logf, pd_conv, pd_h) = cd_mixer(
                hp, cd_w_in, cd_w_out, c_f_bias, d_conv_w, d_conv_b, d_w_a, d_b_a, d_w_x, d_b_x, d_lambda, None)
            ms, (sc_k, sc_v, sc_logf, sd_conv, sd_h) = cd_mixer(
                hs, cd_w_in, cd_w_out, c_f_bias, d_conv_w, d_conv_b, d_w_a, d_b_a, d_w_x, d_b_x, d_lambda,
                (cache_c_k, cache_c_v, cache_c_logf, state_d_conv, state_d_h))
        yp = yp + mp
        ys = ys + ms
        yp = yp + swiglu(rmsnorm(yp, norm_ffn_g[layer]), ffn_w1[layer], ffn_w3[layer], ffn_w2[layer])
        ys = ys + swiglu(rmsnorm(ys, norm_ffn_g[layer]), ffn_w1[layer], ffn_w3[layer], ffn_w2[layer])
    y_prompt = rmsnorm(yp, final_g)
    y_sample = rmsnorm(ys, final_g)
    return (y_prompt, y_sample,
            pa_k, pa_v, pb_k, pb_v, pc_k, pc_v, pc_logf, pd_conv, pd_h,
            sa_k, sa_v, sb_k, sb_v, sc_k, sc_v, sc_logf, sd_conv, sd_h)
```

```python
import math
import numpy as np
import ml_dtypes
import concourse.bass as bass
import concourse.mybir as mybir
from concourse.bass_utils import run_bass_kernel_spmd

F32 = mybir.dt.float32
BF16 = mybir.dt.bfloat16
AF = mybir.ActivationFunctionType
ALU = mybir.AluOpType
AX = mybir.AxisListType

D = 1024
HID = 2816
TT = 512
DEC = 64
CH = 1024
NEG = -1.0e30
EPS = 1e-6
NDS = 24


class Op:
    __slots__ = ("eng", "fn", "dma", "deps", "sig", "val", "sem", "prev")


class Prog:
    ENGS = ("pe", "act", "dve", "pool", "sp")

    def __init__(self, nc):
        self.nc = nc
        self.ops = {e: [] for e in self.ENGS}
        self.lastw = {}
        self.readers = {}

    def add(self, eng, fn, r=(), w=(), dma=False):
        op = Op()
        op.eng = eng; op.fn = fn; op.dma = dma; op.sig = False; op.val = 0; op.sem = None; op.prev = 0
        deps = {}
        for k in r:
            lw = self.lastw.get(k)
            if lw is not None:
                deps[lw] = True
        for k in w:
            lw = self.lastw.get(k)
            if lw is not None:
                deps[lw] = True
            for rd in self.readers.get(k, ()):
                deps.setdefault(rd, False)
        for k in r:
            lst = self.readers.setdefault(k, [])
            if not dma:
                lst[:] = [o for o in lst if o.dma or o.eng != eng]
            lst.append(op)
        for k in w:
            self.lastw[k] = op
            self.readers[k] = []
        fd = []
        for d, strong in deps.items():
            if d is op:
                continue
            if d.eng == eng and not d.dma and not dma:
                if eng == "pe" or not strong:
                    continue
            d.sig = True
            fd.append(d)
        op.deps = fd
        self.ops[eng].append(op)
        return op

    def emit(self):
        nc = self.nc
        esem = {e: nc.alloc_semaphore("sem_" + e) for e in ("pe", "act", "dve", "pool")}
        dsem = {q: [nc.alloc_semaphore("dq_%s_%d" % (q, i)) for i in range(NDS)] for q in ("sp", "pool", "act")}
        final = {}
        for e in self.ENGS:
            c = 0
            di = 0
            dcount = [0] * NDS
            for op in self.ops[e]:
                if op.dma:
                    s = di % NDS
                    di += 1
                    op.prev = 16 * dcount[s]
                    dcount[s] += 1
                    op.sem = dsem[e][s]
                    op.val = 16 * dcount[s]
                    final[op.sem] = op.val
                elif op.sig:
                    c += 1
                    op.sem = esem[e]
                    op.val = c

        def run(e, eng):
            waited = {}
            for op in self.ops[e]:
                need = {}
                for d in op.deps:
                    if need.get(d.sem, 0) < d.val:
                        need[d.sem] = d.val
                if op.dma and op.prev > 0 and need.get(op.sem, 0) < op.prev:
                    need[op.sem] = op.prev
                for s, v in need.items():
                    if waited.get(s, 0) < v:
                        eng.wait_ge(s, v)
                        waited[s] = v
                ins = op.fn(eng)
                if op.dma:
                    ins.then_inc(op.sem, 16)
                elif op.sig:
                    ins.then_inc(op.sem, 1)
            if e == "sp":
                for s, v in final.items():
                    if waited.get(s, 0) < v:
                        eng.wait_ge(s, v)

        with nc.Block() as block:
            @block.tensor
            def _(eng):
                run("pe", eng)

            @block.scalar
            def _(eng):
                run("act", eng)

            @block.vector
            def _(eng):
                run("dve", eng)

            @block.gpsimd
            def _(eng):
                run("pool", eng)

            @block.sync
            def _(eng):
                run("sp", eng)


def _bf16r(a):
    return np.asarray(a, np.float32).astype(ml_dtypes.bfloat16).astype(np.float32)


def make_consts(nda):
    c = {}
    c["ident_f"] = np.eye(128, dtype=np.float32)
    c["ident_b"] = np.eye(128, dtype=np.float32).astype(ml_dtypes.bfloat16)
    t = np.arange(128)
    c["tri_f"] = (t[:, None] <= t[None, :]).astype(np.float32)
    sel = np.zeros((128, 128), np.float32); sel[127, :] = 1.0
    c["sel_f"] = sel
    c["anti_f"] = np.ascontiguousarray(np.eye(128, dtype=np.float32)[::-1])
    i = np.arange(TT)
    ri = _bf16r(i.astype(np.float32))
    slopes = np.array([2.0 ** (-8.0 * (h + 1) / 4) for h in range(4)], np.float32)
    cq = np.zeros((1, 8, TT), np.float32)
    for h in range(4):
        for cc in range(2):
            cq[0, 2 * h + cc] = -slopes[h] * ri
    assert np.array_equal(_bf16r(cq), cq)
    c["cqA"] = cq.astype(ml_dtypes.bfloat16)
    j = np.arange(128)
    t0 = np.zeros((128, 4, TT), np.float32)
    cm = np.zeros((128, 4, TT), np.float32)
    for js in range(4):
        jp = 128 * js + j
        vis = (jp[:, None] // 64) <= (i[None, :] // 64)
        t0[:, js, :] = np.where(vis, ri[None, :] - np.abs(i[None, :] - jp[:, None]), -4.0e30)
        cm[:, js, :] = np.where(jp[:, None] <= i[None, :], 0.0, NEG)
    c["t0m"] = t0
    c["cmask"] = cm.astype(ml_dtypes.bfloat16)
    bk = np.zeros((128, 4, nda), np.float32)
    for h in range(4):
        bk[:, h, :] = slopes[h] * (j[:, None] - 128.0 * np.arange(nda)[None, :])
    c["bkA"] = bk
    bm = np.zeros((128, 8, TT), np.float32)
    dvals = [512, 384, 256, 128, 0, -128, -256, -384]
    for dk, dv in enumerate(dvals):
        kc_ = np.floor_divide(-dv + j, 64)
        qc_ = i // 64
        vis = (kc_[:, None] >= qc_[None, :] - 8) & (kc_[:, None] <= qc_[None, :])
        bm[:, dk, :] = np.where(vis, 0.0, NEG)
    c["bmask"] = bm.astype(ml_dtypes.bfloat16)
    return c, slopes, dvals


class _Stop(Exception):
    pass


def build(SEQ, PAST, stop=999):
    def chk(stage):
        if stage > stop:
            raise _Stop()
    NT = SEQ // TT
    NKT_P = SEQ // 128
    NKT_S = PAST // 128 + 1
    NDA = max(SEQ, PAST) // 128 + 2
    consts, slopes, dvals = make_consts(NDA)
    nc = bass.Bass("TRN2", target_bir_lowering=False)
    P = Prog(nc)

    def din(name, shape, dt=F32):
        return nc.dram_tensor(name, list(shape), dt, kind="ExternalInput").ap()

    def dout(name, shape):
        return nc.dram_tensor(name, list(shape), F32, kind="ExternalOutput").ap()

    def dscr(name, shape, dt=BF16):
        return nc.dram_tensor(name, list(shape), dt).ap()

    def sb(name, shape, dt=F32):
        return nc.alloc_sbuf_tensor(name, list(shape), dt)

    x_p = din("x_p", [SEQ, D]); x_s = din("x_s", [DEC, D])
    cak = din("cache_a_k", [PAST, 512]); cav = din("cache_a_v", [PAST, 512])
    cbk = din("cache_b_k", [512, 512]); cbv = din("cache_b_v", [512, 512])
    cck = din("cache_c_k", [PAST, 512]); ccv = din("cache_c_v", [PAST, 512])
    cclf = din("cache_c_logf", [PAST, 8])
    sdc = din("state_d_conv", [128, 4, 3]); sdh = din("state_d_h", [128, 4])
    gT_in = din("gT_in", [4, D]); fing = din("final_g", [D])
    ab_w_in = din("ab_w_in", [D, 3072]); ab_w_out = din("ab_w_out", [D, D])
    a_lambda = din("a_lambda", [4, 64]); a_subln = din("a_subln_g", [64, 2]); b_rel = din("b_rel_bias", [8, 257])
    cd_w_in = din("cd_w_in", [D, 2568]); cd_w_out = din("cd_w_out", [D, D]); c_f_bias = din("c_f_bias", [8])
    d_conv_w = din("d_conv_w", [128, 4, 4]); d_conv_b = din("d_conv_b", [128, 4])
    d_w_a = din("d_w_a", [8, 64, 64]); d_b_a = din("d_b_a", [128, 4]); d_w_x = din("d_w_x", [8, 64, 64])
    d_b_x = din("d_b_x", [128, 4]); d_lam = din("d_lambda", [128, 4])
    w1 = din("ffn_w1", [2, D, HID]); w3 = din("ffn_w3", [2, D, HID]); w2 = din("ffn_w2", [2, HID, D])
    cin = {}
    for k, v in consts.items():
        cin[k] = din("c_" + k, v.shape, BF16 if v.dtype == ml_dtypes.bfloat16 else F32)

    O = {}
    for g, T in (("p", SEQ), ("s", DEC)):
        O[g + "y"] = dout(g + "_y", [T, D])
        O[g + "a_k"] = dout(g + "_a_k", [T, 512]); O[g + "a_v"] = dout(g + "_a_v", [T, 512])
        O[g + "b_k"] = dout(g + "_b_k", [512, 512]); O[g + "b_v"] = dout(g + "_b_v", [512, 512])
        O[g + "c_k"] = dout(g + "_c_k", [T, 512]); O[g + "c_v"] = dout(g + "_c_v", [T, 512])
        O[g + "c_logf"] = dout(g + "_c_logf", [T, 8])
        O[g + "d_conv"] = dout(g + "_d_conv", [128, 4, 3]); O[g + "d_h"] = dout(g + "_d_h", [128, 4])

    WP = {}

    def wpanel(key, src_rows_fn, npart, nkc, ncols):
        t = dscr("wp_%s" % key, [npart, nkc, ncols])
        WP[key] = (t, npart, nkc, ncols)
        for kc in range(nkc):
            src = src_rows_fn(kc)
            P.add("pool", lambda e, o=t[:, kc, :], i=src: e.dma_start(out=o, in_=i), r=(), w=(("wp", key, kc),), dma=True)

    for pi in range(6):
        wpanel("abin%d" % pi, lambda kc, pi=pi: ab_w_in[kc * 128:(kc + 1) * 128, pi * 512:(pi + 1) * 512], 128, 8, 512)
    for nh in range(2):
        for kp in range(2):
            wpanel("about%d_%d" % (nh, kp),
                   lambda kc, nh=nh, kp=kp: ab_w_out[(kp * 8 + kc) * 64:(kp * 8 + kc + 1) * 64, nh * 512:(nh + 1) * 512], 64, 8, 512)
    for pi in range(3):
        wpanel("cdin%d" % pi, lambda kc, pi=pi: cd_w_in[kc * 128:(kc + 1) * 128, pi * 512:(pi + 1) * 512], 128, 8, 512)
    wpanel("cdcf", lambda kc: cd_w_in[kc * 128:(kc + 1) * 128, 1536:1544], 128, 8, 8)
    for pi in range(2):
        wpanel("cdin%d" % (3 + pi), lambda kc, pi=pi: cd_w_in[kc * 128:(kc + 1) * 128, 1544 + pi * 512:1544 + (pi + 1) * 512], 128, 8, 512)
    for nh in range(2):
        wpanel("cdoutc%d" % nh, lambda kc, nh=nh: cd_w_out[kc * 64:(kc + 1) * 64, nh * 512:(nh + 1) * 512], 64, 8, 512)
        wpanel("cdoutd%d" % nh, lambda kc, nh=nh: cd_w_out[512 + kc * 128:512 + (kc + 1) * 128, nh * 512:(nh + 1) * 512], 128, 4, 512)
    HP = [(0, 512), (512, 512), (1024, 512), (1536, 512), (2048, 512), (2560, 256)]
    KP2 = [(0, 8), (8, 8), (16, 6)]
    for L in range(2):
        for pi, (c0, cn) in enumerate(HP):
            wpanel("w1_%d_%d" % (L, pi), lambda kc, L=L, c0=c0, cn=cn: w1[L, kc * 128:(kc + 1) * 128, c0:c0 + cn], 128, 8, cn)
            wpanel("w3_%d_%d" % (L, pi), lambda kc, L=L, c0=c0, cn=cn: w3[L, kc * 128:(kc + 1) * 128, c0:c0 + cn], 128, 8, cn)
        for nh in range(2):
            for kp, (k0, kn) in enumerate(KP2):
                wpanel("w2_%d_%d_%d" % (L, nh, kp),
                       lambda kc, L=L, nh=nh, k0=k0: w2[L, (k0 + kc) * 128:(k0 + kc + 1) * 128, nh * 512:(nh + 1) * 512], 128, kn, 512)

    SCR = {}
    for g, Tt, nkt in (("p", SEQ, NKT_P), ("s", PAST + DEC, NKT_S)):
        for m in "ABC":
            SCR[(g, m, "k")] = dscr("kt_%s%s" % (g, m), [8, 64, Tt])
            if m == "A":
                SCR[(g, m, "v")] = dscr("v_%s%s" % (g, m), [4, 128, nkt, 130])
            else:
                SCR[(g, m, "v")] = dscr("v_%s%s" % (g, m), [8, 128, nkt, 65])
    text = dscr("text", [8, 1536], F32)
    bbias = dscr("bbias", [8, 128, 8, TT])

    ident_f = sb("ident_f", [128, 128]); ident_b = sb("ident_b", [128, 128], BF16)
    tri_f = sb("tri_f", [128, 128]); sel_f = sb("sel_f", [128, 128]); anti_f = sb("anti_f", [128, 128])
    ones_f = sb("ones_f", [128, 64]); ones_b = sb("ones_b", [128, 64], BF16)
    t0m = sb("t0m", [128, 4, TT]); cmask = sb("cmask", [128, 4, TT], BF16)
    bkA = sb("bkA", [128, 4, NDA])
    gbc = sb("gbc", [128, 4, D], BF16); gfin = sb("gfin", [128, D])
    cfb = sb("cfb", [128, 8])
    brt = sb("brt", [8, 257]); txs = sb("txs", [8, 1536])
    neglam = sb("neglam", [128, 1]); lamt = sb("lamt", [128, 4, 64]); lamp = sb("lamp", [128, 2, 64]); lams = sb("lams", [128, 2])
    subg = sb("subg", [64, 2])
    spt = sb("spt", [128, 4]); m8sp = sb("m8sp", [128, 4]); m16sp = sb("m16sp", [128, 4])
    convw = sb("convw", [128, 4, 4]); convb = sb("convb", [128, 4]); bat = sb("bat", [128, 4]); bxt = sb("bxt", [128, 4])
    bdf = sb("bdf", [128, 4, 128]); bda = sb("bda", [128, 4, 128], BF16); bdx = sb("bdx", [128, 4, 128], BF16)
    xres = sb("xres", [128, 4, D])
    xn = sb("xn", [128, D], BF16); sqj = sb("sqj", [128, D], BF16)
    ssum = sb("ssum", [128, 2]); rstd = sb("rstd", [128, 1])
    hT = sb("hT", [128, 8, TT], BF16)
    QT1 = sb("QT1", [128, 8, TT], BF16)
    QT = {"A": QT1, "B": QT1, "C": QT1}
    KTs = sb("KTs", [64, 8, TT], BF16)
    Vs = sb("Vs", [128, 4, 8, 65], BF16)
    kvo = [sb("kvo%d" % i, [128, 512]) for i in range(2)]
    odT = sb("odT", [128, 4, TT], BF16)
    GT = sb("GT", [128, 22, TT], BF16)
    OT = GT
    NW = 3
    wring = [sb("wring%d" % i, [128, 8, 512], BF16) for i in range(NW)]
    NKR = 3
    kring = [sb("kring%d" % i, [128, CH], BF16) for i in range(NKR)]
    vring = [sb("vring%d" % i, [128, CH // 128, 130], BF16) for i in range(NKR)]
    NPT = 4
    ptr = [sb("ptr%d" % i, [128, TT], BF16) for i in range(NPT)]
    dtmp = [sb("dtmp%d" % i, [128, TT]) for i in range(2)]
    bbr = [sb("bbr%d" % i, [128, 8, TT], BF16) for i in range(1)]
    rl = sb("rl", [65, TT])
    TB = [sb("tb%d" % i, [128, TT]) for i in range(8)]
    sq2 = sb("sq2", [64, 2, TT], BF16)
    negF = {"p": sb("negF_p", [128, NKT_P, 8]), "s": sb("negF_s", [128, NKT_S, 8])}
    lft = sb("lft", [128, 8]); lfo = sb("lfo", [128, 8]); lfb = sb("lfb", [128, 8], BF16); FTb = sb("FTb", [8, TT], BF16)
    dxbuf = {"p": sb("dxbuf_p", [128, 4, TT + 3]), "s": sb("dxbuf_s", [128, 4, DEC + 3])}
    hst = {"p": sb("hst_p", [128, 4]), "s": sb("hst_s", [128, 4])}
    ub = sb("ub", [128, TT], BF16)

    ps = [nc.alloc_psum_tensor("ps%d" % i, [128, 512], F32) for i in range(7)]
    ptb = nc.alloc_psum_tensor("ptb", [128, 8, 128], BF16)

    rot = {"sps": 0, "w": 0, "k": 0, "pt": 0, "dt": 0, "bb": 0, "kvo": 0, "ps": 0}

    def nxt(name, n):
        v = rot[name]
        rot[name] = (v + 1) % n
        return v

    def dma(q, out, in_, r, w, **kw):
        return P.add(q, lambda e: e.dma_start(out=out, in_=in_, **kw), r=r, w=w, dma=True)

    def load_w(key):
        t, npart, nkc, ncols = WP[key]
        s = nxt("w", NW)
        dma("sp", wring[s][0:npart, 0:nkc, 0:ncols], t[:, :, :], r=tuple(("wp", key, kc) for kc in range(nkc)), w=(("wr", s),))
        return s

    def prologue():
        for k, t in (("ident_f", ident_f), ("ident_b", ident_b), ("tri_f", tri_f), ("sel_f", sel_f), ("anti_f", anti_f), ("t0m", t0m),
                     ("cmask", cmask), ("bkA", bkA)):
            sl = tuple(slice(None) for _ in consts[k].shape)
            dma("sp", t[sl], cin[k][sl], r=(), w=(k,))
        P.add("pool", lambda e: e.memset(ones_f[:, :], 1.0), w=("ones_f",))
        P.add("pool", lambda e: e.memset(ones_b[:, :], 1.0), w=("ones_b",))
        P.add("pool", lambda e: e.memset(QT1[64:128, :, :], 0.0), w=("QT_z", "QT_aug"))
        for i in range(NKR):
            P.add("pool", lambda e, i=i: e.memset(kring[i][64:128, :], 0.0), w=(("kr1", i),))
            P.add("pool", lambda e, i=i: e.memset(kring[i][64:65, :], 1.0), w=(("kr1", i),))
            P.add("pool", lambda e, i=i: e.memset(vring[i][:, :, :], 0.0), w=(("vr0", i), ("vr", i)))
        P.add("pool", lambda e: e.memset(Vs[:, :, :, 64:65], 1.0), w=("Vs1",))
        for n in range(4):
            for hh in range(2):
                dma("sp", TB[hh][:, :], gT_in[n, hh * 512:(hh + 1) * 512].partition_broadcast(128), r=(), w=(("T", hh),))
                P.add("dve", lambda e, n=n, hh=hh: e.tensor_copy(out=gbc[:, n, hh * 512:(hh + 1) * 512], in_=TB[hh][:, :]), r=(("T", hh),), w=("gbc",))
        dma("sp", gfin[:, :], fing.partition_broadcast(128), r=(), w=("gfin",))
        dma("sp", cfb[:, :], c_f_bias.partition_broadcast(128), r=(), w=("cfb",))
        dma("sp", lamt[:, :, :], a_lambda.partition_broadcast(128), r=(), w=("lamt",))
        P.add("dve", lambda e: e.tensor_tensor(out=lamp[:, 0, :], in0=lamt[:, 0, :], in1=lamt[:, 1, :], op=ALU.mult), r=("lamt",), w=("lamp0",))
        P.add("dve", lambda e: e.tensor_tensor(out=lamp[:, 1, :], in0=lamt[:, 2, :], in1=lamt[:, 3, :], op=ALU.mult), r=("lamt",), w=("lamp1",))
        P.add("dve", lambda e: e.reduce_sum(out=lams[:, :], in_=lamp[:, :, :], axis=AX.X), r=("lamp0", "lamp1"), w=("lams",))
        P.add("act", lambda e: e.activation(out=lams[:, :], in_=lams[:, :], func=AF.Exp), r=("lams",), w=("lams",))
        lam_init = 0.8 - 0.6 * math.exp(0.0)
        P.add("dve", lambda e: e.tensor_tensor(out=neglam[:, :], in0=lams[:, 1:2], in1=lams[:, 0:1], op=ALU.subtract), r=("lams",), w=("neglam",))
        P.add("dve", lambda e: e.tensor_scalar(out=neglam[:, :], in0=neglam[:, :], scalar1=-lam_init, scalar2=None, op0=ALU.add), r=("neglam",), w=("neglam",))
        dma("sp", subg[:, :], a_subln[:, :], r=(), w=("subg",))
        P.add("dve", lambda e: e.tensor_scalar(out=subg[:, :], in0=subg[:, :], scalar1=1.0 - lam_init, scalar2=None, op0=ALU.mult), r=("subg",), w=("subg",))
        dma("sp", spt[:, :], d_lam[:, :], r=(), w=("spt",))
        P.add("act", lambda e: e.activation(out=spt[:, :], in_=spt[:, :], func=AF.Exp, scale=-1.0), r=("spt",), w=("spt",))
        P.add("act", lambda e: e.activation(out=spt[:, :], in_=spt[:, :], func=AF.Ln, bias=1.0), r=("spt",), w=("spt",))
        P.add("dve", lambda e: e.tensor_scalar(out=m8sp[:, :], in0=spt[:, :], scalar1=-8.0, scalar2=None, op0=ALU.mult), r=("spt",), w=("m8sp",))
        P.add("dve", lambda e: e.tensor_scalar(out=m16sp[:, :], in0=spt[:, :], scalar1=-16.0, scalar2=None, op0=ALU.mult), r=("spt",), w=("m16sp",))
        dma("sp", convw[:, :, :], d_conv_w[:, :, :], r=(), w=("convw",))
        for t, src, k in ((convb, d_conv_b, "convb"), (bat, d_b_a, "bat"), (bxt, d_b_x, "bxt")):
            dma("sp", t[:, :], src[:, :], r=(), w=(k,))
        for wsrc, dst, k in ((d_w_a, bda, "bda"), (d_w_x, bdx, "bdx")):
            P.add("dve", lambda e: e.memset(bdf[:, :, :], 0.0), w=("bdf",))
            for nn in range(2):
                src = wsrc.rearrange("(c two) i j -> two i c j", two=2)[nn]
                dma("sp", bdf[nn * 64:(nn + 1) * 64, :, nn * 64:(nn + 1) * 64], src, r=(), w=("bdf",))
            P.add("dve", lambda e, dst=dst: e.tensor_copy(out=dst[:, :, :], in_=bdf[:, :, :]), r=("bdf",), w=(k,))
        dma("sp", brt[:, :], b_rel[:, :], r=(), w=("brt",))
        P.add("dve", lambda e: e.memset(txs[:, :], 0.0), w=("txs",))
        P.add("dve", lambda e: e.tensor_scalar(out=txs[:, 0:383], in0=txs[:, 0:383], scalar1=brt[:, 0:1], scalar2=None, op0=ALU.add), r=("brt", "txs"), w=("txs",))
        P.add("dve", lambda e: e.tensor_copy(out=txs[:, 383:640], in_=brt[:, 0:257]), r=("brt", "txs"), w=("txs",))
        P.add("dve", lambda e: e.tensor_scalar(out=txs[:, 640:1536], in0=txs[:, 640:1536], scalar1=brt[:, 256:257], scalar2=None, op0=ALU.add), r=("brt", "txs"), w=("txs",))
        dma("pool", text[:, :], txs[:, :], r=("txs",), w=("text",))
        for h in range(8):
            for dk, dv in enumerate(dvals):
                s = nxt("dt", 2)
                src = bass.AP(tensor=text.tensor, offset=h * 1536 + dv + 384, ap=[[1, 128], [1, TT]])
                dma("sp", dtmp[s][:, :], src, r=("text",), w=(("dtmp", s),))
                bnk = nxt("ps", 6)
                P.add("pe", lambda e, s=s, bnk=bnk: e.matmul(ps[bnk][:, :], lhsT=anti_f[:, :], rhs=dtmp[s][:, :], start=True, stop=True),
                      r=(("dtmp", s), "anti_f"), w=(("ps", bnk),))
                pslot = nxt("pt", NPT)
                dma_in = cin["bmask"]
                P.add("pool", lambda e, pslot=pslot, dk=dk: e.dma_start(out=ptr[pslot][:, :], in_=dma_in[:, dk, :]),
                      r=(), w=(("ptr", pslot),), dma=True)
                P.add("dve", lambda e, bnk=bnk, pslot=pslot: e.tensor_tensor(out=ptr[pslot][:, :], in0=ps[bnk][:, :], in1=ptr[pslot][:, :], op=ALU.add),
                      r=(("ps", bnk), ("ptr", pslot)), w=(("ptr", pslot),))
                dma("pool", bbias[h, :, dk, :], ptr[pslot][:, :], r=(("ptr", pslot),), w=("bbias",))

    def rstd_block(G, b):
        bs = G["bs"]
        P.add("dve", lambda e, b=b: e.tensor_tensor(out=sqj[0:bs, :], in0=xres[0:bs, b, :], in1=xres[0:bs, b, :], op=ALU.mult), r=(("xres", b),), w=("sqj",))
        P.add("dve", lambda e: e.reduce_sum(out=ssum[0:bs, 0:1], in_=sqj[0:bs, :], axis=AX.X), r=("sqj",), w=("ssum",))
        P.add("dve", lambda e: e.tensor_scalar(out=ssum[0:bs, 0:1], in0=ssum[0:bs, 0:1], scalar1=1.0 / D, scalar2=EPS, op0=ALU.mult, op1=ALU.add), r=("ssum",), w=("ssum",))
        P.add("act", lambda e: e.activation(out=ssum[0:bs, 1:2], in_=ssum[0:bs, 0:1], func=AF.Ln), r=("ssum",), w=("ssum2",))
        P.add("act", lambda e: e.activation(out=rstd[0:bs, :], in_=ssum[0:bs, 1:2], func=AF.Exp, scale=-0.5), r=("ssum2",), w=("rstd",))

    def norm_T(G, nidx):
        bs, nblk, ntok = G["bs"], G["nblk"], G["ntok"]
        for b in range(nblk):
            rstd_block(G, b)
            P.add("dve", lambda e, b=b: e.scalar_tensor_tensor(out=xn[0:bs, :], in0=xres[0:bs, b, :], scalar=rstd[0:bs, 0:1], in1=gbc[0:bs, nidx, :], op0=ALU.mult, op1=ALU.mult),
                  r=(("xres", b), "rstd", "gbc"), w=("xn",))
            for kc in range(8):
                P.add("pe", lambda e, kc=kc: e.transpose(out=ptb[:, kc, :], in_=xn[:, kc * 128:(kc + 1) * 128], identity=ident_b[:, :]),
                      r=("xn", "ident_b"), w=(("ptb", kc),))
            evac(hT[:, :, b * bs:(b + 1) * bs], ptb[:, :, 0:bs], r=tuple(("ptb", kc) for kc in range(8)), w=(("hT", b, 0), ("hT", b, 1)))

    def hT_keys(G):
        return tuple(("hT", b, j) for b in range(G["nblk"]) for j in range(2))

    evac_tog = [0]

    def evac(out, in_, r, w, scale=None):
        evac_tog[0] ^= 1
        if evac_tog[0]:
            if scale is None:
                P.add("act", lambda e: e.activation(out=out, in_=in_, func=AF.Copy), r=r, w=w)
            else:
                P.add("act", lambda e: e.activation(out=out, in_=in_, func=AF.Copy, scale=float(scale)), r=r, w=w)
        else:
            if scale is None:
                P.add("dve", lambda e: e.tensor_copy(out=out, in_=in_), r=r, w=w)
            else:
                P.add("dve", lambda e: e.tensor_scalar(out=out, in0=in_, scalar1=float(scale), scalar2=None, op0=ALU.mult), r=r, w=w)

    def proj_units(G, ws, units, dst_fn, dst_keys_fn, scale=None):
        ntok = G["ntok"]
        for u in units:
            bnk = nxt("ps", 6)
            for kc in range(8):
                P.add("pe", lambda e, kc=kc, u=u, bnk=bnk: e.matmul(ps[bnk][0:64, 0:ntok], lhsT=wring[ws][:, kc, u * 64:(u + 1) * 64], rhs=hT[:, kc, 0:ntok],
                                                                    start=(kc == 0), stop=(kc == 7)),
                      r=(("wr", ws),) + hT_keys(G), w=(("ps", bnk),))
            evac(dst_fn(u), ps[bnk][0:64, 0:ntok], r=(("ps", bnk),), w=dst_keys_fn(u), scale=scale)

    def proj_tok(G, ws, ncols, cb):
        bs, nblk = G["bs"], G["nblk"]
        for b in range(nblk):
            bnk = nxt("ps", 6)
            for kc in range(8):
                P.add("pe", lambda e, kc=kc, b=b, bnk=bnk: e.matmul(ps[bnk][0:bs, 0:ncols], lhsT=hT[:, kc, b * bs:(b + 1) * bs], rhs=wring[ws][:, kc, 0:ncols],
                                                                    start=(kc == 0), stop=(kc == 7)),
                      r=(("wr", ws), ("hT", b, 0), ("hT", b, 1)), w=(("ps", bnk),))
            cb(b, bnk)

    def out_rows(G, name, b):
        bs = G["bs"]
        r0 = (G["orow_b"] if name in ("b_k", "b_v") else G["orow"]) + b * bs
        return O[G["g"] + name][r0:r0 + bs, :]

    def k_tok_out(G, ws, name, also=None):
        bs = G["bs"]
        if name is None and also is None:
            return

        def cb(b, bnk):
            s = nxt("kvo", 2)
            evac(kvo[s][0:bs, :], ps[bnk][0:bs, 0:512], r=(("ps", bnk),), w=(("kvo", s),))
            if name is not None:
                dma("pool", out_rows(G, name, b), kvo[s][0:bs, :], r=(("kvo", s),), w=(("out", name, G["g"], G["orow"], b),))
            if also is not None:
                also(b, bnk, s)
        proj_tok(G, ws, 512, cb)

    def v_stage(G, b, s):
        bs = G["bs"]
        P.add("dve", lambda e: e.tensor_copy(out=Vs[0:bs, b, :, 0:64], in_=kvo[s][0:bs, :].rearrange("p (u d) -> p u d", d=64)),
              r=(("kvo", s), "Vs1"), w=(("Vs", b),))

    def write_kv_scratch(G, m):
        g, bs, nblk, ntok, q0 = G["g"], G["bs"], G["nblk"], G["ntok"], G["q0"]
        kd = SCR[(g, m, "k")]
        dma("pool", kd[:, :, q0:q0 + ntok].rearrange("u d t -> d u t"), KTs[:, :, 0:ntok], r=tuple(("KTs", u) for u in range(8)), w=((g, m, "k", q0 // TT),))
        vd = SCR[(g, m, "v")]
        kt0 = q0 // 128
        vkeys = tuple(("Vs", b) for b in range(nblk))
        if m == "A":
            for h in range(4):
                src = Vs[0:bs, 0:nblk, 2 * h:2 * h + 2, :].rearrange("p b two x -> p b (two x)")
                dma("pool", vd[h, 0:bs, kt0:kt0 + nblk, :], src, r=vkeys, w=((g, m, "v", q0 // TT, h),))
        else:
            for u in range(8):
                dma("pool", vd[u, 0:bs, kt0:kt0 + nblk, :], Vs[0:bs, 0:nblk, u, :], r=vkeys, w=((g, m, "v", q0 // TT, u),))

    oset = [0]

    def attention(G, m):
        g, ntok, q0 = G["g"], G["ntok"], G["q0"]
        nq = ntok
        kd, vd = SCR[(g, m, "k")], SCR[(g, m, "v")]
        nhalf = 2 if m == "A" else 1
        xw = 65 * nhalf
        kstart = 0 if m != "B" else max(G["kmin"], q0 - 512)
        kend = q0 + ntok
        pend_fin = [None]

        def fin_recip(ob):
            P.add("dve", lambda e: e.reciprocal(out=rl[64:65, 0:nq], in_=ps[ob][64:65, 0:nq]), r=(("ps", ob),), w=("rl",))

        def finalize(u, ob, h):
            P.add("pe", lambda e: e.matmul(ps[6][0:64, 0:nq], lhsT=ones_f[64:65, 0:64], rhs=rl[64:65, 0:nq], start=True, stop=True),
                  r=("rl", "ones_f"), w=(("ps", 6),))
            P.add("act", lambda e: e.activation(out=TB[7][0:64, 0:nq], in_=ps[6][0:64, 0:nq], func=AF.Copy), r=(("ps", 6),), w=(("T", 7),))
            for hf in range(nhalf):
                bnk = ob + hf
                if m == "A":
                    cc = u % 2
                    P.add("dve", lambda e, bnk=bnk, cc=cc, hf=hf: e.tensor_tensor(out=TB[cc * 2 + hf][0:64, 0:nq], in0=ps[bnk][0:64, 0:nq], in1=TB[7][0:64, 0:nq], op=ALU.mult),
                          r=(("ps", bnk), ("T", 7)), w=(("T", cc * 2 + hf),))
                else:
                    chunk = 8 + u if m == "B" else u
                    P.add("dve", lambda e, bnk=bnk, chunk=chunk: e.tensor_tensor(out=OT[0:64, chunk, 0:nq], in0=ps[bnk][0:64, 0:nq], in1=TB[7][0:64, 0:nq], op=ALU.mult),
                          r=(("ps", bnk), ("T", 7)), w=(("GT", chunk),))
            if m == "A" and u % 2 == 1:
                for hf in range(2):
                    P.add("dve", lambda e, hf=hf: e.scalar_tensor_tensor(out=TB[4 + hf][0:64, 0:nq], in0=TB[2 + hf][0:64, 0:nq], scalar=neglam[0:64, 0:1], in1=TB[hf][0:64, 0:nq], op0=ALU.mult, op1=ALU.add),
                          r=(("T", hf), ("T", 2 + hf), "neglam"), w=(("T", 4 + hf),))
                    P.add("act", lambda e, hf=hf: e.activation(out=sq2[0:64, hf, 0:nq], in_=TB[4 + hf][0:64, 0:nq], func=AF.Square), r=(("T", 4 + hf),), w=(("sq2", hf),))
                for hf in range(2):
                    P.add("pe", lambda e, hf=hf: e.matmul(ps[6][0:64, 0:nq], lhsT=ones_b[0:64, 0:64], rhs=sq2[0:64, hf, 0:nq], start=(hf == 0), stop=(hf == 1)),
                          r=(("sq2", hf), "ones_b"), w=(("ps", 6),))
                P.add("dve", lambda e: e.tensor_scalar(out=TB[6][0:64, 0:nq], in0=ps[6][0:64, 0:nq], scalar1=1.0 / 128, scalar2=EPS, op0=ALU.mult, op1=ALU.add), r=(("ps", 6),), w=(("T", 6),))
                P.add("act", lambda e: e.activation(out=TB[6][0:64, 0:nq], in_=TB[6][0:64, 0:nq], func=AF.Ln), r=(("T", 6),), w=(("T", 6),))
                P.add("act", lambda e: e.activation(out=TB[6][0:64, 0:nq], in_=TB[6][0:64, 0:nq], func=AF.Exp, scale=-0.5), r=(("T", 6),), w=(("T", 6),))
                for hf in range(2):
                    P.add("dve", lambda e, hf=hf, h=h: e.scalar_tensor_tensor(out=OT[0:64, 2 * h + hf, 0:nq], in0=TB[4 + hf][0:64, 0:nq], scalar=subg[0:64, hf:hf + 1], in1=TB[6][0:64, 0:nq], op0=ALU.mult, op1=ALU.mult),
                          r=(("T", 4 + hf), ("T", 6), "subg"), w=(("GT", 2 * h + hf),))

        def flush_fin():
            if pend_fin[0] is not None:
                f = pend_fin[0]
                pend_fin[0] = None
                f()

        for u in range(8):
            ob = 2 + 2 * oset[0]
            oset[0] ^= 1
            h = u // 2 if m == "A" else u
            vh = h if m == "A" else u
            if m == "B":
                s_bb = nxt("bb", 1)
                dma("sp", bbr[s_bb][:, :, :], bbias[u, :, :, :], r=("bbias",), w=(("bbr", s_bb),))
            first = True
            c0 = (kstart // CH) * CH
            chunks = []
            while c0 < kend:
                chunks.append((max(c0, kstart), min(c0 + CH, kend)))
                c0 += CH
            pend_pv = None
            ntile = 0
            for (lo, hi) in chunks:
                s = nxt("k", NKR)
                dkeys = tuple((g, m, "k", t) for t in range(lo // TT, (hi - 1) // TT + 1))
                dma("sp", kring[s][0:64, 0:hi - lo], kd[u, :, lo:hi], r=dkeys, w=(("kr", s),))
                kt_lo, kt_hi = lo // 128, (hi + 127) // 128
                vkeys = tuple((g, m, "v", t, vh) for t in range(lo // TT, (hi - 1) // TT + 1))
                dma("sp", vring[s][:, 0:kt_hi - kt_lo, 0:xw], vd[vh, :, kt_lo:kt_hi, :], r=vkeys, w=(("vr", s),))
                k0 = lo
                while k0 < hi:
                    nk = min(128, hi - k0)
                    off = k0 - lo
                    kt = k0 // 128 - kt_lo
                    last = (k0 + nk >= kend)
                    sbk = nxt("sps", 2)
                    P.add("pe", lambda e, s=s, off=off, nk=nk, sbk=sbk, u=u: e.matmul(ps[sbk][0:nk, 0:nq], lhsT=kring[s][0:128, off:off + nk], rhs=QT[m][0:128, u, 0:nq], start=True, stop=True),
                          r=(("kr", s), ("kr1", s), ("QT", u), "QT_aug", "QT_z"), w=(("ps", sbk),))
                    diag = k0 >= q0
                    bias = 0.0
                    src = ps[sbk][0:nk, 0:nq]
                    srck = (("ps", sbk),)
                    if m == "A":
                        if diag:
                            js = (k0 - q0) // 128
                            ds = nxt("dt", 2)
                            P.add("dve", lambda e, js=js, ds=ds, nk=nk, sbk=sbk, h=h: e.scalar_tensor_tensor(out=dtmp[ds][0:nk, 0:nq], in0=t0m[0:nk, js, 0:nq], scalar=float(slopes[h]), in1=ps[sbk][0:nk, 0:nq], op0=ALU.mult, op1=ALU.add),
                                  r=(("ps", sbk), "t0m"), w=(("dtmp", ds),))
                            src = dtmp[ds][0:nk, 0:nq]; srck = (("dtmp", ds),)
                        else:
                            d128 = (q0 - k0) // 128
                            bias = bkA[0:nk, h, d128:d128 + 1]
                    elif m == "C":
                        ktabs = k0 // 128
                        bias = negF[g][0:nk, ktabs, u:u + 1]
                        if diag:
                            js = (k0 - q0) // 128
                            ds = nxt("dt", 2)
                            P.add("dve", lambda e, js=js, ds=ds, nk=nk, sbk=sbk: e.tensor_tensor(out=dtmp[ds][0:nk, 0:nq], in0=ps[sbk][0:nk, 0:nq], in1=cmask[0:nk, js, 0:nq], op=ALU.add),
                                  r=(("ps", sbk), "cmask"), w=(("dtmp", ds),))
                            src = dtmp[ds][0:nk, 0:nq]; srck = (("dtmp", ds),)
                    else:
                        dk = (512 - (q0 - k0)) // 128
                        ds = nxt("dt", 2)
                        P.add("dve", lambda e, dk=dk, ds=ds, nk=nk, sbk=sbk, s_bb=s_bb: e.tensor_tensor(out=dtmp[ds][0:nk, 0:nq], in0=ps[sbk][0:nk, 0:nq], in1=bbr[s_bb][0:nk, dk, 0:nq], op=ALU.add),
                              r=(("ps", sbk), ("bbr", s_bb)), w=(("dtmp", ds),))
                        src = dtmp[ds][0:nk, 0:nq]; srck = (("dtmp", ds),)
                    pslot = nxt("pt", NPT)
                    bkeys = ("bkA", ("negF", g)) if not isinstance(bias, float) else ()
                    P.add("act", lambda e, pslot=pslot, src=src, bias=bias, nk=nk: e.activation(out=ptr[pslot][0:nk, 0:nq], in_=src, func=AF.Exp, bias=bias),
                          r=srck + bkeys, w=(("ptr", pslot),))

                    def pv(s=s, kt=kt, nk=nk, pslot=pslot, first=first, last=last, ob=ob):
                        for hf in range(nhalf):
                            if hf == 0:
                                P.add("pe", lambda e: e.matmul(ps[ob][0:128, 0:nq], lhsT=vring[s][0:nk, kt, 0:128], rhs=ptr[pslot][0:nk, 0:nq], start=first, stop=last),
                                      r=(("vr", s), ("vr0", s), ("ptr", pslot)), w=(("ps", ob),))
                            else:
                                P.add("pe", lambda e: e.matmul(ps[ob + 1][0:65, 0:nq], lhsT=vring[s][0:nk, kt, 65:130], rhs=ptr[pslot][0:nk, 0:nq], start=first, stop=last),
                                      r=(("vr", s), ("vr0", s), ("ptr", pslot)), w=(("ps", ob + 1),))
                    if pend_pv is not None:
                        pend_pv()
                    pend_pv = pv
                    ntile += 1
                    if ntile == 5:
                        flush_fin()
                    first = False
                    k0 += nk
            if pend_pv is not None:
                pend_pv()
            flush_fin()
            fin_recip(ob)
            pend_fin[0] = (lambda u=u, ob=ob, h=h: finalize(u, ob, h))
        flush_fin()

    def out_proj(G, panels):
        bs, nblk = G["bs"], G["nblk"]
        for nh in range(2):
            total = sum(pn[2] for pn in panels[nh])
            cnt = [0] * nblk
            for (wkey, npart, nkc, lhs_fn) in panels[nh]:
                ws = load_w(wkey)
                for b in range(nblk):
                    for kc in range(nkc):
                        ap, keys = lhs_fn(kc, b)
                        st = (cnt[b] == 0)
                        cnt[b] += 1
                        sp_ = (cnt[b] == total)
                        P.add("pe", lambda e, ap=ap, ws=ws, kc=kc, b=b, st=st, sp_=sp_, npart=npart: e.matmul(ps[2 + b][0:bs, 0:512], lhsT=ap, rhs=wring[ws][0:npart, kc, 0:512], start=st, stop=sp_),
                              r=(("wr", ws),) + keys, w=(("ps", 2 + b),))
            for b in range(nblk):
                P.add("dve", lambda e, b=b, nh=nh: e.tensor_tensor(out=xres[0:bs, b, nh * 512:(nh + 1) * 512], in0=ps[2 + b][0:bs, 0:512], in1=xres[0:bs, b, nh * 512:(nh + 1) * 512], op=ALU.add),
                      r=(("ps", 2 + b), ("xres", b)), w=(("xres", b),))

    def ffn(G, L):
        bs, nblk, ntok = G["bs"], G["nblk"], G["ntok"]
        norm_T(G, 2 * L + 1)
        for pi, (c0, cn) in enumerate(HP):
            wa = load_w("w1_%d_%d" % (L, pi))
            wb = load_w("w3_%d_%d" % (L, pi))
            for j in range(cn // 128):
                hc = c0 // 128 + j
                b1 = nxt("ps", 6)
                for kc in range(8):
                    P.add("pe", lambda e, kc=kc, j=j, b1=b1, wa=wa: e.matmul(ps[b1][:, 0:ntok], lhsT=wring[wa][:, kc, j * 128:(j + 1) * 128], rhs=hT[:, kc, 0:ntok], start=(kc == 0), stop=(kc == 7)),
                          r=(("wr", wa),) + hT_keys(G), w=(("ps", b1),))
                b3 = nxt("ps", 6)
                for kc in range(8):
                    P.add("pe", lambda e, kc=kc, j=j, b3=b3, wb=wb: e.matmul(ps[b3][:, 0:ntok], lhsT=wring[wb][:, kc, j * 128:(j + 1) * 128], rhs=hT[:, kc, 0:ntok], start=(kc == 0), stop=(kc == 7)),
                          r=(("wr", wb),) + hT_keys(G), w=(("ps", b3),))
                ds = nxt("dt", 2)
                P.add("act", lambda e, b1=b1, ds=ds: e.activation(out=dtmp[ds][:, 0:ntok], in_=ps[b1][:, 0:ntok], func=AF.Silu), r=(("ps", b1),), w=(("dtmp", ds),))
                P.add("dve", lambda e, b3=b3, ds=ds, hc=hc: e.tensor_tensor(out=GT[:, hc, 0:ntok], in0=ps[b3][:, 0:ntok], in1=dtmp[ds][:, 0:ntok], op=ALU.mult),
                      r=(("ps", b3), ("dtmp", ds)), w=(("GT", hc),))
        panels = []
        for nh in range(2):
            pl = []
            for kp, (k0, kn) in enumerate(KP2):
                pl.append(("w2_%d_%d_%d" % (L, nh, kp), 128, kn,
                           lambda kc, b, k0=k0: (GT[:, k0 + kc, b * bs:(b + 1) * bs], (("GT", k0 + kc),))))
            panels.append(pl)
        out_proj(G, panels)

    def layer0(G):
        g, bs, nblk, ntok = G["g"], G["bs"], G["nblk"], G["ntok"]
        chk(3.1)
        norm_T(G, 0)
        chk(3.2)
        for m, base in (("A", 0), ("B", 3)):
            ws = load_w("abin%d" % base)
            proj_units(G, ws, range(8), lambda u, m=m: QT[m][0:64, u, 0:ntok], lambda u, m=m: (("QT", u),), scale=0.125)
            chk(3.4)
            ws = load_w("abin%d" % (base + 1))
            proj_units(G, ws, range(8), lambda u: KTs[0:64, u, 0:ntok], lambda u: (("KTs", u),))
            chk(3.5)
            kname = ("a_k" if m == "A" else "b_k")
            vname = ("a_v" if m == "A" else "b_v")
            if m == "B":
                if g == "p":
                    kname = kname if G["last"] else None
                    vname = vname if G["last"] else None
            k_tok_out(G, ws, kname)
            chk(3.6)
            ws = load_w("abin%d" % (base + 2))
            chk(3.62)
            import os
            if os.environ.get("NOVS", "0") == "1":
                k_tok_out(G, ws, vname)
            else:
                k_tok_out(G, ws, vname, also=lambda b, bnk, s: v_stage(G, b, s))
            chk(4)
            write_kv_scratch(G, m)
            chk(5)
            if m == "A":
                dma("sp", QT1[64:65, :, :], cin["cqA"][:, :, :], r=(), w=("QT_aug",))
            else:
                P.add("pool", lambda e: e.memset(QT1[64:65, :, :], 0.0), w=("QT_aug",))
            attention(G, m)
            chk(6)
        panels = []
        for nh in range(2):
            pl = []
            for kp in range(2):
                pl.append(("about%d_%d" % (nh, kp), 64, 8,
                           lambda kc, b, kp=kp: (OT[0:64, kp * 8 + kc, b * bs:(b + 1) * bs], (("GT", kp * 8 + kc),))))
            panels.append(pl)
        out_proj(G, panels)
        ffn(G, 0)

    def logf_block(G, b, bnk):
        g, bs = G["g"], G["bs"]
        P.add("dve", lambda e: e.tensor_tensor(out=lft[0:bs, :], in0=ps[bnk][0:bs, 0:8], in1=cfb[0:bs, :], op=ALU.add), r=(("ps", bnk), "cfb"), w=("lft",))
        P.add("act", lambda e: e.activation(out=lft[0:bs, :], in_=lft[0:bs, :], func=AF.Exp, scale=-1.0), r=("lft",), w=("lft",))
        P.add("act", lambda e: e.activation(out=lft[0:bs, :], in_=lft[0:bs, :], func=AF.Ln, bias=1.0), r=("lft",), w=("lft",))
        P.add("dve", lambda e: e.tensor_scalar(out=lfo[0:bs, :], in0=lft[0:bs, :], scalar1=-1.0, scalar2=None, op0=ALU.mult), r=("lft",), w=("lfo",))
        dma("pool", out_rows(G, "c_logf", b), lfo[0:bs, :], r=("lfo",), w=(("out", "c_logf", G["g"], G["orow"], b),))
        cum_block(G, G["q0"] // 128 + b, bs, b)

    def cum_block(G, ktabs, bs, b=None):
        g = G["g"]
        has_prev = ktabs > 0
        P.add("pe", lambda e: e.matmul(ps[6][0:bs, 0:8], lhsT=tri_f[0:bs, 0:bs], rhs=lft[0:bs, :], start=True, stop=not has_prev), r=("lft", "tri_f"), w=(("ps", 6),))
        if has_prev:
            P.add("pe", lambda e: e.matmul(ps[6][0:bs, 0:8], lhsT=sel_f[0:128, 0:bs], rhs=negF[g][0:128, ktabs - 1, :], start=False, stop=True),
                  r=(("negF", g), "sel_f"), w=(("ps", 6),))
        P.add("dve", lambda e: e.tensor_copy(out=negF[g][0:bs, ktabs, :], in_=ps[6][0:bs, 0:8]), r=(("ps", 6),), w=(("negF", g),))
        if b is not None:
            P.add("dve", lambda e: e.tensor_scalar(out=lfb[0:bs, :], in0=negF[g][0:bs, ktabs, :], scalar1=-1.0, scalar2=None, op0=ALU.mult), r=(("negF", g),), w=("lfb",))
            P.add("pe", lambda e: e.transpose(out=ptb[0:8, 0, :], in_=lfb[:, :], identity=ident_b[:, :]), r=("lfb", "ident_b"), w=(("ptb", 0),))
            P.add("act", lambda e: e.activation(out=FTb[0:8, b * bs:(b + 1) * bs], in_=ptb[0:8, 0, 0:bs], func=AF.Copy), r=(("ptb", 0),), w=("FTb",))

    def dbranch(G, wsx, wsg):
        g, bs, nblk, ntok = G["g"], G["bs"], G["nblk"], G["ntok"]
        xb = dxbuf[g]
        u_, r_, i_, a_, b_, x_, t_ = TB[0:7]
        for cb in range(4):
            bx = nxt("ps", 6)
            for kc in range(8):
                P.add("pe", lambda e, kc=kc, bx=bx, cb=cb: e.matmul(ps[bx][:, 0:ntok], lhsT=wring[wsx][:, kc, cb * 128:(cb + 1) * 128], rhs=hT[:, kc, 0:ntok], start=(kc == 0), stop=(kc == 7)),
                      r=(("wr", wsx),) + hT_keys(G), w=(("ps", bx),))
            bg = nxt("ps", 6)
            for kc in range(8):
                P.add("pe", lambda e, kc=kc, bg=bg, cb=cb: e.matmul(ps[bg][:, 0:ntok], lhsT=wring[wsg][:, kc, cb * 128:(cb + 1) * 128], rhs=hT[:, kc, 0:ntok], start=(kc == 0), stop=(kc == 7)),
                      r=(("wr", wsg),) + hT_keys(G), w=(("ps", bg),))
            kx = ("dxbuf", g, cb)
            P.add("dve", lambda e, bx=bx, cb=cb: e.tensor_copy(out=xb[:, cb, 3:3 + ntok], in_=ps[bx][:, 0:ntok]), r=(("ps", bx),), w=(kx,))
            P.add("dve", lambda e, cb=cb: e.tensor_scalar(out=u_[:, 0:ntok], in0=xb[:, cb, 3:3 + ntok], scalar1=convw[:, cb, 3:4], scalar2=convb[:, cb:cb + 1], op0=ALU.mult, op1=ALU.add),
                  r=(kx, "convw", "convb"), w=(("T", 0),))
            for j in range(3):
                P.add("dve", lambda e, cb=cb, j=j: e.scalar_tensor_tensor(out=u_[:, 0:ntok], in0=xb[:, cb, j:j + ntok], scalar=convw[:, cb, j:j + 1], in1=u_[:, 0:ntok], op0=ALU.mult, op1=ALU.add),
                      r=(kx, "convw", ("T", 0)), w=(("T", 0),))
            P.add("dve", lambda e, cb=cb: e.tensor_copy(out=xb[:, cb, 0:3], in_=xb[:, cb, ntok:ntok + 3]), r=(kx, ("T", 0)), w=(kx,))
            P.add("act", lambda e: e.activation(out=ub[:, 0:ntok], in_=u_[:, 0:ntok], func=AF.Copy), r=(("T", 0),), w=("ub",))
            ba = nxt("ps", 6)
            P.add("pe", lambda e, ba=ba, cb=cb: e.matmul(ps[ba][:, 0:ntok], lhsT=bda[:, cb, :], rhs=ub[:, 0:ntok], start=True, stop=True), r=("ub", "bda"), w=(("ps", ba),))
            bi = nxt("ps", 6)
            P.add("pe", lambda e, bi=bi, cb=cb: e.matmul(ps[bi][:, 0:ntok], lhsT=bdx[:, cb, :], rhs=ub[:, 0:ntok], start=True, stop=True), r=("ub", "bdx"), w=(("ps", bi),))
            P.add("act", lambda e, ba=ba, cb=cb: e.activation(out=r_[:, 0:ntok], in_=ps[ba][:, 0:ntok], func=AF.Sigmoid, bias=bat[:, cb:cb + 1]), r=(("ps", ba), "bat"), w=(("T", 1),))
            P.add("act", lambda e, bi=bi, cb=cb: e.activation(out=i_[:, 0:ntok], in_=ps[bi][:, 0:ntok], func=AF.Sigmoid, bias=bxt[:, cb:cb + 1]), r=(("ps", bi), "bxt"), w=(("T", 2),))
            P.add("act", lambda e, cb=cb: e.activation(out=a_[:, 0:ntok], in_=r_[:, 0:ntok], func=AF.Exp, scale=m8sp[:, cb:cb + 1]), r=(("T", 1), "m8sp"), w=(("T", 3),))
            P.add("act", lambda e, cb=cb: e.activation(out=b_[:, 0:ntok], in_=r_[:, 0:ntok], func=AF.Exp, scale=m16sp[:, cb:cb + 1]), r=(("T", 1), "m16sp"), w=(("T", 4),))
            P.add("dve", lambda e: e.tensor_scalar(out=b_[:, 0:ntok], in0=b_[:, 0:ntok], scalar1=-1.0, scalar2=1.0, op0=ALU.mult, op1=ALU.add), r=(("T", 4),), w=(("T", 4),))
            P.add("act", lambda e: e.activation(out=b_[:, 0:ntok], in_=b_[:, 0:ntok], func=AF.Ln), r=(("T", 4),), w=(("T", 4),))
            P.add("act", lambda e: e.activation(out=b_[:, 0:ntok], in_=b_[:, 0:ntok], func=AF.Exp, scale=0.5), r=(("T", 4),), w=(("T", 4),))
            P.add("dve", lambda e: e.tensor_tensor(out=i_[:, 0:ntok], in0=i_[:, 0:ntok], in1=u_[:, 0:ntok], op=ALU.mult), r=(("T", 2), ("T", 0)), w=(("T", 2),))
            P.add("dve", lambda e: e.tensor_tensor(out=b_[:, 0:ntok], in0=b_[:, 0:ntok], in1=i_[:, 0:ntok], op=ALU.mult), r=(("T", 2), ("T", 4)), w=(("T", 4),))
            kh = ("hst", g)
            P.add("dve", lambda e, cb=cb: e.tensor_tensor_scan(out=r_[:, 0:ntok], data0=a_[:, 0:ntok], data1=b_[:, 0:ntok], initial=hst[g][:, cb:cb + 1], op0=ALU.mult, op1=ALU.add),
                  r=(("T", 3), ("T", 4), kh), w=(("T", 1),))
            P.add("dve", lambda e, cb=cb: e.tensor_copy(out=hst[g][:, cb:cb + 1], in_=r_[:, ntok - 1:ntok]), r=(("T", 1),), w=(kh,))
            P.add("act", lambda e, bg=bg: e.activation(out=x_[:, 0:ntok], in_=ps[bg][:, 0:ntok], func=AF.Copy), r=(("ps", bg),), w=(("T", 5),))
            P.add("dve", lambda e: e.tensor_tensor(out=t_[:, 0:ntok], in0=x_[:, 0:ntok], in1=x_[:, 0:ntok], op=ALU.mult), r=(("T", 5),), w=(("T", 6),))
            P.add("dve", lambda e: e.tensor_scalar(out=t_[:, 0:ntok], in0=t_[:, 0:ntok], scalar1=0.044715, scalar2=1.0, op0=ALU.mult, op1=ALU.add), r=(("T", 6),), w=(("T", 6),))
            P.add("dve", lambda e: e.tensor_tensor(out=t_[:, 0:ntok], in0=t_[:, 0:ntok], in1=x_[:, 0:ntok], op=ALU.mult), r=(("T", 6), ("T", 5)), w=(("T", 6),))
            P.add("act", lambda e: e.activation(out=t_[:, 0:ntok], in_=t_[:, 0:ntok], func=AF.Sigmoid, scale=1.5957691216057308), r=(("T", 6),), w=(("T", 6),))
            P.add("dve", lambda e: e.tensor_tensor(out=t_[:, 0:ntok], in0=t_[:, 0:ntok], in1=x_[:, 0:ntok], op=ALU.mult), r=(("T", 6), ("T", 5)), w=(("T", 6),))
            P.add("dve", lambda e, cb=cb: e.tensor_tensor(out=odT[:, cb, 0:ntok], in0=t_[:, 0:ntok], in1=r_[:, 0:ntok], op=ALU.mult), r=(("T", 6), ("T", 1)), w=(("odT", cb),))
        if G["last"]:
            dma("pool", O[g + "d_conv"][:, :, :], xb[:, :, 0:3], r=tuple(("dxbuf", g, c) for c in range(4)), w=(("out", "d_conv"),))
            dma("pool", O[g + "d_h"][:, :], hst[g][:, :], r=(("hst", g),), w=(("out", "d_h"),))

    def layer1(G):
        g, bs, nblk, ntok = G["g"], G["bs"], G["nblk"], G["ntok"]
        norm_T(G, 2)
        ws = load_w("cdin0")
        proj_units(G, ws, range(8), lambda u: QT["C"][0:64, u, 0:ntok], lambda u: (("QT", u),), scale=0.125)
        ws = load_w("cdin1")
        proj_units(G, ws, range(8), lambda u: KTs[0:64, u, 0:ntok], lambda u: (("KTs", u),))
        k_tok_out(G, ws, "c_k")
        ws = load_w("cdin2")
        k_tok_out(G, ws, "c_v", also=lambda b, bnk, s: v_stage(G, b, s))
        write_kv_scratch(G, "C")
        ws = load_w("cdcf")
        proj_tok(G, ws, 8, lambda b, bnk: logf_block(G, b, bnk))
        dma("sp", QT["C"][64:65, :, 0:ntok], FTb[0:8, 0:ntok], r=("FTb",), w=("QT_aug",))
        attention(G, "C")
        chk(8)
        wsx = load_w("cdin3")
        wsg = load_w("cdin4")
        dbranch(G, wsx, wsg)
        panels = []
        for nh in range(2):
            pl = [("cdoutc%d" % nh, 64, 8, lambda kc, b: (OT[0:64, kc, b * bs:(b + 1) * bs], (("GT", kc),))),
                  ("cdoutd%d" % nh, 128, 4, lambda kc, b: (odT[:, kc, b * bs:(b + 1) * bs], (("odT", kc),)))]
            panels.append(pl)
        out_proj(G, panels)
        ffn(G, 1)

    def final_norm(G):
        g, bs, nblk = G["g"], G["bs"], G["nblk"]
        for b in range(nblk):
            rstd_block(G, b)
            for nh in range(2):
                ti = (2 * b + nh) % 8
                P.add("dve", lambda e, b=b, nh=nh, ti=ti: e.scalar_tensor_tensor(out=TB[ti][0:bs, :], in0=xres[0:bs, b, nh * 512:(nh + 1) * 512], scalar=rstd[0:bs, 0:1], in1=gfin[0:bs, nh * 512:(nh + 1) * 512], op0=ALU.mult, op1=ALU.mult),
                      r=(("xres", b), "rstd", "gfin"), w=(("T", ti),))
                dma("pool", out_rows(G, "y", b)[:, nh * 512:(nh + 1) * 512], TB[ti][0:bs, :], r=(("T", ti),), w=(("out", "y", G["g"], G["orow"], b, nh),))

    def run_tile(G):
        bs, nblk = G["bs"], G["nblk"]
        src = G["x"]
        dma("sp", xres[0:bs, 0:nblk, :], src.rearrange("(b p) f -> p b f", p=bs), r=(), w=tuple(("xres", b) for b in range(nblk)))
        layer0(G)
        chk(7)
        layer1(G)
        chk(9)
        final_norm(G)

    def ingest(m, csrc_k, csrc_v, ntok_c, pos0, roll=None):
        for t0 in range(0, ntok_c, TT):
            q0 = pos0 + t0
            Gc = {"g": "s", "bs": 128, "nblk": 4, "ntok": TT, "q0": q0}
            for b in range(4):
                tk = b % 2
                tv = 2 + b % 2
                r0 = t0 + b * 128
                dma("sp", TB[tk][:, :], csrc_k[r0:r0 + 128, :], r=(), w=(("T", tk),))
                dma("sp", TB[tv][:, :], csrc_v[r0:r0 + 128, :], r=(), w=(("T", tv),))
                import os
                rmode = os.environ.get("ROLLMODE", "1")
                if roll is not None and rmode != "0":
                    p0 = DEC if r0 == 0 else 0
                    for (Oo, tt) in ((roll[0], tk), (roll[1], tv)):
                        if rmode == "2":
                            if r0 == 0:
                                dma("pool", O["sa_k"][0:64, :], TB[tt][64:128, :], r=(("T", tt),), w=(("out", "roll"),))
                        elif rmode == "3":
                            dma("sp", Oo[r0 + p0 - DEC:r0 + 128 - DEC, :], TB[tt][p0:128, :], r=(("T", tt),), w=(("out", "roll"),))
                        else:
                            dma("pool", Oo[r0 + p0 - DEC:r0 + 128 - DEC, :], TB[tt][p0:128, :], r=(("T", tt),), w=(("out", "roll"),))
                P.add("dve", lambda e, tk=tk: e.tensor_copy(out=xn[:, 0:512], in_=TB[tk][:, :]), r=(("T", tk),), w=("xn",))
                for u in range(8):
                    P.add("pe", lambda e, u=u: e.transpose(out=ptb[0:64, u, :], in_=xn[:, u * 64:(u + 1) * 64], identity=ident_b[:, :]),
                          r=("xn", "ident_b"), w=(("ptb", u),))
                evac(KTs[0:64, :, b * 128:(b + 1) * 128], ptb[0:64, :, :], r=tuple(("ptb", u) for u in range(8)), w=tuple(("KTs", u) for u in range(8)))
                P.add("dve", lambda e, b=b, tv=tv: e.tensor_copy(out=Vs[:, b, :, 0:64], in_=TB[tv][:, :].rearrange("p (u d) -> p u d", d=64)), r=(("T", tv), "Vs1"), w=(("Vs", b),))
            write_kv_scratch(Gc, m)

    epsc = sb("epsc", [128, 1])
    def main_seq(chk):
        chk(0)
        prologue()
        chk(1)

        Gs = {"g": "s", "bs": DEC, "nblk": 1, "ntok": DEC, "q0": PAST, "orow": 0, "last": True, "x": x_s, "kmin": PAST - 512}
        ingest("A", cak, cav, PAST, 0)
        ingest("B", cbk, cbv, 512, PAST - 512, roll=(O["sb_k"], O["sb_v"]))
        ingest("C", cck, ccv, PAST, 0)
        chk(2)
        nb_c = PAST // 128
        lcs = sb("lcs", [128, nb_c, 8])
        dma("sp", lcs[:, :, :], cclf.rearrange("(b p) h -> p b h", p=128), r=(), w=("lcs",))
        for b in range(nb_c):
            P.add("dve", lambda e, b=b: e.tensor_scalar(out=lft[:, :], in0=lcs[:, b, :], scalar1=-1.0, scalar2=None, op0=ALU.mult), r=("lcs",), w=("lft",))
            cum_block(Gs, b, 128)
        chk(2.3)
        dma("sp", dxbuf["s"][:, :, 0:3], sdc[:, :, :], r=(), w=tuple(("dxbuf", "s", c) for c in range(4)))
        dma("sp", hst["s"][:, :], sdh[:, :], r=(), w=(("hst", "s"),))
        chk(2.6)
        Gs["orow_b"] = 448
        chk(3)
        import os
        if os.environ.get("SKIPS", "0") != "1":
            run_tile(Gs)
            chk(10)

        for cb in range(4):
            P.add("dve", lambda e, cb=cb: e.memset(dxbuf["p"][:, cb, 0:3], 0.0), w=(("dxbuf", "p", cb),))
        P.add("dve", lambda e: e.memset(hst["p"][:, :], 0.0), w=(("hst", "p"),))
        for t in range(NT):
            Gp = {"g": "p", "bs": 128, "nblk": 4, "ntok": TT, "q0": t * TT, "orow": t * TT, "last": t == NT - 1,
                  "x": x_p[t * TT:(t + 1) * TT, :], "kmin": 0, "orow_b": 0}
            run_tile(Gp)


    P.add("pool", lambda e: e.memset(epsc[:, :], EPS), w=("epsc",))
    P.add("pool", lambda e: e.memset(xn[:, :], 0.0), w=("xn",))
    P.add("pool", lambda e: e.memset(lfb[:, :], 0.0), w=("lfb",))
    try:
        main_seq(chk)
    except _Stop:
        pass

    P.emit()
    return nc, consts


_NAMES_B = ("b_k", "b_v")


_CACHE = {}


def _get_prog(SEQ, PAST):
    key = (SEQ, PAST)
    if key not in _CACHE:
        _CACHE[key] = build(SEQ, PAST)
    return _CACHE[key]


def kernel(**inp):
    x_prompt = np.asarray(inp["x_prompt"]); x_sample = np.asarray(inp["x_sample"])
    NB, SEQ, _ = x_prompt.shape
    NS = x_sample.shape[0]
    PAST = inp["cache_a_k"].shape[1]
    nc, consts = _get_prog(SEQ, PAST)
    f = lambda a: np.ascontiguousarray(np.asarray(a, dtype=np.float32))
    shared = {}
    for k in ("final_g", "ab_w_in", "ab_w_out", "a_lambda", "b_rel_bias", "cd_w_in",
              "cd_w_out", "c_f_bias", "d_w_a", "d_w_x", "ffn_w1", "ffn_w3", "ffn_w2"):
        shared[k] = f(inp[k])
    pc = lambda v: f(np.asarray(v, np.float32).reshape(4, 128).T)
    shared["d_b_a"] = pc(inp["d_b_a"]); shared["d_b_x"] = pc(inp["d_b_x"])
    shared["d_conv_b"] = pc(inp["d_conv_b"]); shared["d_lambda"] = pc(inp["d_lambda"])
    shared["d_conv_w"] = f(np.asarray(inp["d_conv_w"], np.float32).reshape(4, 4, 128).transpose(2, 1, 0))
    shared["a_subln_g"] = f(np.asarray(inp["a_subln_g"], np.float32).reshape(2, 64).T)
    gs = np.stack([np.asarray(inp["norm_mix_g"])[0], np.asarray(inp["norm_ffn_g"])[0],
                   np.asarray(inp["norm_mix_g"])[1], np.asarray(inp["norm_ffn_g"])[1]], 0).astype(np.float32)
    shared["gT_in"] = f(gs)
    for k, v in consts.items():
        shared["c_" + k] = v
    in_maps = []
    for c in range(8):
        m = dict(shared)
        m["x_p"] = f(x_prompt[c % NB]); m["x_s"] = f(x_sample[c % NS])
        s = c % NS
        m["cache_a_k"] = f(inp["cache_a_k"][s]).reshape(PAST, 512); m["cache_a_v"] = f(inp["cache_a_v"][s]).reshape(PAST, 512)
        m["cache_b_k"] = f(inp["cache_b_k"][s]).reshape(512, 512); m["cache_b_v"] = f(inp["cache_b_v"][s]).reshape(512, 512)
        m["cache_c_k"] = f(inp["cache_c_k"][s]).reshape(PAST, 512); m["cache_c_v"] = f(inp["cache_c_v"][s]).reshape(PAST, 512)
        m["cache_c_logf"] = f(inp["cache_c_logf"][s])
        m["state_d_conv"] = f(np.asarray(inp["state_d_conv"][s], np.float32).reshape(3, 4, 128).transpose(2, 1, 0))
        m["state_d_h"] = f(np.asarray(inp["state_d_h"][s], np.float32).reshape(4, 128).T)
        in_maps.append(m)
    res = run_bass_kernel_spmd(nc, in_maps, core_ids=list(range(8)))
    R = res.results

    def stack(name, cores, shape):
        if name.endswith("d_conv"):
            return np.stack([np.asarray(R[c][name], dtype=np.float32).reshape(128, 4, 3).transpose(2, 1, 0).reshape(3, 512) for c in cores], axis=0)
        if name.endswith("d_h"):
            return np.stack([np.asarray(R[c][name], dtype=np.float32).reshape(128, 4).T.reshape(512) for c in cores], axis=0)
        return np.stack([np.asarray(R[c][name], dtype=np.float32).reshape(shape) for c in cores], axis=0)
    pc = list(range(NB)); sc = list(range(NS))
    outs = (
        stack("p_y", pc, (SEQ, D)), stack("s_y", sc, (DEC, D)),
        stack("p_a_k", pc, (SEQ, 4, 128)), stack("p_a_v", pc, (SEQ, 4, 128)),
        stack("p_b_k", pc, (512, 8, 64)), stack("p_b_v", pc, (512, 8, 64)),
        stack("p_c_k", pc, (SEQ, 8, 64)), stack("p_c_v", pc, (SEQ, 8, 64)),
        stack("p_c_logf", pc, (SEQ, 8)), stack("p_d_conv", pc, (3, 512)), stack("p_d_h", pc, (512,)),
        stack("s_a_k", sc, (DEC, 4, 128)), stack("s_a_v", sc, (DEC, 4, 128)),
        stack("s_b_k", sc, (512, 8, 64)), stack("s_b_v", sc, (512, 8, 64)),
        stack("s_c_k", sc, (DEC, 8, 64)), stack("s_c_v", sc, (DEC, 8, 64)),
        stack("s_c_logf", sc, (DEC, 8)), stack("s_d_conv", sc, (3, 512)), stack("s_d_h", sc, (512,)),
    )
    return outs
```

```python
import math
import numpy as np
import ml_dtypes
import concourse.bass as bass
import concourse.mybir as mybir
from concourse.bass_utils import run_bass_kernel_spmd

F32 = mybir.dt.float32
BF16 = mybir.dt.bfloat16
AF = mybir.ActivationFunctionType
ALU = mybir.AluOpType
AX = mybir.AxisListType

D = 1024
HID = 2816
TT = 512
DEC = 64
CH = 1024
NEG = -1.0e30
EPS = 1e-6
NDS = 24


class Op:
    __slots__ = ("eng", "fn", "dma", "deps", "sig", "val", "sem", "prev")


class Prog:
    ENGS = ("pe", "act", "dve", "pool", "sp")

    def __init__(self, nc):
        self.nc = nc
        self.ops = {e: [] for e in self.ENGS}
        self.lastw = {}
        self.readers = {}

    def add(self, eng, fn, r=(), w=(), dma=False):
        op = Op()
        op.eng = eng; op.fn = fn; op.dma = dma; op.sig = False; op.val = 0; op.sem = None; op.prev = 0
        deps = {}
        for k in r:
            lw = self.lastw.get(k)
            if lw is not None:
                deps[lw] = True
        for k in w:
            lw = self.lastw.get(k)
            if lw is not None:
                deps[lw] = True
            for rd in self.readers.get(k, ()):
                deps.setdefault(rd, False)
        for k in r:
            lst = self.readers.setdefault(k, [])
            if not dma:
                lst[:] = [o for o in lst if o.dma or o.eng != eng]
            lst.append(op)
        for k in w:
            self.lastw[k] = op
            self.readers[k] = []
        fd = []
        for d, strong in deps.items():
            if d is op:
                continue
            if d.eng == eng and not d.dma and not dma:
                if eng == "pe" or not strong:
                    continue
            d.sig = True
            fd.append(d)
        op.deps = fd
        self.ops[eng].append(op)
        return op

    def emit(self):
        nc = self.nc
        esem = {e: nc.alloc_semaphore("sem_" + e) for e in ("pe", "act", "dve", "pool")}
        dsem = {q: [nc.alloc_semaphore("dq_%s_%d" % (q, i)) for i in range(NDS)] for q in ("sp", "pool", "act")}
        final = {}
        for e in self.ENGS:
            c = 0
            di = 0
            dcount = [0] * NDS
            for op in self.ops[e]:
                if op.dma:
                    s = di % NDS
                    di += 1
                    op.prev = 16 * dcount[s]
                    dcount[s] += 1
                    op.sem = dsem[e][s]
                    op.val = 16 * dcount[s]
                    final[op.sem] = op.val
                elif op.sig:
                    c += 1
                    op.sem = esem[e]
                    op.val = c

        def run(e, eng):
            waited = {}
            for op in self.ops[e]:
                need = {}
                for d in op.deps:
                    if need.get(d.sem, 0) < d.val:
                        need[d.sem] = d.val
                if op.dma and op.prev > 0 and need.get(op.sem, 0) < op.prev:
                    need[op.sem] = op.prev
                for s, v in need.items():
                    if waited.get(s, 0) < v:
                        eng.wait_ge(s, v)
                        waited[s] = v
                ins = op.fn(eng)
                if op.dma:
                    ins.then_inc(op.sem, 16)
                elif op.sig:
                    ins.then_inc(op.sem, 1)
            if e == "sp":
                for s, v in final.items():
                    if waited.get(s, 0) < v:
                        eng.wait_ge(s, v)

        with nc.Block() as block:
            @block.tensor
            def _(eng):
                run("pe", eng)

            @block.scalar
            def _(eng):
                run("act", eng)

            @block.vector
            def _(eng):
                run("dve", eng)

            @block.gpsimd
            def _(eng):
                run("pool", eng)

            @block.sync
            def _(eng):
                run("sp", eng)


def _bf16r(a):
    return np.asarray(a, np.float32).astype(ml_dtypes.bfloat16).astype(np.float32)


def make_consts(nda):
    c = {}
    c["ident_f"] = np.eye(128, dtype=np.float32)
    c["ident_b"] = np.eye(128, dtype=np.float32).astype(ml_dtypes.bfloat16)
    t = np.arange(128)
    c["tri_f"] = (t[:, None] <= t[None, :]).astype(np.float32)
    sel = np.zeros((128, 128), np.float32); sel[127, :] = 1.0
    c["sel_f"] = sel
    c["anti_f"] = np.ascontiguousarray(np.eye(128, dtype=np.float32)[::-1])
    i = np.arange(TT)
    ri = _bf16r(i.astype(np.float32))
    slopes = np.array([2.0 ** (-8.0 * (h + 1) / 4) for h in range(4)], np.float32)
    cq = np.zeros((1, 8, TT), np.float32)
    for h in range(4):
        for cc in range(2):
            cq[0, 2 * h + cc] = -slopes[h] * ri
    assert np.array_equal(_bf16r(cq), cq)
    c["cqA"] = cq.astype(ml_dtypes.bfloat16)
    j = np.arange(128)
    t0 = np.zeros((128, 4, TT), np.float32)
    cm = np.zeros((128, 4, TT), np.float32)
    for js in range(4):
        jp = 128 * js + j
        vis = (jp[:, None] // 64) <= (i[None, :] // 64)
        t0[:, js, :] = np.where(vis, ri[None, :] - np.abs(i[None, :] - jp[:, None]), -4.0e30)
        cm[:, js, :] = np.where(jp[:, None] <= i[None, :], 0.0, NEG)
    c["t0m"] = t0
    c["cmask"] = cm.astype(ml_dtypes.bfloat16)
    bk = np.zeros((128, 4, nda), np.float32)
    for h in range(4):
        bk[:, h, :] = slopes[h] * (j[:, None] - 128.0 * np.arange(nda)[None, :])
    c["bkA"] = bk
    bm = np.zeros((128, 8, TT), np.float32)
    dvals = [512, 384, 256, 128, 0, -128, -256, -384]
    for dk, dv in enumerate(dvals):
        kc_ = np.floor_divide(-dv + j, 64)
        qc_ = i // 64
        vis = (kc_[:, None] >= qc_[None, :] - 8) & (kc_[:, None] <= qc_[None, :])
        bm[:, dk, :] = np.where(vis, 0.0, NEG)
    c["bmask"] = bm.astype(ml_dtypes.bfloat16)
    return c, slopes, dvals


class _Stop(Exception):
    pass


def build(SEQ, PAST, stop=999):
    def chk(stage):
        if stage > stop:
            raise _Stop()
    NT = SEQ // TT
    NKT_P = SEQ // 128
    NKT_S = PAST // 128 + 1
    NDA = max(SEQ, PAST) // 128 + 2
    consts, slopes, dvals = make_consts(NDA)
    nc = bass.Bass("TRN2", target_bir_lowering=False)
    P = Prog(nc)

    def din(name, shape, dt=F32):
        return nc.dram_tensor(name, list(shape), dt, kind="ExternalInput").ap()

    def dout(name, shape):
        return nc.dram_tensor(name, list(shape), F32, kind="ExternalOutput").ap()

    def dscr(name, shape, dt=BF16):
        return nc.dram_tensor(name, list(shape), dt).ap()

    def sb(name, shape, dt=F32):
        return nc.alloc_sbuf_tensor(name, list(shape), dt)

    x_p = din("x_p", [SEQ, D]); x_s = din("x_s", [DEC, D])
    cak = din("cache_a_k", [PAST, 512]); cav = din("cache_a_v", [PAST, 512])
    cbk = din("cache_b_k", [512, 512]); cbv = din("cache_b_v", [512, 512])
    cck = din("cache_c_k", [PAST, 512]); ccv = din("cache_c_v", [PAST, 512])
    cclf = din("cache_c_logf", [PAST, 8])
    sdc = din("state_d_conv", [128, 4, 3]); sdh = din("state_d_h", [128, 4])
    gT_in = din("gT_in", [4, D]); fing = din("final_g", [D])
    ab_w_in = din("ab_w_in", [D, 3072]); ab_w_out = din("ab_w_out", [D, D])
    a_lambda = din("a_lambda", [4, 64]); a_subln = din("a_subln_g", [64, 2]); b_rel = din("b_rel_bias", [8, 257])
    cd_w_in = din("cd_w_in", [D, 2568]); cd_w_out = din("cd_w_out", [D, D]); c_f_bias = din("c_f_bias", [8])
    d_conv_w = din("d_conv_w", [128, 4, 4]); d_conv_b = din("d_conv_b", [128, 4])
    d_w_a = din("d_w_a", [8, 64, 64]); d_b_a = din("d_b_a", [128, 4]); d_w_x = din("d_w_x", [8, 64, 64])
    d_b_x = din("d_b_x", [128, 4]); d_lam = din("d_lambda", [128, 4])
    w1 = din("ffn_w1", [2, D, HID]); w3 = din("ffn_w3", [2, D, HID]); w2 = din("ffn_w2", [2, HID, D])
    cin = {}
    for k, v in consts.items():
        cin[k] = din("c_" + k, v.shape, BF16 if v.dtype == ml_dtypes.bfloat16 else F32)

    O = {}
    for g, T in (("p", SEQ), ("s", DEC)):
        O[g + "y"] = dout(g + "_y", [T, D])
        O[g + "a_k"] = dout(g + "_a_k", [T, 512]); O[g + "a_v"] = dout(g + "_a_v", [T, 512])
        O[g + "b_k"] = dout(g + "_b_k", [512, 512]); O[g + "b_v"] = dout(g + "_b_v", [512, 512])
        O[g + "c_k"] = dout(g + "_c_k", [T, 512]); O[g + "c_v"] = dout(g + "_c_v", [T, 512])
        O[g + "c_logf"] = dout(g + "_c_logf", [T, 8])
        O[g + "d_conv"] = dout(g + "_d_conv", [128, 4, 3]); O[g + "d_h"] = dout(g + "_d_h", [128, 4])

    WP = {}

    def wpanel(key, src_rows_fn, npart, nkc, ncols):
        t = dscr("wp_%s" % key, [npart, nkc, ncols])
        WP[key] = (t, npart, nkc, ncols)
        for kc in range(nkc):
            src = src_rows_fn(kc)
            P.add("pool", lambda e, o=t[:, kc, :], i=src: e.dma_start(out=o, in_=i), r=(), w=(("wp", key, kc),), dma=True)

    for pi in range(6):
        wpanel("abin%d" % pi, lambda kc, pi=pi: ab_w_in[kc * 128:(kc + 1) * 128, pi * 512:(pi + 1) * 512], 128, 8, 512)
    for nh in range(2):
        for kp in range(2):
            wpanel("about%d_%d" % (nh, kp),
                   lambda kc, nh=nh, kp=kp: ab_w_out[(kp * 8 + kc) * 64:(kp * 8 + kc + 1) * 64, nh * 512:(nh + 1) * 512], 64, 8, 512)
    for pi in range(3):
        wpanel("cdin%d" % pi, lambda kc, pi=pi: cd_w_in[kc * 128:(kc + 1) * 128, pi * 512:(pi + 1) * 512], 128, 8, 512)
    wpanel("cdcf", lambda kc: cd_w_in[kc * 128:(kc + 1) * 128, 1536:1544], 128, 8, 8)
    for pi in range(2):
        wpanel("cdin%d" % (3 + pi), lambda kc, pi=pi: cd_w_in[kc * 128:(kc + 1) * 128, 1544 + pi * 512:1544 + (pi + 1) * 512], 128, 8, 512)
    for nh in range(2):
        wpanel("cdoutc%d" % nh, lambda kc, nh=nh: cd_w_out[kc * 64:(kc + 1) * 64, nh * 512:(nh + 1) * 512], 64, 8, 512)
        wpanel("cdoutd%d" % nh, lambda kc, nh=nh: cd_w_out[512 + kc * 128:512 + (kc + 1) * 128, nh * 512:(nh + 1) * 512], 128, 4, 512)
    HP = [(0, 512), (512, 512), (1024, 512), (1536, 512), (2048, 512), (2560, 256)]
    KP2 = [(0, 8), (8, 8), (16, 6)]
    for L in range(2):
        for pi, (c0, cn) in enumerate(HP):
            wpanel("w1_%d_%d" % (L, pi), lambda kc, L=L, c0=c0, cn=cn: w1[L, kc * 128:(kc + 1) * 128, c0:c0 + cn], 128, 8, cn)
            wpanel("w3_%d_%d" % (L, pi), lambda kc, L=L, c0=c0, cn=cn: w3[L, kc * 128:(kc + 1) * 128, c0:c0 + cn], 128, 8, cn)
        for nh in range(2):
            for kp, (k0, kn) in enumerate(KP2):
                wpanel("w2_%d_%d_%d" % (L, nh, kp),
                       lambda kc, L=L, nh=nh, k0=k0: w2[L, (k0 + kc) * 128:(k0 + kc + 1) * 128, nh * 512:(nh + 1) * 512], 128, kn, 512)

    SCR = {}
    for g, Tt, nkt in (("p", SEQ, NKT_P), ("s", PAST + DEC, NKT_S)):
        for m in "ABC":
            SCR[(g, m, "k")] = dscr("kt_%s%s" % (g, m), [8, 64, Tt])
            if m == "A":
                SCR[(g, m, "v")] = dscr("v_%s%s" % (g, m), [4, 128, nkt, 130])
            else:
                SCR[(g, m, "v")] = dscr("v_%s%s" % (g, m), [8, 128, nkt, 65])
    text = dscr("text", [8, 1536], F32)
    bbias = dscr("bbias", [8, 128, 8, TT])

    ident_f = sb("ident_f", [128, 128]); ident_b = sb("ident_b", [128, 128], BF16)
    tri_f = sb("tri_f", [128, 128]); sel_f = sb("sel_f", [128, 128]); anti_f = sb("anti_f", [128, 128])
    ones_f = sb("ones_f", [128, 64]); ones_b = sb("ones_b", [128, 64], BF16)
    t0m = sb("t0m", [128, 4, TT]); cmask = sb("cmask", [128, 4, TT], BF16)
    bkA = sb("bkA", [128, 4, NDA])
    gbc = sb("gbc", [128, 4, D], BF16); gfin = sb("gfin", [128, D])
    cfb = sb("cfb", [128, 8])
    brt = sb("brt", [8, 257]); txs = sb("txs", [8, 1536])
    neglam = sb("neglam", [128, 1]); lamt = sb("lamt", [128, 4, 64]); lamp = sb("lamp", [128, 2, 64]); lams = sb("lams", [128, 2])
    subg = sb("subg", [64, 2])
    spt = sb("spt", [128, 4]); m8sp = sb("m8sp", [128, 4]); m16sp = sb("m16sp", [128, 4])
    convw = sb("convw", [128, 4, 4]); convb = sb("convb", [128, 4]); bat = sb("bat", [128, 4]); bxt = sb("bxt", [128, 4])
    bdf = sb("bdf", [128, 4, 128]); bda = sb("bda", [128, 4, 128], BF16); bdx = sb("bdx", [128, 4, 128], BF16)
    xres = sb("xres", [128, 4, D])
    xn = sb("xn", [128, D], BF16); sqj = sb("sqj", [128, D], BF16)
    ss4 = sb("ss4", [128, 4]); ss4n = sb("ss4n", [128, 4]); ln4 = sb("ln4", [128, 4]); rstd4 = sb("rstd4", [128, 4])
    hT = sb("hT", [128, 8, TT], BF16)
    QT1 = sb("QT1", [128, 8, TT], BF16)
    QT = {"A": QT1, "B": QT1, "C": QT1}
    KTs = sb("KTs", [64, 8, TT], BF16)
    Vs = sb("Vs", [128, 4, 8, 65], BF16)
    kvo = [sb("kvo%d" % i, [128, 512]) for i in range(2)]
    odT = sb("odT", [128, 4, TT], BF16)
    GT = sb("GT", [128, 22, TT], BF16)
    OT = GT
    NW = 3
    wring = [sb("wring%d" % i, [128, 8, 512], BF16) for i in range(NW)]
    NKR = 3
    kring = [sb("kring%d" % i, [128, CH], BF16) for i in range(NKR)]
    vring = [sb("vring%d" % i, [128, CH // 128, 130], BF16) for i in range(NKR)]
    NPT = 4
    ptr = [sb("ptr%d" % i, [128, TT], BF16) for i in range(NPT)]
    dtmp = [sb("dtmp%d" % i, [128, TT]) for i in range(2)]
    bbr = [sb("bbr%d" % i, [128, 8, TT], BF16) for i in range(1)]
    rl = sb("rl", [65, TT])
    TB = [sb("tb%d" % i, [128, TT]) for i in range(8)]
    sq2 = sb("sq2", [64, 2, TT], BF16)
    negF = {"p": sb("negF_p", [128, NKT_P, 8]), "s": sb("negF_s", [128, NKT_S, 8])}
    lft = sb("lft", [128, 8]); lfo = sb("lfo", [128, 8]); lfb = sb("lfb", [128, 8], BF16); FTb = sb("FTb", [8, TT], BF16)
    dxbuf = {"p": sb("dxbuf_p", [128, 4, TT + 3]), "s": sb("dxbuf_s", [128, 4, DEC + 3])}
    hst = {"p": sb("hst_p", [128, 4]), "s": sb("hst_s", [128, 4])}
    ub = sb("ub", [128, TT], BF16)

    ps = [nc.alloc_psum_tensor("ps%d" % i, [128, 512], F32) for i in range(7)]
    ptb = nc.alloc_psum_tensor("ptb", [128, 8, 128], BF16)

    rot = {"sps": 0, "w": 0, "k": 0, "pt": 0, "dt": 0, "bb": 0, "kvo": 0, "ps": 0}

    def nxt(name, n):
        v = rot[name]
        rot[name] = (v + 1) % n
        return v

    def dma(q, out, in_, r, w, **kw):
        return P.add(q, lambda e: e.dma_start(out=out, in_=in_, **kw), r=r, w=w, dma=True)

    def load_w(key):
        t, npart, nkc, ncols = WP[key]
        s = nxt("w", NW)
        dma("sp", wring[s][0:npart, 0:nkc, 0:ncols], t[:, :, :], r=tuple(("wp", key, kc) for kc in range(nkc)), w=(("wr", s),))
        return s

    def prologue():
        for k, t in (("ident_f", ident_f), ("ident_b", ident_b), ("tri_f", tri_f), ("sel_f", sel_f), ("anti_f", anti_f), ("t0m", t0m),
                     ("cmask", cmask), ("bkA", bkA)):
            sl = tuple(slice(None) for _ in consts[k].shape)
            dma("sp", t[sl], cin[k][sl], r=(), w=(k,))
        P.add("pool", lambda e: e.memset(ones_f[:, :], 1.0), w=("ones_f",))
        P.add("pool", lambda e: e.memset(ones_b[:, :], 1.0), w=("ones_b",))
        P.add("pool", lambda e: e.memset(QT1[64:128, :, :], 0.0), w=("QT_z", "QT_aug"))
        for i in range(NKR):
            P.add("pool", lambda e, i=i: e.memset(kring[i][64:128, :], 0.0), w=(("kr1", i),))
            P.add("pool", lambda e, i=i: e.memset(kring[i][64:65, :], 1.0), w=(("kr1", i),))
            P.add("pool", lambda e, i=i: e.memset(vring[i][:, :, :], 0.0), w=(("vr0", i), ("vr", i)))
        P.add("pool", lambda e: e.memset(Vs[:, :, :, 64:65], 1.0), w=("Vs1",))
        for n in range(4):
            for hh in range(2):
                dma("sp", TB[hh][:, :], gT_in[n, hh * 512:(hh + 1) * 512].partition_broadcast(128), r=(), w=(("T", hh),))
                P.add("dve", lambda e, n=n, hh=hh: e.tensor_copy(out=gbc[:, n, hh * 512:(hh + 1) * 512], in_=TB[hh][:, :]), r=(("T", hh),), w=("gbc",))
        dma("sp", gfin[:, :], fing.partition_broadcast(128), r=(), w=("gfin",))
        dma("sp", cfb[:, :], c_f_bias.partition_broadcast(128), r=(), w=("cfb",))
        dma("sp", lamt[:, :, :], a_lambda.partition_broadcast(128), r=(), w=("lamt",))
        P.add("dve", lambda e: e.tensor_tensor(out=lamp[:, 0, :], in0=lamt[:, 0, :], in1=lamt[:, 1, :], op=ALU.mult), r=("lamt",), w=("lamp0",))
        P.add("dve", lambda e: e.tensor_tensor(out=lamp[:, 1, :], in0=lamt[:, 2, :], in1=lamt[:, 3, :], op=ALU.mult), r=("lamt",), w=("lamp1",))
        P.add("dve", lambda e: e.reduce_sum(out=lams[:, :], in_=lamp[:, :, :], axis=AX.X), r=("lamp0", "lamp1"), w=("lams",))
        P.add("act", lambda e: e.activation(out=lams[:, :], in_=lams[:, :], func=AF.Exp), r=("lams",), w=("lams",))
        lam_init = 0.8 - 0.6 * math.exp(0.0)
        P.add("dve", lambda e: e.tensor_tensor(out=neglam[:, :], in0=lams[:, 1:2], in1=lams[:, 0:1], op=ALU.subtract), r=("lams",), w=("neglam",))
        P.add("dve", lambda e: e.tensor_scalar(out=neglam[:, :], in0=neglam[:, :], scalar1=-lam_init, scalar2=None, op0=ALU.add), r=("neglam",), w=("neglam",))
        dma("sp", subg[:, :], a_subln[:, :], r=(), w=("subg",))
        P.add("dve", lambda e: e.tensor_scalar(out=subg[:, :], in0=subg[:, :], scalar1=1.0 - lam_init, scalar2=None, op0=ALU.mult), r=("subg",), w=("subg",))
        dma("sp", spt[:, :], d_lam[:, :], r=(), w=("spt",))
        P.add("act", lambda e: e.activation(out=spt[:, :], in_=spt[:, :], func=AF.Exp, scale=-1.0), r=("spt",), w=("spt",))
        P.add("act", lambda e: e.activation(out=spt[:, :], in_=spt[:, :], func=AF.Ln, bias=1.0), r=("spt",), w=("spt",))
        P.add("dve", lambda e: e.tensor_scalar(out=m8sp[:, :], in0=spt[:, :], scalar1=-8.0, scalar2=None, op0=ALU.mult), r=("spt",), w=("m8sp",))
        P.add("dve", lambda e: e.tensor_scalar(out=m16sp[:, :], in0=spt[:, :], scalar1=-16.0, scalar2=None, op0=ALU.mult), r=("spt",), w=("m16sp",))
        dma("sp", convw[:, :, :], d_conv_w[:, :, :], r=(), w=("convw",))
        for t, src, k in ((convb, d_conv_b, "convb"), (bat, d_b_a, "bat"), (bxt, d_b_x, "bxt")):
            dma("sp", t[:, :], src[:, :], r=(), w=(k,))
        for wsrc, dst, k in ((d_w_a, bda, "bda"), (d_w_x, bdx, "bdx")):
            P.add("dve", lambda e: e.memset(bdf[:, :, :], 0.0), w=("bdf",))
            for nn in range(2):
                src = wsrc.rearrange("(c two) i j -> two i c j", two=2)[nn]
                dma("sp", bdf[nn * 64:(nn + 1) * 64, :, nn * 64:(nn + 1) * 64], src, r=(), w=("bdf",))
            P.add("dve", lambda e, dst=dst: e.tensor_copy(out=dst[:, :, :], in_=bdf[:, :, :]), r=("bdf",), w=(k,))
        dma("sp", brt[:, :], b_rel[:, :], r=(), w=("brt",))
        P.add("dve", lambda e: e.memset(txs[:, :], 0.0), w=("txs",))
        P.add("dve", lambda e: e.tensor_scalar(out=txs[:, 0:383], in0=txs[:, 0:383], scalar1=brt[:, 0:1], scalar2=None, op0=ALU.add), r=("brt", "txs"), w=("txs",))
        P.add("dve", lambda e: e.tensor_copy(out=txs[:, 383:640], in_=brt[:, 0:257]), r=("brt", "txs"), w=("txs",))
        P.add("dve", lambda e: e.tensor_scalar(out=txs[:, 640:1536], in0=txs[:, 640:1536], scalar1=brt[:, 256:257], scalar2=None, op0=ALU.add), r=("brt", "txs"), w=("txs",))
        dma("pool", text[:, :], txs[:, :], r=("txs",), w=("text",))
        for h in range(8):
            for dk, dv in enumerate(dvals):
                s = nxt("dt", 2)
                src = bass.AP(tensor=text.tensor, offset=h * 1536 + dv + 384, ap=[[1, 128], [1, TT]])
                dma("sp", dtmp[s][:, :], src, r=("text",), w=(("dtmp", s),))
                bnk = nxt("ps", 6)
                P.add("pe", lambda e, s=s, bnk=bnk: e.matmul(ps[bnk][:, :], lhsT=anti_f[:, :], rhs=dtmp[s][:, :], start=True, stop=True),
                      r=(("dtmp", s), "anti_f"), w=(("ps", bnk),))
                pslot = nxt("pt", NPT)
                dma_in = cin["bmask"]
                P.add("pool", lambda e, pslot=pslot, dk=dk: e.dma_start(out=ptr[pslot][:, :], in_=dma_in[:, dk, :]),
                      r=(), w=(("ptr", pslot),), dma=True)
                P.add("dve", lambda e, bnk=bnk, pslot=pslot: e.tensor_tensor(out=ptr[pslot][:, :], in0=ps[bnk][:, :], in1=ptr[pslot][:, :], op=ALU.add),
                      r=(("ps", bnk), ("ptr", pslot)), w=(("ptr", pslot),))
                dma("pool", bbias[h, :, dk, :], ptr[pslot][:, :], r=(("ptr", pslot),), w=("bbias",))

    def rstd_all(G):
        bs, nblk = G["bs"], G["nblk"]
        for b in range(nblk):
            P.add("dve", lambda e, b=b: e.tensor_tensor(out=sqj[0:bs, :], in0=xres[0:bs, b, :], in1=xres[0:bs, b, :], op=ALU.mult), r=(("xres", b),), w=("sqj",))
            P.add("dve", lambda e, b=b: e.reduce_sum(out=ss4[0:bs, b:b + 1], in_=sqj[0:bs, :], axis=AX.X), r=("sqj",), w=(("ss4", b),))
        P.add("dve", lambda e: e.tensor_scalar(out=ss4n[0:bs, 0:nblk], in0=ss4[0:bs, 0:nblk], scalar1=1.0 / D, scalar2=EPS, op0=ALU.mult, op1=ALU.add),
              r=tuple(("ss4", b) for b in range(nblk)), w=("ss4n",))
        P.add("act", lambda e: e.activation(out=ln4[0:bs, 0:nblk], in_=ss4n[0:bs, 0:nblk], func=AF.Ln), r=("ss4n",), w=("ln4",))
        P.add("act", lambda e: e.activation(out=rstd4[0:bs, 0:nblk], in_=ln4[0:bs, 0:nblk], func=AF.Exp, scale=-0.5), r=("ln4",), w=("rstd4",))

    def norm_T(G, nidx):
        bs, nblk, ntok = G["bs"], G["nblk"], G["ntok"]
        rstd_all(G)
        for b in range(nblk):
            P.add("dve", lambda e, b=b: e.scalar_tensor_tensor(out=xn[0:bs, :], in0=xres[0:bs, b, :], scalar=rstd4[0:bs, b:b + 1], in1=gbc[0:bs, nidx, :], op0=ALU.mult, op1=ALU.mult),
                  r=(("xres", b), "rstd4", "gbc"), w=("xn",))
            for kc in range(8):
                P.add("pe", lambda e, kc=kc: e.transpose(out=ptb[:, kc, :], in_=xn[:, kc * 128:(kc + 1) * 128], identity=ident_b[:, :]),
                      r=("xn", "ident_b"), w=(("ptb", kc),))
            evac(hT[:, :, b * bs:(b + 1) * bs], ptb[:, :, 0:bs], r=tuple(("ptb", kc) for kc in range(8)), w=(("hT", b, 0), ("hT", b, 1)))

    def hT_keys(G):
        return tuple(("hT", b, j) for b in range(G["nblk"]) for j in range(2))

    evac_tog = [0]

    def evac(out, in_, r, w, scale=None):
        evac_tog[0] ^= 1
        if evac_tog[0]:
            if scale is None:
                P.add("act", lambda e: e.activation(out=out, in_=in_, func=AF.Copy), r=r, w=w)
            else:
                P.add("act", lambda e: e.activation(out=out, in_=in_, func=AF.Copy, scale=float(scale)), r=r, w=w)
        else:
            if scale is None:
                P.add("dve", lambda e: e.tensor_copy(out=out, in_=in_), r=r, w=w)
            else:
                P.add("dve", lambda e: e.tensor_scalar(out=out, in0=in_, scalar1=float(scale), scalar2=None, op0=ALU.mult), r=r, w=w)

    def proj_units(G, ws, units, dst_fn, dst_keys_fn, scale=None):
        ntok = G["ntok"]
        for u in units:
            bnk = nxt("ps", 6)
            for kc in range(8):
                P.add("pe", lambda e, kc=kc, u=u, bnk=bnk: e.matmul(ps[bnk][0:64, 0:ntok], lhsT=wring[ws][:, kc, u * 64:(u + 1) * 64], rhs=hT[:, kc, 0:ntok],
                                                                    start=(kc == 0), stop=(kc == 7)),
                      r=(("wr", ws),) + hT_keys(G), w=(("ps", bnk),))
            evac(dst_fn(u), ps[bnk][0:64, 0:ntok], r=(("ps", bnk),), w=dst_keys_fn(u), scale=scale)

    def proj_tok(G, ws, ncols, cb):
        bs, nblk = G["bs"], G["nblk"]
        for b in range(nblk):
            bnk = nxt("ps", 6)
            for kc in range(8):
                P.add("pe", lambda e, kc=kc, b=b, bnk=bnk: e.matmul(ps[bnk][0:bs, 0:ncols], lhsT=hT[:, kc, b * bs:(b + 1) * bs], rhs=wring[ws][:, kc, 0:ncols],
                                                                    start=(kc == 0), stop=(kc == 7)),
                      r=(("wr", ws), ("hT", b, 0), ("hT", b, 1)), w=(("ps", bnk),))
            cb(b, bnk)

    def out_rows(G, name, b):
        bs = G["bs"]
        r0 = (G["orow_b"] if name in ("b_k", "b_v") else G["orow"]) + b * bs
        return O[G["g"] + name][r0:r0 + bs, :]

    def k_tok_out(G, ws, name, also=None):
        bs = G["bs"]
        if name is None and also is None:
            return

        def cb(b, bnk):
            s = nxt("kvo", 2)
            evac(kvo[s][0:bs, :], ps[bnk][0:bs, 0:512], r=(("ps", bnk),), w=(("kvo", s),))
            if name is not None:
                dma("pool", out_rows(G, name, b), kvo[s][0:bs, :], r=(("kvo", s),), w=(("out", name, G["g"], G["orow"], b),))
            if also is not None:
                also(b, bnk, s)
        proj_tok(G, ws, 512, cb)

    def v_stage(G, b, s):
        bs = G["bs"]
        P.add("dve", lambda e: e.tensor_copy(out=Vs[0:bs, b, :, 0:64], in_=kvo[s][0:bs, :].rearrange("p (u d) -> p u d", d=64)),
              r=(("kvo", s), "Vs1"), w=(("Vs", b),))

    def write_kv_scratch(G, m):
        g, bs, nblk, ntok, q0 = G["g"], G["bs"], G["nblk"], G["ntok"], G["q0"]
        kd = SCR[(g, m, "k")]
        dma("pool", kd[:, :, q0:q0 + ntok].rearrange("u d t -> d u t"), KTs[:, :, 0:ntok], r=tuple(("KTs", u) for u in range(8)), w=((g, m, "k", q0 // TT),))
        vd = SCR[(g, m, "v")]
        kt0 = q0 // 128
        vkeys = tuple(("Vs", b) for b in range(nblk))
        if m == "A":
            for h in range(4):
                src = Vs[0:bs, 0:nblk, 2 * h:2 * h + 2, :].rearrange("p b two x -> p b (two x)")
                dma("pool", vd[h, 0:bs, kt0:kt0 + nblk, :], src, r=vkeys, w=((g, m, "v", q0 // TT, h),))
        else:
            for u in range(8):
                dma("pool", vd[u, 0:bs, kt0:kt0 + nblk, :], Vs[0:bs, 0:nblk, u, :], r=vkeys, w=((g, m, "v", q0 // TT, u),))

    oset = [0]

    def attention(G, m):
        g, ntok, q0 = G["g"], G["ntok"], G["q0"]
        nq = ntok
        kd, vd = SCR[(g, m, "k")], SCR[(g, m, "v")]
        nhalf = 2 if m == "A" else 1
        xw = 65 * nhalf
        kstart = 0 if m != "B" else max(G["kmin"], q0 - 512)
        kend = q0 + ntok
        pend_fin = [None]

        def fin_recip(ob):
            P.add("dve", lambda e: e.reciprocal(out=rl[64:65, 0:nq], in_=ps[ob][64:65, 0:nq]), r=(("ps", ob),), w=("rl",))

        def finalize(u, ob, h):
            P.add("pe", lambda e: e.matmul(ps[6][0:64, 0:nq], lhsT=ones_f[64:65, 0:64], rhs=rl[64:65, 0:nq], start=True, stop=True),
                  r=("rl", "ones_f"), w=(("ps", 6),))
            P.add("act", lambda e: e.activation(out=TB[7][0:64, 0:nq], in_=ps[6][0:64, 0:nq], func=AF.Copy), r=(("ps", 6),), w=(("T", 7),))
            for hf in range(nhalf):
                bnk = ob + hf
                if m == "A":
                    cc = u % 2
                    P.add("dve", lambda e, bnk=bnk, cc=cc, hf=hf: e.tensor_tensor(out=TB[cc * 2 + hf][0:64, 0:nq], in0=ps[bnk][0:64, 0:nq], in1=TB[7][0:64, 0:nq], op=ALU.mult),
                          r=(("ps", bnk), ("T", 7)), w=(("T", cc * 2 + hf),))
                else:
                    chunk = 8 + u if m == "B" else u
                    P.add("dve", lambda e, bnk=bnk, chunk=chunk: e.tensor_tensor(out=OT[0:64, chunk, 0:nq], in0=ps[bnk][0:64, 0:nq], in1=TB[7][0:64, 0:nq], op=ALU.mult),
                          r=(("ps", bnk), ("T", 7)), w=(("GT", chunk),))
            if m == "A" and u % 2 == 1:
                for hf in range(2):
                    P.add("dve", lambda e, hf=hf: e.scalar_tensor_tensor(out=TB[4 + hf][0:64, 0:nq], in0=TB[2 + hf][0:64, 0:nq], scalar=neglam[0:64, 0:1], in1=TB[hf][0:64, 0:nq], op0=ALU.mult, op1=ALU.add),
                          r=(("T", hf), ("T", 2 + hf), "neglam"), w=(("T", 4 + hf),))
                    P.add("act", lambda e, hf=hf: e.activation(out=sq2[0:64, hf, 0:nq], in_=TB[4 + hf][0:64, 0:nq], func=AF.Square), r=(("T", 4 + hf),), w=(("sq2", hf),))
                for hf in range(2):
                    P.add("pe", lambda e, hf=hf: e.matmul(ps[6][0:64, 0:nq], lhsT=ones_b[0:64, 0:64], rhs=sq2[0:64, hf, 0:nq], start=(hf == 0), stop=(hf == 1)),
                          r=(("sq2", hf), "ones_b"), w=(("ps", 6),))
                P.add("dve", lambda e: e.tensor_scalar(out=TB[6][0:64, 0:nq], in0=ps[6][0:64, 0:nq], scalar1=1.0 / 128, scalar2=EPS, op0=ALU.mult, op1=ALU.add), r=(("ps", 6),), w=(("T", 6),))
                P.add("act", lambda e: e.activation(out=TB[6][0:64, 0:nq], in_=TB[6][0:64, 0:nq], func=AF.Ln), r=(("T", 6),), w=(("T", 6),))
                P.add("act", lambda e: e.activation(out=TB[6][0:64, 0:nq], in_=TB[6][0:64, 0:nq], func=AF.Exp, scale=-0.5), r=(("T", 6),), w=(("T", 6),))
                for hf in range(2):
                    P.add("dve", lambda e, hf=hf, h=h: e.scalar_tensor_tensor(out=OT[0:64, 2 * h + hf, 0:nq], in0=TB[4 + hf][0:64, 0:nq], scalar=subg[0:64, hf:hf + 1], in1=TB[6][0:64, 0:nq], op0=ALU.mult, op1=ALU.mult),
                          r=(("T", 4 + hf), ("T", 6), "subg"), w=(("GT", 2 * h + hf),))

        def flush_fin():
            if pend_fin[0] is not None:
                f = pend_fin[0]
                pend_fin[0] = None
                f()

        for u in range(8):
            ob = 2 + 2 * oset[0]
            oset[0] ^= 1
            h = u // 2 if m == "A" else u
            vh = h if m == "A" else u
            if m == "B":
                s_bb = 0
                for k0_ in range(kstart, kend, 128):
                    dk_ = (512 - (q0 - k0_)) // 128
                    dma("sp", bbr[0][:, dk_, :], bbias[u, :, dk_, :], r=("bbias",), w=(("bbr", dk_),))
            first = True
            c0 = (kstart // CH) * CH
            chunks = []
            while c0 < kend:
                chunks.append((max(c0, kstart), min(c0 + CH, kend)))
                c0 += CH
            pend_pv = None
            ntile = 0
            for (lo, hi) in chunks:
                s = nxt("k", NKR)
                dkeys = tuple((g, m, "k", t) for t in range(lo // TT, (hi - 1) // TT + 1))
                dma("sp", kring[s][0:64, 0:hi - lo], kd[u, :, lo:hi], r=dkeys, w=(("kr", s),))
                kt_lo, kt_hi = lo // 128, (hi + 127) // 128
                vkeys = tuple((g, m, "v", t, vh) for t in range(lo // TT, (hi - 1) // TT + 1))
                dma("sp", vring[s][:, 0:kt_hi - kt_lo, 0:xw], vd[vh, :, kt_lo:kt_hi, :], r=vkeys, w=(("vr", s),))
                k0 = lo
                while k0 < hi:
                    nk = min(128, hi - k0)
                    off = k0 - lo
                    kt = k0 // 128 - kt_lo
                    last = (k0 + nk >= kend)
                    sbk = nxt("sps", 2)
                    P.add("pe", lambda e, s=s, off=off, nk=nk, sbk=sbk, u=u: e.matmul(ps[sbk][0:nk, 0:nq], lhsT=kring[s][0:128, off:off + nk], rhs=QT[m][0:128, u, 0:nq], start=True, stop=True),
                          r=(("kr", s), ("kr1", s), ("QT", u), "QT_aug", "QT_z"), w=(("ps", sbk),))
                    diag = k0 >= q0
                    bias = 0.0
                    src = ps[sbk][0:nk, 0:nq]
                    srck = (("ps", sbk),)
                    if m == "A":
                        if diag:
                            js = (k0 - q0) // 128
                            ds = nxt("dt", 2)
                            P.add("dve", lambda e, js=js, ds=ds, nk=nk, sbk=sbk, h=h: e.scalar_tensor_tensor(out=dtmp[ds][0:nk, 0:nq], in0=t0m[0:nk, js, 0:nq], scalar=float(slopes[h]), in1=ps[sbk][0:nk, 0:nq], op0=ALU.mult, op1=ALU.add),
                                  r=(("ps", sbk), "t0m"), w=(("dtmp", ds),))
                            src = dtmp[ds][0:nk, 0:nq]; srck = (("dtmp", ds),)
                        else:
                            d128 = (q0 - k0) // 128
                            bias = bkA[0:nk, h, d128:d128 + 1]
                    elif m == "C":
                        ktabs = k0 // 128
                        bias = negF[g][0:nk, ktabs, u:u + 1]
                        if diag:
                            js = (k0 - q0) // 128
                            ds = nxt("dt", 2)
                            P.add("dve", lambda e, js=js, ds=ds, nk=nk, sbk=sbk: e.tensor_tensor(out=dtmp[ds][0:nk, 0:nq], in0=ps[sbk][0:nk, 0:nq], in1=cmask[0:nk, js, 0:nq], op=ALU.add),
                                  r=(("ps", sbk), "cmask"), w=(("dtmp", ds),))
                            src = dtmp[ds][0:nk, 0:nq]; srck = (("dtmp", ds),)
                    else:
                        dk = (512 - (q0 - k0)) // 128
                        ds = nxt("dt", 2)
                        P.add("dve", lambda e, dk=dk, ds=ds, nk=nk, sbk=sbk, s_bb=s_bb: e.tensor_tensor(out=dtmp[ds][0:nk, 0:nq], in0=ps[sbk][0:nk, 0:nq], in1=bbr[s_bb][0:nk, dk, 0:nq], op=ALU.add),
                              r=(("ps", sbk), ("bbr", dk)), w=(("dtmp", ds),))
                        src = dtmp[ds][0:nk, 0:nq]; srck = (("dtmp", ds),)
                    pslot = nxt("pt", NPT)
                    bkeys = ("bkA", ("negF", g)) if not isinstance(bias, float) else ()
                    P.add("act", lambda e, pslot=pslot, src=src, bias=bias, nk=nk: e.activation(out=ptr[pslot][0:nk, 0:nq], in_=src, func=AF.Exp, bias=bias),
                          r=srck + bkeys, w=(("ptr", pslot),))

                    def pv(s=s, kt=kt, nk=nk, pslot=pslot, first=first, last=last, ob=ob):
                        for hf in range(nhalf):
                            if hf == 0:
                                P.add("pe", lambda e: e.matmul(ps[ob][0:128, 0:nq], lhsT=vring[s][0:nk, kt, 0:128], rhs=ptr[pslot][0:nk, 0:nq], start=first, stop=last),
                                      r=(("vr", s), ("vr0", s), ("ptr", pslot)), w=(("ps", ob),))
                            else:
                                P.add("pe", lambda e: e.matmul(ps[ob + 1][0:65, 0:nq], lhsT=vring[s][0:nk, kt, 65:130], rhs=ptr[pslot][0:nk, 0:nq], start=first, stop=last),
                                      r=(("vr", s), ("vr0", s), ("ptr", pslot)), w=(("ps", ob + 1),))
                    if pend_pv is not None:
                        pend_pv()
                    pend_pv = pv
                    ntile += 1
                    if ntile == 5:
                        flush_fin()
                    first = False
                    k0 += nk
            if pend_pv is not None:
                pend_pv()
            flush_fin()
            fin_recip(ob)
            pend_fin[0] = (lambda u=u, ob=ob, h=h: finalize(u, ob, h))
        flush_fin()

    def out_proj(G, panels):
        bs, nblk = G["bs"], G["nblk"]
        for nh in range(2):
            total = sum(pn[2] for pn in panels[nh])
            cnt = [0] * nblk
            for (wkey, npart, nkc, lhs_fn) in panels[nh]:
                ws = load_w(wkey)
                for b in range(nblk):
                    for kc in range(nkc):
                        ap, keys = lhs_fn(kc, b)
                        st = (cnt[b] == 0)
                        cnt[b] += 1
                        sp_ = (cnt[b] == total)
                        P.add("pe", lambda e, ap=ap, ws=ws, kc=kc, b=b, st=st, sp_=sp_, npart=npart: e.matmul(ps[2 + b][0:bs, 0:512], lhsT=ap, rhs=wring[ws][0:npart, kc, 0:512], start=st, stop=sp_),
                              r=(("wr", ws),) + keys, w=(("ps", 2 + b),))
            for b in range(nblk):
                P.add("dve", lambda e, b=b, nh=nh: e.tensor_tensor(out=xres[0:bs, b, nh * 512:(nh + 1) * 512], in0=ps[2 + b][0:bs, 0:512], in1=xres[0:bs, b, nh * 512:(nh + 1) * 512], op=ALU.add),
                      r=(("ps", 2 + b), ("xres", b)), w=(("xres", b),))

    def ffn(G, L):
        bs, nblk, ntok = G["bs"], G["nblk"], G["ntok"]
        norm_T(G, 2 * L + 1)
        for pi, (c0, cn) in enumerate(HP):
            wa = load_w("w1_%d_%d" % (L, pi))
            wb = load_w("w3_%d_%d" % (L, pi))
            for j in range(cn // 128):
                hc = c0 // 128 + j
                b1 = nxt("ps", 6)
                for kc in range(8):
                    P.add("pe", lambda e, kc=kc, j=j, b1=b1, wa=wa: e.matmul(ps[b1][:, 0:ntok], lhsT=wring[wa][:, kc, j * 128:(j + 1) * 128], rhs=hT[:, kc, 0:ntok], start=(kc == 0), stop=(kc == 7)),
                          r=(("wr", wa),) + hT_keys(G), w=(("ps", b1),))
                b3 = nxt("ps", 6)
                for kc in range(8):
                    P.add("pe", lambda e, kc=kc, j=j, b3=b3, wb=wb: e.matmul(ps[b3][:, 0:ntok], lhsT=wring[wb][:, kc, j * 128:(j + 1) * 128], rhs=hT[:, kc, 0:ntok], start=(kc == 0), stop=(kc == 7)),
                          r=(("wr", wb),) + hT_keys(G), w=(("ps", b3),))
                ds = nxt("dt", 2)
                P.add("act", lambda e, b1=b1, ds=ds: e.activation(out=dtmp[ds][:, 0:ntok], in_=ps[b1][:, 0:ntok], func=AF.Silu), r=(("ps", b1),), w=(("dtmp", ds),))
                P.add("dve", lambda e, b3=b3, ds=ds, hc=hc: e.tensor_tensor(out=GT[:, hc, 0:ntok], in0=ps[b3][:, 0:ntok], in1=dtmp[ds][:, 0:ntok], op=ALU.mult),
                      r=(("ps", b3), ("dtmp", ds)), w=(("GT", hc),))
        panels = []
        for nh in range(2):
            pl = []
            for kp, (k0, kn) in enumerate(KP2):
                pl.append(("w2_%d_%d_%d" % (L, nh, kp), 128, kn,
                           lambda kc, b, k0=k0: (GT[:, k0 + kc, b * bs:(b + 1) * bs], (("GT", k0 + kc),))))
            panels.append(pl)
        out_proj(G, panels)

    def layer0(G):
        g, bs, nblk, ntok = G["g"], G["bs"], G["nblk"], G["ntok"]
        chk(3.1)
        norm_T(G, 0)
        chk(3.2)
        for m, base in (("A", 0), ("B", 3)):
            ws = load_w("abin%d" % base)
            proj_units(G, ws, range(8), lambda u, m=m: QT[m][0:64, u, 0:ntok], lambda u, m=m: (("QT", u),), scale=0.125)
            chk(3.4)
            ws = load_w("abin%d" % (base + 1))
            proj_units(G, ws, range(8), lambda u: KTs[0:64, u, 0:ntok], lambda u: (("KTs", u),))
            chk(3.5)
            kname = ("a_k" if m == "A" else "b_k")
            vname = ("a_v" if m == "A" else "b_v")
            if m == "B":
                if g == "p":
                    kname = kname if G["last"] else None
                    vname = vname if G["last"] else None
            k_tok_out(G, ws, kname)
            chk(3.6)
            ws = load_w("abin%d" % (base + 2))
            chk(3.62)
            import os
            if os.environ.get("NOVS", "0") == "1":
                k_tok_out(G, ws, vname)
            else:
                k_tok_out(G, ws, vname, also=lambda b, bnk, s: v_stage(G, b, s))
            chk(4)
            write_kv_scratch(G, m)
            chk(5)
            if m == "A":
                dma("sp", QT1[64:65, :, :], cin["cqA"][:, :, :], r=(), w=("QT_aug",))
            else:
                P.add("pool", lambda e: e.memset(QT1[64:65, :, :], 0.0), w=("QT_aug",))
            attention(G, m)
            chk(6)
        panels = []
        for nh in range(2):
            pl = []
            for kp in range(2):
                pl.append(("about%d_%d" % (nh, kp), 64, 8,
                           lambda kc, b, kp=kp: (OT[0:64, kp * 8 + kc, b * bs:(b + 1) * bs], (("GT", kp * 8 + kc),))))
            panels.append(pl)
        out_proj(G, panels)
        ffn(G, 0)

    def logf_block(G, b, bnk):
        g, bs = G["g"], G["bs"]
        P.add("dve", lambda e: e.tensor_tensor(out=lft[0:bs, :], in0=ps[bnk][0:bs, 0:8], in1=cfb[0:bs, :], op=ALU.add), r=(("ps", bnk), "cfb"), w=("lft",))
        P.add("act", lambda e: e.activation(out=lft[0:bs, :], in_=lft[0:bs, :], func=AF.Exp, scale=-1.0), r=("lft",), w=("lft",))
        P.add("act", lambda e: e.activation(out=lft[0:bs, :], in_=lft[0:bs, :], func=AF.Ln, bias=1.0), r=("lft",), w=("lft",))
        P.add("dve", lambda e: e.tensor_scalar(out=lfo[0:bs, :], in0=lft[0:bs, :], scalar1=-1.0, scalar2=None, op0=ALU.mult), r=("lft",), w=("lfo",))
        dma("pool", out_rows(G, "c_logf", b), lfo[0:bs, :], r=("lfo",), w=(("out", "c_logf", G["g"], G["orow"], b),))
        cum_block(G, G["q0"] // 128 + b, bs, b)

    def cum_block(G, ktabs, bs, b=None):
        g = G["g"]
        has_prev = ktabs > 0
        P.add("pe", lambda e: e.matmul(ps[6][0:bs, 0:8], lhsT=tri_f[0:bs, 0:bs], rhs=lft[0:bs, :], start=True, stop=not has_prev), r=("lft", "tri_f"), w=(("ps", 6),))
        if has_prev:
            P.add("pe", lambda e: e.matmul(ps[6][0:bs, 0:8], lhsT=sel_f[0:128, 0:bs], rhs=negF[g][0:128, ktabs - 1, :], start=False, stop=True),
                  r=(("negF", g), "sel_f"), w=(("ps", 6),))
        P.add("dve", lambda e: e.tensor_copy(out=negF[g][0:bs, ktabs, :], in_=ps[6][0:bs, 0:8]), r=(("ps", 6),), w=(("negF", g),))
        if b is not None:
            P.add("dve", lambda e: e.tensor_scalar(out=lfb[0:bs, :], in0=negF[g][0:bs, ktabs, :], scalar1=-1.0, scalar2=None, op0=ALU.mult), r=(("negF", g),), w=("lfb",))
            P.add("pe", lambda e: e.transpose(out=ptb[0:8, 0, :], in_=lfb[:, :], identity=ident_b[:, :]), r=("lfb", "ident_b"), w=(("ptb", 0),))
            P.add("act", lambda e: e.activation(out=FTb[0:8, b * bs:(b + 1) * bs], in_=ptb[0:8, 0, 0:bs], func=AF.Copy), r=(("ptb", 0),), w=("FTb",))

    def dbranch(G, wsx, wsg):
        g, bs, nblk, ntok = G["g"], G["bs"], G["nblk"], G["ntok"]
        xb = dxbuf[g]
        u_, r_, i_, a_, b_, x_, t_ = TB[0:7]
        for cb in range(4):
            bx = nxt("ps", 6)
            for kc in range(8):
                P.add("pe", lambda e, kc=kc, bx=bx, cb=cb: e.matmul(ps[bx][:, 0:ntok], lhsT=wring[wsx][:, kc, cb * 128:(cb + 1) * 128], rhs=hT[:, kc, 0:ntok], start=(kc == 0), stop=(kc == 7)),
                      r=(("wr", wsx),) + hT_keys(G), w=(("ps", bx),))
            bg = nxt("ps", 6)
            for kc in range(8):
                P.add("pe", lambda e, kc=kc, bg=bg, cb=cb: e.matmul(ps[bg][:, 0:ntok], lhsT=wring[wsg][:, kc, cb * 128:(cb + 1) * 128], rhs=hT[:, kc, 0:ntok], start=(kc == 0), stop=(kc == 7)),
                      r=(("wr", wsg),) + hT_keys(G), w=(("ps", bg),))
            kx = ("dxbuf", g, cb)
            P.add("dve", lambda e, bx=bx, cb=cb: e.tensor_copy(out=xb[:, cb, 3:3 + ntok], in_=ps[bx][:, 0:ntok]), r=(("ps", bx),), w=(kx,))
            P.add("dve", lambda e, cb=cb: e.tensor_scalar(out=u_[:, 0:ntok], in0=xb[:, cb, 3:3 + ntok], scalar1=convw[:, cb, 3:4], scalar2=convb[:, cb:cb + 1], op0=ALU.mult, op1=ALU.add),
                  r=(kx, "convw", "convb"), w=(("T", 0),))
            for j in range(3):
                P.add("dve", lambda e, cb=cb, j=j: e.scalar_tensor_tensor(out=u_[:, 0:ntok], in0=xb[:, cb, j:j + ntok], scalar=convw[:, cb, j:j + 1], in1=u_[:, 0:ntok], op0=ALU.mult, op1=ALU.add),
                      r=(kx, "convw", ("T", 0)), w=(("T", 0),))
            P.add("dve", lambda e, cb=cb: e.tensor_copy(out=xb[:, cb, 0:3], in_=xb[:, cb, ntok:ntok + 3]), r=(kx, ("T", 0)), w=(kx,))
            P.add("act", lambda e: e.activation(out=ub[:, 0:ntok], in_=u_[:, 0:ntok], func=AF.Copy), r=(("T", 0),), w=("ub",))
            ba = nxt("ps", 6)
            P.add("pe", lambda e, ba=ba, cb=cb: e.matmul(ps[ba][:, 0:ntok], lhsT=bda[:, cb, :], rhs=ub[:, 0:ntok], start=True, stop=True), r=("ub", "bda"), w=(("ps", ba),))
            bi = nxt("ps", 6)
            P.add("pe", lambda e, bi=bi, cb=cb: e.matmul(ps[bi][:, 0:ntok], lhsT=bdx[:, cb, :], rhs=ub[:, 0:ntok], start=True, stop=True), r=("ub", "bdx"), w=(("ps", bi),))
            P.add("act", lambda e, ba=ba, cb=cb: e.activation(out=r_[:, 0:ntok], in_=ps[ba][:, 0:ntok], func=AF.Sigmoid, bias=bat[:, cb:cb + 1]), r=(("ps", ba), "bat"), w=(("T", 1),))
            P.add("act", lambda e, bi=bi, cb=cb: e.activation(out=i_[:, 0:ntok], in_=ps[bi][:, 0:ntok], func=AF.Sigmoid, bias=bxt[:, cb:cb + 1]), r=(("ps", bi), "bxt"), w=(("T", 2),))
            P.add("act", lambda e, cb=cb: e.activation(out=a_[:, 0:ntok], in_=r_[:, 0:ntok], func=AF.Exp, scale=m8sp[:, cb:cb + 1]), r=(("T", 1), "m8sp"), w=(("T", 3),))
            P.add("act", lambda e, cb=cb: e.activation(out=b_[:, 0:ntok], in_=r_[:, 0:ntok], func=AF.Exp, scale=m16sp[:, cb:cb + 1]), r=(("T", 1), "m16sp"), w=(("T", 4),))
            P.add("dve", lambda e: e.tensor_scalar(out=b_[:, 0:ntok], in0=b_[:, 0:ntok], scalar1=-1.0, scalar2=1.0, op0=ALU.mult, op1=ALU.add), r=(("T", 4),), w=(("T", 4),))
            P.add("act", lambda e: e.activation(out=b_[:, 0:ntok], in_=b_[:, 0:ntok], func=AF.Ln), r=(("T", 4),), w=(("T", 4),))
            P.add("act", lambda e: e.activation(out=b_[:, 0:ntok], in_=b_[:, 0:ntok], func=AF.Exp, scale=0.5), r=(("T", 4),), w=(("T", 4),))
            P.add("dve", lambda e: e.tensor_tensor(out=i_[:, 0:ntok], in0=i_[:, 0:ntok], in1=u_[:, 0:ntok], op=ALU.mult), r=(("T", 2), ("T", 0)), w=(("T", 2),))
            P.add("dve", lambda e: e.tensor_tensor(out=b_[:, 0:ntok], in0=b_[:, 0:ntok], in1=i_[:, 0:ntok], op=ALU.mult), r=(("T", 2), ("T", 4)), w=(("T", 4),))
            kh = ("hst", g)
            P.add("dve", lambda e, cb=cb: e.tensor_tensor_scan(out=r_[:, 0:ntok], data0=a_[:, 0:ntok], data1=b_[:, 0:ntok], initial=hst[g][:, cb:cb + 1], op0=ALU.mult, op1=ALU.add),
                  r=(("T", 3), ("T", 4), kh), w=(("T", 1),))
            P.add("dve", lambda e, cb=cb: e.tensor_copy(out=hst[g][:, cb:cb + 1], in_=r_[:, ntok - 1:ntok]), r=(("T", 1),), w=(kh,))
            P.add("act", lambda e, bg=bg: e.activation(out=x_[:, 0:ntok], in_=ps[bg][:, 0:ntok], func=AF.Copy), r=(("ps", bg),), w=(("T", 5),))
            P.add("dve", lambda e: e.tensor_tensor(out=t_[:, 0:ntok], in0=x_[:, 0:ntok], in1=x_[:, 0:ntok], op=ALU.mult), r=(("T", 5),), w=(("T", 6),))
            P.add("dve", lambda e: e.tensor_scalar(out=t_[:, 0:ntok], in0=t_[:, 0:ntok], scalar1=0.044715, scalar2=1.0, op0=ALU.mult, op1=ALU.add), r=(("T", 6),), w=(("T", 6),))
            P.add("dve", lambda e: e.tensor_tensor(out=t_[:, 0:ntok], in0=t_[:, 0:ntok], in1=x_[:, 0:ntok], op=ALU.mult), r=(("T", 6), ("T", 5)), w=(("T", 6),))
            P.add("act", lambda e: e.activation(out=t_[:, 0:ntok], in_=t_[:, 0:ntok], func=AF.Sigmoid, scale=1.5957691216057308), r=(("T", 6),), w=(("T", 6),))
            P.add("dve", lambda e: e.tensor_tensor(out=t_[:, 0:ntok], in0=t_[:, 0:ntok], in1=x_[:, 0:ntok], op=ALU.mult), r=(("T", 6), ("T", 5)), w=(("T", 6),))
            P.add("dve", lambda e, cb=cb: e.tensor_tensor(out=odT[:, cb, 0:ntok], in0=t_[:, 0:ntok], in1=r_[:, 0:ntok], op=ALU.mult), r=(("T", 6), ("T", 1)), w=(("odT", cb),))
        if G["last"]:
            dma("pool", O[g + "d_conv"][:, :, :], xb[:, :, 0:3], r=tuple(("dxbuf", g, c) for c in range(4)), w=(("out", "d_conv"),))
            dma("pool", O[g + "d_h"][:, :], hst[g][:, :], r=(("hst", g),), w=(("out", "d_h"),))

    def layer1(G):
        g, bs, nblk, ntok = G["g"], G["bs"], G["nblk"], G["ntok"]
        norm_T(G, 2)
        ws = load_w("cdin0")
        proj_units(G, ws, range(8), lambda u: QT["C"][0:64, u, 0:ntok], lambda u: (("QT", u),), scale=0.125)
        ws = load_w("cdin1")
        proj_units(G, ws, range(8), lambda u: KTs[0:64, u, 0:ntok], lambda u: (("KTs", u),))
        k_tok_out(G, ws, "c_k")
        ws = load_w("cdin2")
        k_tok_out(G, ws, "c_v", also=lambda b, bnk, s: v_stage(G, b, s))
        write_kv_scratch(G, "C")
        ws = load_w("cdcf")
        proj_tok(G, ws, 8, lambda b, bnk: logf_block(G, b, bnk))
        dma("sp", QT["C"][64:65, :, 0:ntok], FTb[0:8, 0:ntok], r=("FTb",), w=("QT_aug",))
        attention(G, "C")
        chk(8)
        wsx = load_w("cdin3")
        wsg = load_w("cdin4")
        dbranch(G, wsx, wsg)
        panels = []
        for nh in range(2):
            pl = [("cdoutc%d" % nh, 64, 8, lambda kc, b: (OT[0:64, kc, b * bs:(b + 1) * bs], (("GT", kc),))),
                  ("cdoutd%d" % nh, 128, 4, lambda kc, b: (odT[:, kc, b * bs:(b + 1) * bs], (("odT", kc),)))]
            panels.append(pl)
        out_proj(G, panels)
        ffn(G, 1)

    def final_norm(G):
        g, bs, nblk = G["g"], G["bs"], G["nblk"]
        rstd_all(G)
        for b in range(nblk):
            for nh in range(2):
                ti = (2 * b + nh) % 8
                P.add("dve", lambda e, b=b, nh=nh, ti=ti: e.scalar_tensor_tensor(out=TB[ti][0:bs, :], in0=xres[0:bs, b, nh * 512:(nh + 1) * 512], scalar=rstd4[0:bs, b:b + 1], in1=gfin[0:bs, nh * 512:(nh + 1) * 512], op0=ALU.mult, op1=ALU.mult),
                      r=(("xres", b), "rstd4", "gfin"), w=(("T", ti),))
                dma("pool", out_rows(G, "y", b)[:, nh * 512:(nh + 1) * 512], TB[ti][0:bs, :], r=(("T", ti),), w=(("out", "y", G["g"], G["orow"], b, nh),))

    def run_tile(G):
        bs, nblk = G["bs"], G["nblk"]
        src = G["x"]
        for b in range(nblk):
            dma("sp", xres[0:bs, b, :], src[b * bs:(b + 1) * bs, :], r=(), w=(("xres", b),))
        layer0(G)
        chk(7)
        layer1(G)
        chk(9)
        final_norm(G)

    def ingest(m, csrc_k, csrc_v, ntok_c, pos0, roll=None):
        for t0 in range(0, ntok_c, TT):
            q0 = pos0 + t0
            Gc = {"g": "s", "bs": 128, "nblk": 4, "ntok": TT, "q0": q0}
            for b in range(4):
                tk = b % 2
                tv = 2 + b % 2
                r0 = t0 + b * 128
                dma("sp", TB[tk][:, :], csrc_k[r0:r0 + 128, :], r=(), w=(("T", tk),))
                dma("sp", TB[tv][:, :], csrc_v[r0:r0 + 128, :], r=(), w=(("T", tv),))
                import os
                rmode = os.environ.get("ROLLMODE", "1")
                if roll is not None and rmode != "0":
                    p0 = DEC if r0 == 0 else 0
                    for (Oo, tt) in ((roll[0], tk), (roll[1], tv)):
                        if rmode == "2":
                            if r0 == 0:
                                dma("pool", O["sa_k"][0:64, :], TB[tt][64:128, :], r=(("T", tt),), w=(("out", "roll"),))
                        elif rmode == "3":
                            dma("sp", Oo[r0 + p0 - DEC:r0 + 128 - DEC, :], TB[tt][p0:128, :], r=(("T", tt),), w=(("out", "roll"),))
                        else:
                            dma("pool", Oo[r0 + p0 - DEC:r0 + 128 - DEC, :], TB[tt][p0:128, :], r=(("T", tt),), w=(("out", "roll"),))
                P.add("dve", lambda e, tk=tk: e.tensor_copy(out=xn[:, 0:512], in_=TB[tk][:, :]), r=(("T", tk),), w=("xn",))
                for u in range(8):
                    P.add("pe", lambda e, u=u: e.transpose(out=ptb[0:64, u, :], in_=xn[:, u * 64:(u + 1) * 64], identity=ident_b[:, :]),
                          r=("xn", "ident_b"), w=(("ptb", u),))
                evac(KTs[0:64, :, b * 128:(b + 1) * 128], ptb[0:64, :, :], r=tuple(("ptb", u) for u in range(8)), w=tuple(("KTs", u) for u in range(8)))
                P.add("dve", lambda e, b=b, tv=tv: e.tensor_copy(out=Vs[:, b, :, 0:64], in_=TB[tv][:, :].rearrange("p (u d) -> p u d", d=64)), r=(("T", tv), "Vs1"), w=(("Vs", b),))
            write_kv_scratch(Gc, m)

    epsc = sb("epsc", [128, 1])
    def main_seq(chk):
        chk(0)
        prologue()
        chk(1)

        Gs = {"g": "s", "bs": DEC, "nblk": 1, "ntok": DEC, "q0": PAST, "orow": 0, "last": True, "x": x_s, "kmin": PAST - 512}
        ingest("A", cak, cav, PAST, 0)
        ingest("B", cbk, cbv, 512, PAST - 512, roll=(O["sb_k"], O["sb_v"]))
        ingest("C", cck, ccv, PAST, 0)
        chk(2)
        nb_c = PAST // 128
        lcs = sb("lcs", [128, nb_c, 8])
        dma("sp", lcs[:, :, :], cclf.rearrange("(b p) h -> p b h", p=128), r=(), w=("lcs",))
        for b in range(nb_c):
            P.add("dve", lambda e, b=b: e.tensor_scalar(out=lft[:, :], in0=lcs[:, b, :], scalar1=-1.0, scalar2=None, op0=ALU.mult), r=("lcs",), w=("lft",))
            cum_block(Gs, b, 128)
        chk(2.3)
        dma("sp", dxbuf["s"][:, :, 0:3], sdc[:, :, :], r=(), w=tuple(("dxbuf", "s", c) for c in range(4)))
        dma("sp", hst["s"][:, :], sdh[:, :], r=(), w=(("hst", "s"),))
        chk(2.6)
        Gs["orow_b"] = 448
        chk(3)
        import os
        if os.environ.get("SKIPS", "0") != "1":
            run_tile(Gs)
            chk(10)

        for cb in range(4):
            P.add("dve", lambda e, cb=cb: e.memset(dxbuf["p"][:, cb, 0:3], 0.0), w=(("dxbuf", "p", cb),))
        P.add("dve", lambda e: e.memset(hst["p"][:, :], 0.0), w=(("hst", "p"),))
        for t in range(NT):
            Gp = {"g": "p", "bs": 128, "nblk": 4, "ntok": TT, "q0": t * TT, "orow": t * TT, "last": t == NT - 1,
                  "x": x_p[t * TT:(t + 1) * TT, :], "kmin": 0, "orow_b": 0}
            run_tile(Gp)


    P.add("pool", lambda e: e.memset(epsc[:, :], EPS), w=("epsc",))
    P.add("pool", lambda e: e.memset(xn[:, :], 0.0), w=("xn",))
    P.add("pool", lambda e: e.memset(lfb[:, :], 0.0), w=("lfb",))
    try:
        main_seq(chk)
    except _Stop:
        pass

    P.emit()
    return nc, consts


_NAMES_B = ("b_k", "b_v")


_CACHE = {}


def _get_prog(SEQ, PAST):
    key = (SEQ, PAST)
    if key not in _CACHE:
        _CACHE[key] = build(SEQ, PAST)
    return _CACHE[key]


def kernel(**inp):
    x_prompt = np.asarray(inp["x_prompt"]); x_sample = np.asarray(inp["x_sample"])
    NB, SEQ, _ = x_prompt.shape
    NS = x_sample.shape[0]
    PAST = inp["cache_a_k"].shape[1]
    nc, consts = _get_prog(SEQ, PAST)
    f = lambda a: np.ascontiguousarray(np.asarray(a, dtype=np.float32))
    shared = {}
    for k in ("final_g", "ab_w_in", "ab_w_out", "a_lambda", "b_rel_bias", "cd_w_in",
              "cd_w_out", "c_f_bias", "d_w_a", "d_w_x", "ffn_w1", "ffn_w3", "ffn_w2"):
        shared[k] = f(inp[k])
    pc = lambda v: f(np.asarray(v, np.float32).reshape(4, 128).T)
    shared["d_b_a"] = pc(inp["d_b_a"]); shared["d_b_x"] = pc(inp["d_b_x"])
    shared["d_conv_b"] = pc(inp["d_conv_b"]); shared["d_lambda"] = pc(inp["d_lambda"])
    shared["d_conv_w"] = f(np.asarray(inp["d_conv_w"], np.float32).reshape(4, 4, 128).transpose(2, 1, 0))
    shared["a_subln_g"] = f(np.asarray(inp["a_subln_g"], np.float32).reshape(2, 64).T)
    gs = np.stack([np.asarray(inp["norm_mix_g"])[0], np.asarray(inp["norm_ffn_g"])[0],
                   np.asarray(inp["norm_mix_g"])[1], np.asarray(inp["norm_ffn_g"])[1]], 0).astype(np.float32)
    shared["gT_in"] = f(gs)
    for k, v in consts.items():
        shared["c_" + k] = v
    in_maps = []
    for c in range(8):
        m = dict(shared)
        m["x_p"] = f(x_prompt[c % NB]); m["x_s"] = f(x_sample[c % NS])
        s = c % NS
        m["cache_a_k"] = f(inp["cache_a_k"][s]).reshape(PAST, 512); m["cache_a_v"] = f(inp["cache_a_v"][s]).reshape(PAST, 512)
        m["cache_b_k"] = f(inp["cache_b_k"][s]).reshape(512, 512); m["cache_b_v"] = f(inp["cache_b_v"][s]).reshape(512, 512)
        m["cache_c_k"] = f(inp["cache_c_k"][s]).reshape(PAST, 512); m["cache_c_v"] = f(inp["cache_c_v"][s]).reshape(PAST, 512)
        m["cache_c_logf"] = f(inp["cache_c_logf"][s])
        m["state_d_conv"] = f(np.asarray(inp["state_d_conv"][s], np.float32).reshape(3, 4, 128).transpose(2, 1, 0))
        m["state_d_h"] = f(np.asarray(inp["state_d_h"][s], np.float32).reshape(4, 128).T)
        in_maps.append(m)
    res = run_bass_kernel_spmd(nc, in_maps, core_ids=list(range(8)))
    R = res.results

    def stack(name, cores, shape):
        if name.endswith("d_conv"):
            return np.stack([np.asarray(R[c][name], dtype=np.float32).reshape(128, 4, 3).transpose(2, 1, 0).reshape(3, 512) for c in cores], axis=0)
        if name.endswith("d_h"):
            return np.stack([np.asarray(R[c][name], dtype=np.float32).reshape(128, 4).T.reshape(512) for c in cores], axis=0)
        return np.stack([np.asarray(R[c][name], dtype=np.float32).reshape(shape) for c in cores], axis=0)
    pc = list(range(NB)); sc = list(range(NS))
    outs = (
        stack("p_y", pc, (SEQ, D)), stack("s_y", sc, (DEC, D)),
        stack("p_a_k", pc, (SEQ, 4, 128)), stack("p_a_v", pc, (SEQ, 4, 128)),
        stack("p_b_k", pc, (512, 8, 64)), stack("p_b_v", pc, (512, 8, 64)),
        stack("p_c_k", pc, (SEQ, 8, 64)), stack("p_c_v", pc, (SEQ, 8, 64)),
        stack("p_c_logf", pc, (SEQ, 8)), stack("p_d_conv", pc, (3, 512)), stack("p_d_h", pc, (512,)),
        stack("s_a_k", sc, (DEC, 4, 128)), stack("s_a_v", sc, (DEC, 4, 128)),
        stack("s_b_k", sc, (512, 8, 64)), stack("s_b_v", sc, (512, 8, 64)),
        stack("s_c_k", sc, (DEC, 8, 64)), stack("s_c_v", sc, (DEC, 8, 64)),
        stack("s_c_logf", sc, (DEC, 8)), stack("s_d_conv", sc, (3, 512)), stack("s_d_h", sc, (512,)),
    )
    return outs
```

```python
import math
import numpy as np
import ml_dtypes
import concourse.bass as bass
import concourse.mybir as mybir
from concourse.bass_utils import run_bass_kernel_spmd

F32 = mybir.dt.float32
BF16 = mybir.dt.bfloat16
AF = mybir.ActivationFunctionType
ALU = mybir.AluOpType
AX = mybir.AxisListType

D = 1024
HID = 2816
TT = 512
DEC = 64
CH = 1024
NEG = -1.0e30
EPS = 1e-6
NDS = 24


class Op:
    __slots__ = ("eng", "fn", "dma", "deps", "sig", "val", "sem", "prev")


class Prog:
    ENGS = ("pe", "act", "dve", "pool", "sp")

    def __init__(self, nc):
        self.nc = nc
        self.ops = {e: [] for e in self.ENGS}
        self.lastw = {}
        self.readers = {}

    def add(self, eng, fn, r=(), w=(), dma=False):
        op = Op()
        op.eng = eng; op.fn = fn; op.dma = dma; op.sig = False; op.val = 0; op.sem = None; op.prev = 0
        deps = {}
        for k in r:
            lw = self.lastw.get(k)
            if lw is not None:
                deps[lw] = True
        for k in w:
            lw = self.lastw.get(k)
            if lw is not None:
                deps[lw] = True
            for rd in self.readers.get(k, ()):
                deps.setdefault(rd, False)
        for k in r:
            lst = self.readers.setdefault(k, [])
            if not dma:
                lst[:] = [o for o in lst if o.dma or o.eng != eng]
            lst.append(op)
        for k in w:
            self.lastw[k] = op
            self.readers[k] = []
        fd = []
        for d, strong in deps.items():
            if d is op:
                continue
            if d.eng == eng and not d.dma and not dma:
                if eng == "pe" or not strong:
                    continue
            d.sig = True
            fd.append(d)
        op.deps = fd
        self.ops[eng].append(op)
        return op

    def emit(self):
        nc = self.nc
        esem = {e: nc.alloc_semaphore("sem_" + e) for e in ("pe", "act", "dve", "pool")}
        dsem = {q: [nc.alloc_semaphore("dq_%s_%d" % (q, i)) for i in range(NDS)] for q in ("sp", "pool", "act")}
        final = {}
        for e in self.ENGS:
            c = 0
            di = 0
            dcount = [0] * NDS
            for op in self.ops[e]:
                if op.dma:
                    s = di % NDS
                    di += 1
                    op.prev = 16 * dcount[s]
                    dcount[s] += 1
                    op.sem = dsem[e][s]
                    op.val = 16 * dcount[s]
                    final[op.sem] = op.val
                elif op.sig:
                    c += 1
                    op.sem = esem[e]
                    op.val = c

        def run(e, eng):
            waited = {}
            for op in self.ops[e]:
                need = {}
                for d in op.deps:
                    if need.get(d.sem, 0) < d.val:
                        need[d.sem] = d.val
                if op.dma and op.prev > 0 and need.get(op.sem, 0) < op.prev:
                    need[op.sem] = op.prev
                for s, v in need.items():
                    if waited.get(s, 0) < v:
                        eng.wait_ge(s, v)
                        waited[s] = v
                ins = op.fn(eng)
                if op.dma:
                    ins.then_inc(op.sem, 16)
                elif op.sig:
                    ins.then_inc(op.sem, 1)
            if e == "sp":
                for s, v in final.items():
                    if waited.get(s, 0) < v:
                        eng.wait_ge(s, v)

        with nc.Block() as block:
            @block.tensor
            def _(eng):
                run("pe", eng)

            @block.scalar
            def _(eng):
                run("act", eng)

            @block.vector
            def _(eng):
                run("dve", eng)

            @block.gpsimd
            def _(eng):
                run("pool", eng)

            @block.sync
            def _(eng):
                run("sp", eng)


def _bf16r(a):
    return np.asarray(a, np.float32).astype(ml_dtypes.bfloat16).astype(np.float32)


def make_consts(nda):
    c = {}
    c["ident_f"] = np.eye(128, dtype=np.float32)
    c["ident_b"] = np.eye(128, dtype=np.float32).astype(ml_dtypes.bfloat16)
    t = np.arange(128)
    c["tri_f"] = (t[:, None] <= t[None, :]).astype(np.float32)
    sel = np.zeros((128, 128), np.float32); sel[127, :] = 1.0
    c["sel_f"] = sel
    c["anti_f"] = np.ascontiguousarray(np.eye(128, dtype=np.float32)[::-1])
    i = np.arange(TT)
    ri = _bf16r(i.astype(np.float32))
    slopes = np.array([2.0 ** (-8.0 * (h + 1) / 4) for h in range(4)], np.float32)
    cq = np.zeros((1, 8, TT), np.float32)
    for h in range(4):
        for cc in range(2):
            cq[0, 2 * h + cc] = -slopes[h] * ri
    assert np.array_equal(_bf16r(cq), cq)
    c["cqA"] = cq.astype(ml_dtypes.bfloat16)
    j = np.arange(128)
    t0 = np.zeros((128, 4, TT), np.float32)
    cm = np.zeros((128, 4, TT), np.float32)
    for js in range(4):
        jp = 128 * js + j
        vis = (jp[:, None] // 64) <= (i[None, :] // 64)
        t0[:, js, :] = np.where(vis, ri[None, :] - np.abs(i[None, :] - jp[:, None]), -4.0e30)
        cm[:, js, :] = np.where(jp[:, None] <= i[None, :], 0.0, NEG)
    c["t0m"] = t0
    c["cmask"] = cm.astype(ml_dtypes.bfloat16)
    bk = np.zeros((128, 4, nda), np.float32)
    for h in range(4):
        bk[:, h, :] = slopes[h] * (j[:, None] - 128.0 * np.arange(nda)[None, :])
    c["bkA"] = bk
    bm = np.zeros((128, 8, TT), np.float32)
    dvals = [512, 384, 256, 128, 0, -128, -256, -384]
    for dk, dv in enumerate(dvals):
        kc_ = np.floor_divide(-dv + j, 64)
        qc_ = i // 64
        vis = (kc_[:, None] >= qc_[None, :] - 8) & (kc_[:, None] <= qc_[None, :])
        bm[:, dk, :] = np.where(vis, 0.0, NEG)
    c["bmask"] = bm.astype(ml_dtypes.bfloat16)
    return c, slopes, dvals


class _Stop(Exception):
    pass


def build(SEQ, PAST, stop=999):
    def chk(stage):
        if stage > stop:
            raise _Stop()
    NT = SEQ // TT
    NKT_P = SEQ // 128
    NKT_S = PAST // 128 + 1
    NDA = max(SEQ, PAST) // 128 + 2
    consts, slopes, dvals = make_consts(NDA)
    nc = bass.Bass("TRN2", target_bir_lowering=False)
    P = Prog(nc)

    def din(name, shape, dt=F32):
        return nc.dram_tensor(name, list(shape), dt, kind="ExternalInput").ap()

    def dout(name, shape):
        return nc.dram_tensor(name, list(shape), F32, kind="ExternalOutput").ap()

    def dscr(name, shape, dt=BF16):
        return nc.dram_tensor(name, list(shape), dt).ap()

    def sb(name, shape, dt=F32):
        return nc.alloc_sbuf_tensor(name, list(shape), dt)

    x_p = din("x_p", [SEQ, D]); x_s = din("x_s", [DEC, D])
    cak = din("cache_a_k", [PAST, 512]); cav = din("cache_a_v", [PAST, 512])
    cbk = din("cache_b_k", [512, 512]); cbv = din("cache_b_v", [512, 512])
    cck = din("cache_c_k", [PAST, 512]); ccv = din("cache_c_v", [PAST, 512])
    cclf = din("cache_c_logf", [PAST, 8])
    sdc = din("state_d_conv", [128, 4, 3]); sdh = din("state_d_h", [128, 4])
    gT_in = din("gT_in", [4, D]); fing = din("final_g", [D])
    ab_w_in = din("ab_w_in", [D, 3072]); ab_w_out = din("ab_w_out", [D, D])
    a_lambda = din("a_lambda", [4, 64]); a_subln = din("a_subln_g", [64, 2]); b_rel = din("b_rel_bias", [8, 257])
    cd_w_in = din("cd_w_in", [D, 2568]); cd_w_out = din("cd_w_out", [D, D]); c_f_bias = din("c_f_bias", [8])
    d_conv_w = din("d_conv_w", [128, 4, 4]); d_conv_b = din("d_conv_b", [128, 4])
    d_w_a = din("d_w_a", [8, 64, 64]); d_b_a = din("d_b_a", [128, 4]); d_w_x = din("d_w_x", [8, 64, 64])
    d_b_x = din("d_b_x", [128, 4]); d_lam = din("d_lambda", [128, 4])
    w1 = din("ffn_w1", [2, D, HID]); w3 = din("ffn_w3", [2, D, HID]); w2 = din("ffn_w2", [2, HID, D])
    cin = {}
    for k, v in consts.items():
        cin[k] = din("c_" + k, v.shape, BF16 if v.dtype == ml_dtypes.bfloat16 else F32)

    O = {}
    for g, T in (("p", SEQ), ("s", DEC)):
        O[g + "y"] = dout(g + "_y", [T, D])
        O[g + "a_k"] = dout(g + "_a_k", [T, 512]); O[g + "a_v"] = dout(g + "_a_v", [T, 512])
        O[g + "b_k"] = dout(g + "_b_k", [512, 512]); O[g + "b_v"] = dout(g + "_b_v", [512, 512])
        O[g + "c_k"] = dout(g + "_c_k", [T, 512]); O[g + "c_v"] = dout(g + "_c_v", [T, 512])
        O[g + "c_logf"] = dout(g + "_c_logf", [T, 8])
        O[g + "d_conv"] = dout(g + "_d_conv", [128, 4, 3]); O[g + "d_h"] = dout(g + "_d_h", [128, 4])

    WP = {}

    def wpanel(key, src_rows_fn, npart, nkc, ncols):
        t = dscr("wp_%s" % key, [npart, nkc, ncols])
        WP[key] = (t, npart, nkc, ncols)
        for kc in range(nkc):
            src = src_rows_fn(kc)
            P.add("pool", lambda e, o=t[:, kc, :], i=src: e.dma_start(out=o, in_=i), r=(), w=(("wp", key, kc),), dma=True)

    for pi in range(6):
        wpanel("abin%d" % pi, lambda kc, pi=pi: ab_w_in[kc * 128:(kc + 1) * 128, pi * 512:(pi + 1) * 512], 128, 8, 512)
    for nh in range(2):
        for kp in range(2):
            wpanel("about%d_%d" % (nh, kp),
                   lambda kc, nh=nh, kp=kp: ab_w_out[(kp * 8 + kc) * 64:(kp * 8 + kc + 1) * 64, nh * 512:(nh + 1) * 512], 64, 8, 512)
    for pi in range(3):
        wpanel("cdin%d" % pi, lambda kc, pi=pi: cd_w_in[kc * 128:(kc + 1) * 128, pi * 512:(pi + 1) * 512], 128, 8, 512)
    wpanel("cdcf", lambda kc: cd_w_in[kc * 128:(kc + 1) * 128, 1536:1544], 128, 8, 8)
    for pi in range(2):
        wpanel("cdin%d" % (3 + pi), lambda kc, pi=pi: cd_w_in[kc * 128:(kc + 1) * 128, 1544 + pi * 512:1544 + (pi + 1) * 512], 128, 8, 512)
    for nh in range(2):
        wpanel("cdoutc%d" % nh, lambda kc, nh=nh: cd_w_out[kc * 64:(kc + 1) * 64, nh * 512:(nh + 1) * 512], 64, 8, 512)
        wpanel("cdoutd%d" % nh, lambda kc, nh=nh: cd_w_out[512 + kc * 128:512 + (kc + 1) * 128, nh * 512:(nh + 1) * 512], 128, 4, 512)
    HP = [(0, 512), (512, 512), (1024, 512), (1536, 512), (2048, 512), (2560, 256)]
    KP2 = [(0, 8), (8, 8), (16, 6)]
    for L in range(2):
        for pi, (c0, cn) in enumerate(HP):
            wpanel("w1_%d_%d" % (L, pi), lambda kc, L=L, c0=c0, cn=cn: w1[L, kc * 128:(kc + 1) * 128, c0:c0 + cn], 128, 8, cn)
            wpanel("w3_%d_%d" % (L, pi), lambda kc, L=L, c0=c0, cn=cn: w3[L, kc * 128:(kc + 1) * 128, c0:c0 + cn], 128, 8, cn)
        for nh in range(2):
            for kp, (k0, kn) in enumerate(KP2):
                wpanel("w2_%d_%d_%d" % (L, nh, kp),
                       lambda kc, L=L, nh=nh, k0=k0: w2[L, (k0 + kc) * 128:(k0 + kc + 1) * 128, nh * 512:(nh + 1) * 512], 128, kn, 512)

    SCR = {}
    for g, Tt, nkt in (("p", SEQ, NKT_P), ("s", PAST + DEC, NKT_S)):
        for m in "ABC":
            SCR[(g, m, "k")] = dscr("kt_%s%s" % (g, m), [8, 64, Tt])
            if m == "A":
                SCR[(g, m, "v")] = dscr("v_%s%s" % (g, m), [4, 128, nkt, 130])
            else:
                SCR[(g, m, "v")] = dscr("v_%s%s" % (g, m), [8, 128, nkt, 65])
    text = dscr("text", [8, 1536], F32)
    bbias = dscr("bbias", [8, 128, 8, TT])

    ident_f = sb("ident_f", [128, 128]); ident_b = sb("ident_b", [128, 128], BF16)
    tri_f = sb("tri_f", [128, 128]); sel_f = sb("sel_f", [128, 128]); anti_f = sb("anti_f", [128, 128])
    ones_f = sb("ones_f", [128, 64]); ones_b = sb("ones_b", [128, 64], BF16)
    t0m = sb("t0m", [128, 4, TT]); cmask = sb("cmask", [128, 4, TT], BF16)
    bkA = sb("bkA", [128, 4, NDA])
    gbc = sb("gbc", [128, 4, D], BF16); gfin = sb("gfin", [128, D])
    cfb = sb("cfb", [128, 8])
    brt = sb("brt", [8, 257]); txs = sb("txs", [8, 1536])
    neglam = sb("neglam", [128, 1]); lamt = sb("lamt", [128, 4, 64]); lamp = sb("lamp", [128, 2, 64]); lams = sb("lams", [128, 2])
    subg = sb("subg", [64, 2])
    spt = sb("spt", [128, 4]); m8sp = sb("m8sp", [128, 4]); m16sp = sb("m16sp", [128, 4])
    convw = sb("convw", [128, 4, 4]); convb = sb("convb", [128, 4]); bat = sb("bat", [128, 4]); bxt = sb("bxt", [128, 4])
    bdf = sb("bdf", [128, 4, 128]); bda = sb("bda", [128, 4, 128], BF16); bdx = sb("bdx", [128, 4, 128], BF16)
    xres = sb("xres", [128, 4, D])
    xn = sb("xn", [128, D], BF16); sqj = sb("sqj", [128, D], BF16)
    ss4 = sb("ss4", [128, 4]); ss4n = sb("ss4n", [128, 4]); ln4 = sb("ln4", [128, 4]); rstd4 = sb("rstd4", [128, 4])
    hT = sb("hT", [128, 8, TT], BF16)
    QT1 = sb("QT1", [128, 8, TT], BF16)
    QT = {"A": QT1, "B": QT1, "C": QT1}
    KTs = sb("KTs", [64, 8, TT], BF16)
    Vs = sb("Vs", [128, 4, 8, 65], BF16)
    kvo = [sb("kvo%d" % i, [128, 512]) for i in range(2)]
    odT = sb("odT", [128, 4, TT], BF16)
    GT = sb("GT", [128, 22, TT], BF16)
    OT = GT
    NW = 3
    wring = [sb("wring%d" % i, [128, 8, 512], BF16) for i in range(NW)]
    NKR = 3
    kring = [sb("kring%d" % i, [128, CH], BF16) for i in range(NKR)]
    vring = [sb("vring%d" % i, [128, CH // 128, 130], BF16) for i in range(NKR)]
    NPT = 4
    ptr = [sb("ptr%d" % i, [128, TT], BF16) for i in range(NPT)]
    dtmp = [sb("dtmp%d" % i, [128, TT]) for i in range(2)]
    bbr = [sb("bbr%d" % i, [128, 8, TT], BF16) for i in range(1)]
    rl = sb("rl", [65, TT])
    TB = [sb("tb%d" % i, [128, TT]) for i in range(8)]
    sq2 = sb("sq2", [64, 2, TT], BF16)
    negF = {"p": sb("negF_p", [128, NKT_P, 8]), "s": sb("negF_s", [128, NKT_S, 8])}
    lft = sb("lft", [128, 8]); lfo = sb("lfo", [128, 8]); lfb = sb("lfb", [128, 8], BF16); FTb = sb("FTb", [8, TT], BF16)
    dxbuf = {"p": sb("dxbuf_p", [128, 4, TT + 3]), "s": sb("dxbuf_s", [128, 4, DEC + 3])}
    hst = {"p": sb("hst_p", [128, 4]), "s": sb("hst_s", [128, 4])}
    ub = sb("ub", [128, TT], BF16)

    ps = [nc.alloc_psum_tensor("ps%d" % i, [128, 512], F32) for i in range(7)]
    ptb = nc.alloc_psum_tensor("ptb", [128, 8, 128], BF16)

    rot = {"sps2": 0, "sps4": 0, "sps": 0, "w": 0, "k": 0, "pt": 0, "dt": 0, "bb": 0, "kvo": 0, "ps": 0}

    def nxt(name, n):
        v = rot[name]
        rot[name] = (v + 1) % n
        return v

    def dma(q, out, in_, r, w, **kw):
        return P.add(q, lambda e: e.dma_start(out=out, in_=in_, **kw), r=r, w=w, dma=True)

    def load_w(key):
        t, npart, nkc, ncols = WP[key]
        s = nxt("w", NW)
        dma("sp", wring[s][0:npart, 0:nkc, 0:ncols], t[:, :, :], r=tuple(("wp", key, kc) for kc in range(nkc)), w=(("wr", s),))
        return s

    def prologue():
        for k, t in (("ident_f", ident_f), ("ident_b", ident_b), ("tri_f", tri_f), ("sel_f", sel_f), ("anti_f", anti_f), ("t0m", t0m),
                     ("cmask", cmask), ("bkA", bkA)):
            sl = tuple(slice(None) for _ in consts[k].shape)
            dma("sp", t[sl], cin[k][sl], r=(), w=(k,))
        P.add("pool", lambda e: e.memset(ones_f[:, :], 1.0), w=("ones_f",))
        P.add("pool", lambda e: e.memset(ones_b[:, :], 1.0), w=("ones_b",))
        P.add("pool", lambda e: e.memset(QT1[64:128, :, :], 0.0), w=("QT_z", "QT_aug"))
        for i in range(NKR):
            P.add("pool", lambda e, i=i: e.memset(kring[i][64:128, :], 0.0), w=(("kr1", i),))
            P.add("pool", lambda e, i=i: e.memset(kring[i][64:65, :], 1.0), w=(("kr1", i),))
            P.add("pool", lambda e, i=i: e.memset(vring[i][:, :, :], 0.0), w=(("vr0", i), ("vr", i)))
        P.add("pool", lambda e: e.memset(Vs[:, :, :, 64:65], 1.0), w=("Vs1",))
        for n in range(4):
            for hh in range(2):
                dma("sp", TB[hh][:, :], gT_in[n, hh * 512:(hh + 1) * 512].partition_broadcast(128), r=(), w=(("T", hh),))
                P.add("dve", lambda e, n=n, hh=hh: e.tensor_copy(out=gbc[:, n, hh * 512:(hh + 1) * 512], in_=TB[hh][:, :]), r=(("T", hh),), w=("gbc",))
        dma("sp", gfin[:, :], fing.partition_broadcast(128), r=(), w=("gfin",))
        dma("sp", cfb[:, :], c_f_bias.partition_broadcast(128), r=(), w=("cfb",))
        dma("sp", lamt[:, :, :], a_lambda.partition_broadcast(128), r=(), w=("lamt",))
        P.add("dve", lambda e: e.tensor_tensor(out=lamp[:, 0, :], in0=lamt[:, 0, :], in1=lamt[:, 1, :], op=ALU.mult), r=("lamt",), w=("lamp0",))
        P.add("dve", lambda e: e.tensor_tensor(out=lamp[:, 1, :], in0=lamt[:, 2, :], in1=lamt[:, 3, :], op=ALU.mult), r=("lamt",), w=("lamp1",))
        P.add("dve", lambda e: e.reduce_sum(out=lams[:, :], in_=lamp[:, :, :], axis=AX.X), r=("lamp0", "lamp1"), w=("lams",))
        P.add("act", lambda e: e.activation(out=lams[:, :], in_=lams[:, :], func=AF.Exp), r=("lams",), w=("lams",))
        lam_init = 0.8 - 0.6 * math.exp(0.0)
        P.add("dve", lambda e: e.tensor_tensor(out=neglam[:, :], in0=lams[:, 1:2], in1=lams[:, 0:1], op=ALU.subtract), r=("lams",), w=("neglam",))
        P.add("dve", lambda e: e.tensor_scalar(out=neglam[:, :], in0=neglam[:, :], scalar1=-lam_init, scalar2=None, op0=ALU.add), r=("neglam",), w=("neglam",))
        dma("sp", subg[:, :], a_subln[:, :], r=(), w=("subg",))
        P.add("dve", lambda e: e.tensor_scalar(out=subg[:, :], in0=subg[:, :], scalar1=1.0 - lam_init, scalar2=None, op0=ALU.mult), r=("subg",), w=("subg",))
        dma("sp", spt[:, :], d_lam[:, :], r=(), w=("spt",))
        P.add("act", lambda e: e.activation(out=spt[:, :], in_=spt[:, :], func=AF.Exp, scale=-1.0), r=("spt",), w=("spt",))
        P.add("act", lambda e: e.activation(out=spt[:, :], in_=spt[:, :], func=AF.Ln, bias=1.0), r=("spt",), w=("spt",))
        P.add("dve", lambda e: e.tensor_scalar(out=m8sp[:, :], in0=spt[:, :], scalar1=-8.0, scalar2=None, op0=ALU.mult), r=("spt",), w=("m8sp",))
        P.add("dve", lambda e: e.tensor_scalar(out=m16sp[:, :], in0=spt[:, :], scalar1=-16.0, scalar2=None, op0=ALU.mult), r=("spt",), w=("m16sp",))
        dma("sp", convw[:, :, :], d_conv_w[:, :, :], r=(), w=("convw",))
        for t, src, k in ((convb, d_conv_b, "convb"), (bat, d_b_a, "bat"), (bxt, d_b_x, "bxt")):
            dma("sp", t[:, :], src[:, :], r=(), w=(k,))
        for wsrc, dst, k in ((d_w_a, bda, "bda"), (d_w_x, bdx, "bdx")):
            P.add("dve", lambda e: e.memset(bdf[:, :, :], 0.0), w=("bdf",))
            for nn in range(2):
                src = wsrc.rearrange("(c two) i j -> two i c j", two=2)[nn]
                dma("sp", bdf[nn * 64:(nn + 1) * 64, :, nn * 64:(nn + 1) * 64], src, r=(), w=("bdf",))
            P.add("dve", lambda e, dst=dst: e.tensor_copy(out=dst[:, :, :], in_=bdf[:, :, :]), r=("bdf",), w=(k,))
        dma("sp", brt[:, :], b_rel[:, :], r=(), w=("brt",))
        P.add("dve", lambda e: e.memset(txs[:, :], 0.0), w=("txs",))
        P.add("dve", lambda e: e.tensor_scalar(out=txs[:, 0:383], in0=txs[:, 0:383], scalar1=brt[:, 0:1], scalar2=None, op0=ALU.add), r=("brt", "txs"), w=("txs",))
        P.add("dve", lambda e: e.tensor_copy(out=txs[:, 383:640], in_=brt[:, 0:257]), r=("brt", "txs"), w=("txs",))
        P.add("dve", lambda e: e.tensor_scalar(out=txs[:, 640:1536], in0=txs[:, 640:1536], scalar1=brt[:, 256:257], scalar2=None, op0=ALU.add), r=("brt", "txs"), w=("txs",))
        dma("pool", text[:, :], txs[:, :], r=("txs",), w=("text",))
        for h in range(8):
            for dk, dv in enumerate(dvals):
                s = nxt("dt", 2)
                src = bass.AP(tensor=text.tensor, offset=h * 1536 + dv + 384, ap=[[1, 128], [1, TT]])
                dma("sp", dtmp[s][:, :], src, r=("text",), w=(("dtmp", s),))
                bnk = nxt("ps", 6)
                P.add("pe", lambda e, s=s, bnk=bnk: e.matmul(ps[bnk][:, :], lhsT=anti_f[:, :], rhs=dtmp[s][:, :], start=True, stop=True),
                      r=(("dtmp", s), "anti_f"), w=(("ps", bnk),))
                pslot = nxt("pt", NPT)
                dma_in = cin["bmask"]
                P.add("pool", lambda e, pslot=pslot, dk=dk: e.dma_start(out=ptr[pslot][:, :], in_=dma_in[:, dk, :]),
                      r=(), w=(("ptr", pslot),), dma=True)
                P.add("dve", lambda e, bnk=bnk, pslot=pslot: e.tensor_tensor(out=ptr[pslot][:, :], in0=ps[bnk][:, :], in1=ptr[pslot][:, :], op=ALU.add),
                      r=(("ps", bnk), ("ptr", pslot)), w=(("ptr", pslot),))
                dma("pool", bbias[h, :, dk, :], ptr[pslot][:, :], r=(("ptr", pslot),), w=("bbias",))

    def rstd_all(G):
        bs, nblk = G["bs"], G["nblk"]
        for b in range(nblk):
            P.add("dve", lambda e, b=b: e.tensor_tensor(out=sqj[0:bs, :], in0=xres[0:bs, b, :], in1=xres[0:bs, b, :], op=ALU.mult), r=(("xres", b),), w=("sqj",))
            P.add("dve", lambda e, b=b: e.reduce_sum(out=ss4[0:bs, b:b + 1], in_=sqj[0:bs, :], axis=AX.X), r=("sqj",), w=(("ss4", b),))
        P.add("dve", lambda e: e.tensor_scalar(out=ss4n[0:bs, 0:nblk], in0=ss4[0:bs, 0:nblk], scalar1=1.0 / D, scalar2=EPS, op0=ALU.mult, op1=ALU.add),
              r=tuple(("ss4", b) for b in range(nblk)), w=("ss4n",))
        P.add("act", lambda e: e.activation(out=ln4[0:bs, 0:nblk], in_=ss4n[0:bs, 0:nblk], func=AF.Ln), r=("ss4n",), w=("ln4",))
        P.add("act", lambda e: e.activation(out=rstd4[0:bs, 0:nblk], in_=ln4[0:bs, 0:nblk], func=AF.Exp, scale=-0.5), r=("ln4",), w=("rstd4",))

    def norm_T(G, nidx):
        bs, nblk, ntok = G["bs"], G["nblk"], G["ntok"]
        rstd_all(G)
        for b in range(nblk):
            P.add("dve", lambda e, b=b: e.scalar_tensor_tensor(out=xn[0:bs, :], in0=xres[0:bs, b, :], scalar=rstd4[0:bs, b:b + 1], in1=gbc[0:bs, nidx, :], op0=ALU.mult, op1=ALU.mult),
                  r=(("xres", b), "rstd4", "gbc"), w=("xn",))
            for kc in range(8):
                P.add("pe", lambda e, kc=kc: e.transpose(out=ptb[:, kc, :], in_=xn[:, kc * 128:(kc + 1) * 128], identity=ident_b[:, :]),
                      r=("xn", "ident_b"), w=(("ptb", kc),))
            evac(hT[:, :, b * bs:(b + 1) * bs], ptb[:, :, 0:bs], r=tuple(("ptb", kc) for kc in range(8)), w=(("hT", b, 0), ("hT", b, 1)))

    def hT_keys(G):
        return tuple(("hT", b, j) for b in range(G["nblk"]) for j in range(2))

    evac_tog = [0]

    def evac(out, in_, r, w, scale=None):
        evac_tog[0] ^= 1
        if evac_tog[0]:
            if scale is None:
                P.add("act", lambda e: e.activation(out=out, in_=in_, func=AF.Copy), r=r, w=w)
            else:
                P.add("act", lambda e: e.activation(out=out, in_=in_, func=AF.Copy, scale=float(scale)), r=r, w=w)
        else:
            if scale is None:
                P.add("dve", lambda e: e.tensor_copy(out=out, in_=in_), r=r, w=w)
            else:
                P.add("dve", lambda e: e.tensor_scalar(out=out, in0=in_, scalar1=float(scale), scalar2=None, op0=ALU.mult), r=r, w=w)

    def proj_units(G, ws, units, dst_fn, dst_keys_fn, scale=None):
        ntok = G["ntok"]
        for u in units:
            bnk = nxt("ps", 6)
            for kc in range(8):
                P.add("pe", lambda e, kc=kc, u=u, bnk=bnk: e.matmul(ps[bnk][0:64, 0:ntok], lhsT=wring[ws][:, kc, u * 64:(u + 1) * 64], rhs=hT[:, kc, 0:ntok],
                                                                    start=(kc == 0), stop=(kc == 7)),
                      r=(("wr", ws),) + hT_keys(G), w=(("ps", bnk),))
            evac(dst_fn(u), ps[bnk][0:64, 0:ntok], r=(("ps", bnk),), w=dst_keys_fn(u), scale=scale)

    def proj_tok(G, ws, ncols, cb):
        bs, nblk = G["bs"], G["nblk"]
        for b in range(nblk):
            bnk = nxt("ps", 6)
            for kc in range(8):
                P.add("pe", lambda e, kc=kc, b=b, bnk=bnk: e.matmul(ps[bnk][0:bs, 0:ncols], lhsT=hT[:, kc, b * bs:(b + 1) * bs], rhs=wring[ws][:, kc, 0:ncols],
                                                                    start=(kc == 0), stop=(kc == 7)),
                      r=(("wr", ws), ("hT", b, 0), ("hT", b, 1)), w=(("ps", bnk),))
            cb(b, bnk)

    def out_rows(G, name, b):
        bs = G["bs"]
        r0 = (G["orow_b"] if name in ("b_k", "b_v") else G["orow"]) + b * bs
        return O[G["g"] + name][r0:r0 + bs, :]

    def k_tok_out(G, ws, name, also=None):
        bs = G["bs"]
        if name is None and also is None:
            return

        def cb(b, bnk):
            s = nxt("kvo", 2)
            evac(kvo[s][0:bs, :], ps[bnk][0:bs, 0:512], r=(("ps", bnk),), w=(("kvo", s),))
            if name is not None:
                dma("pool", out_rows(G, name, b), kvo[s][0:bs, :], r=(("kvo", s),), w=(("out", name, G["g"], G["orow"], b),))
            if also is not None:
                also(b, bnk, s)
        proj_tok(G, ws, 512, cb)

    def v_stage(G, b, s):
        bs = G["bs"]
        P.add("dve", lambda e: e.tensor_copy(out=Vs[0:bs, b, :, 0:64], in_=kvo[s][0:bs, :].rearrange("p (u d) -> p u d", d=64)),
              r=(("kvo", s), "Vs1"), w=(("Vs", b),))

    def write_kv_scratch(G, m):
        g, bs, nblk, ntok, q0 = G["g"], G["bs"], G["nblk"], G["ntok"], G["q0"]
        kd = SCR[(g, m, "k")]
        dma("pool", kd[:, :, q0:q0 + ntok].rearrange("u d t -> d u t"), KTs[:, :, 0:ntok], r=tuple(("KTs", u) for u in range(8)), w=((g, m, "k", q0 // TT),))
        vd = SCR[(g, m, "v")]
        kt0 = q0 // 128
        vkeys = tuple(("Vs", b) for b in range(nblk))
        if m == "A":
            for h in range(4):
                src = Vs[0:bs, 0:nblk, 2 * h:2 * h + 2, :].rearrange("p b two x -> p b (two x)")
                dma("pool", vd[h, 0:bs, kt0:kt0 + nblk, :], src, r=vkeys, w=((g, m, "v", q0 // TT, h),))
        else:
            for u in range(8):
                dma("pool", vd[u, 0:bs, kt0:kt0 + nblk, :], Vs[0:bs, 0:nblk, u, :], r=vkeys, w=((g, m, "v", q0 // TT, u),))

    oset = [0]

    def attention(G, m):
        g, ntok, q0 = G["g"], G["ntok"], G["q0"]
        nq = ntok
        kd, vd = SCR[(g, m, "k")], SCR[(g, m, "v")]
        nhalf = 2 if m == "A" else 1
        xw = 65 * nhalf
        kstart = 0 if m != "B" else max(G["kmin"], q0 - 512)
        kend = q0 + ntok
        pend_fin = [None]

        def fin_recip(ob):
            P.add("dve", lambda e: e.reciprocal(out=rl[64:65, 0:nq], in_=ps[ob][64:65, 0:nq]), r=(("ps", ob),), w=("rl",))

        def finalize(u, ob, h):
            P.add("pe", lambda e: e.matmul(ps[6][0:64, 0:nq], lhsT=ones_f[64:65, 0:64], rhs=rl[64:65, 0:nq], start=True, stop=True),
                  r=("rl", "ones_f"), w=(("ps", 6),))
            P.add("act", lambda e: e.activation(out=TB[7][0:64, 0:nq], in_=ps[6][0:64, 0:nq], func=AF.Copy), r=(("ps", 6),), w=(("T", 7),))
            for hf in range(nhalf):
                bnk = ob + hf
                if m == "A":
                    cc = u % 2
                    P.add("dve", lambda e, bnk=bnk, cc=cc, hf=hf: e.tensor_tensor(out=TB[cc * 2 + hf][0:64, 0:nq], in0=ps[bnk][0:64, 0:nq], in1=TB[7][0:64, 0:nq], op=ALU.mult),
                          r=(("ps", bnk), ("T", 7)), w=(("T", cc * 2 + hf),))
                else:
                    chunk = 8 + u if m == "B" else u
                    P.add("dve", lambda e, bnk=bnk, chunk=chunk: e.tensor_tensor(out=OT[0:64, chunk, 0:nq], in0=ps[bnk][0:64, 0:nq], in1=TB[7][0:64, 0:nq], op=ALU.mult),
                          r=(("ps", bnk), ("T", 7)), w=(("GT", chunk),))
            if m == "A" and u % 2 == 1:
                for hf in range(2):
                    P.add("dve", lambda e, hf=hf: e.scalar_tensor_tensor(out=TB[4 + hf][0:64, 0:nq], in0=TB[2 + hf][0:64, 0:nq], scalar=neglam[0:64, 0:1], in1=TB[hf][0:64, 0:nq], op0=ALU.mult, op1=ALU.add),
                          r=(("T", hf), ("T", 2 + hf), "neglam"), w=(("T", 4 + hf),))
                    P.add("act", lambda e, hf=hf: e.activation(out=sq2[0:64, hf, 0:nq], in_=TB[4 + hf][0:64, 0:nq], func=AF.Square), r=(("T", 4 + hf),), w=(("sq2", hf),))
                for hf in range(2):
                    P.add("pe", lambda e, hf=hf: e.matmul(ps[6][0:64, 0:nq], lhsT=ones_b[0:64, 0:64], rhs=sq2[0:64, hf, 0:nq], start=(hf == 0), stop=(hf == 1)),
                          r=(("sq2", hf), "ones_b"), w=(("ps", 6),))
                P.add("dve", lambda e: e.tensor_scalar(out=TB[6][0:64, 0:nq], in0=ps[6][0:64, 0:nq], scalar1=1.0 / 128, scalar2=EPS, op0=ALU.mult, op1=ALU.add), r=(("ps", 6),), w=(("T", 6),))
                P.add("act", lambda e: e.activation(out=TB[6][0:64, 0:nq], in_=TB[6][0:64, 0:nq], func=AF.Ln), r=(("T", 6),), w=(("T", 6),))
                P.add("act", lambda e: e.activation(out=TB[6][0:64, 0:nq], in_=TB[6][0:64, 0:nq], func=AF.Exp, scale=-0.5), r=(("T", 6),), w=(("T", 6),))
                for hf in range(2):
                    P.add("dve", lambda e, hf=hf, h=h: e.scalar_tensor_tensor(out=OT[0:64, 2 * h + hf, 0:nq], in0=TB[4 + hf][0:64, 0:nq], scalar=subg[0:64, hf:hf + 1], in1=TB[6][0:64, 0:nq], op0=ALU.mult, op1=ALU.mult),
                          r=(("T", 4 + hf), ("T", 6), "subg"), w=(("GT", 2 * h + hf),))

        def flush_fin():
            if pend_fin[0] is not None:
                f = pend_fin[0]
                pend_fin[0] = None
                f()

        for u in range(8):
            ob = 2 + 2 * oset[0]
            oset[0] ^= 1
            h = u // 2 if m == "A" else u
            vh = h if m == "A" else u
            if m == "B":
                s_bb = 0
                for k0_ in range(kstart, kend, 128):
                    dk_ = (512 - (q0 - k0_)) // 128
                    dma("sp", bbr[0][:, dk_, :], bbias[u, :, dk_, :], r=("bbias",), w=(("bbr", dk_),))
            first = True
            c0 = (kstart // CH) * CH
            chunks = []
            while c0 < kend:
                chunks.append((max(c0, kstart), min(c0 + CH, kend)))
                c0 += CH
            pend_q = []
            sbanks = (0, 1) if nhalf == 2 else (0, 1, 3, 5)
            depth = len(sbanks) // 2
            ntile = 0
            for (lo, hi) in chunks:
                s = nxt("k", NKR)
                dkeys = tuple((g, m, "k", t) for t in range(lo // TT, (hi - 1) // TT + 1))
                dma("sp", kring[s][0:64, 0:hi - lo], kd[u, :, lo:hi], r=dkeys, w=(("kr", s),))
                kt_lo, kt_hi = lo // 128, (hi + 127) // 128
                vkeys = tuple((g, m, "v", t, vh) for t in range(lo // TT, (hi - 1) // TT + 1))
                dma("sp", vring[s][:, 0:kt_hi - kt_lo, 0:xw], vd[vh, :, kt_lo:kt_hi, :], r=vkeys, w=(("vr", s),))
                k0 = lo
                while k0 < hi:
                    nk = min(128, hi - k0)
                    off = k0 - lo
                    kt = k0 // 128 - kt_lo
                    last = (k0 + nk >= kend)
                    sbk = sbanks[nxt("sps%d" % len(sbanks), len(sbanks))]
                    P.add("pe", lambda e, s=s, off=off, nk=nk, sbk=sbk, u=u: e.matmul(ps[sbk][0:nk, 0:nq], lhsT=kring[s][0:128, off:off + nk], rhs=QT[m][0:128, u, 0:nq], start=True, stop=True),
                          r=(("kr", s), ("kr1", s), ("QT", u), "QT_aug", "QT_z"), w=(("ps", sbk),))
                    diag = k0 >= q0
                    bias = 0.0
                    src = ps[sbk][0:nk, 0:nq]
                    srck = (("ps", sbk),)
                    if m == "A":
                        if diag:
                            js = (k0 - q0) // 128
                            ds = nxt("dt", 2)
                            P.add("dve", lambda e, js=js, ds=ds, nk=nk, sbk=sbk, h=h: e.scalar_tensor_tensor(out=dtmp[ds][0:nk, 0:nq], in0=t0m[0:nk, js, 0:nq], scalar=float(slopes[h]), in1=ps[sbk][0:nk, 0:nq], op0=ALU.mult, op1=ALU.add),
                                  r=(("ps", sbk), "t0m"), w=(("dtmp", ds),))
                            src = dtmp[ds][0:nk, 0:nq]; srck = (("dtmp", ds),)
                        else:
                            d128 = (q0 - k0) // 128
                            bias = bkA[0:nk, h, d128:d128 + 1]
                    elif m == "C":
                        ktabs = k0 // 128
                        bias = negF[g][0:nk, ktabs, u:u + 1]
                        if diag:
                            js = (k0 - q0) // 128
                            ds = nxt("dt", 2)
                            P.add("dve", lambda e, js=js, ds=ds, nk=nk, sbk=sbk: e.tensor_tensor(out=dtmp[ds][0:nk, 0:nq], in0=ps[sbk][0:nk, 0:nq], in1=cmask[0:nk, js, 0:nq], op=ALU.add),
                                  r=(("ps", sbk), "cmask"), w=(("dtmp", ds),))
                            src = dtmp[ds][0:nk, 0:nq]; srck = (("dtmp", ds),)
                    else:
                        dk = (512 - (q0 - k0)) // 128
                        ds = nxt("dt", 2)
                        P.add("dve", lambda e, dk=dk, ds=ds, nk=nk, sbk=sbk, s_bb=s_bb: e.tensor_tensor(out=dtmp[ds][0:nk, 0:nq], in0=ps[sbk][0:nk, 0:nq], in1=bbr[s_bb][0:nk, dk, 0:nq], op=ALU.add),
                              r=(("ps", sbk), ("bbr", dk)), w=(("dtmp", ds),))
                        src = dtmp[ds][0:nk, 0:nq]; srck = (("dtmp", ds),)
                    pslot = nxt("pt", NPT)
                    bkeys = ("bkA", ("negF", g)) if not isinstance(bias, float) else ()
                    P.add("act", lambda e, pslot=pslot, src=src, bias=bias, nk=nk: e.activation(out=ptr[pslot][0:nk, 0:nq], in_=src, func=AF.Exp, bias=bias),
                          r=srck + bkeys, w=(("ptr", pslot),))

                    def pv(s=s, kt=kt, nk=nk, pslot=pslot, first=first, last=last, ob=ob):
                        for hf in range(nhalf):
                            if hf == 0:
                                P.add("pe", lambda e: e.matmul(ps[ob][0:128, 0:nq], lhsT=vring[s][0:nk, kt, 0:128], rhs=ptr[pslot][0:nk, 0:nq], start=first, stop=last),
                                      r=(("vr", s), ("vr0", s), ("ptr", pslot)), w=(("ps", ob),))
                            else:
                                P.add("pe", lambda e: e.matmul(ps[ob + 1][0:65, 0:nq], lhsT=vring[s][0:nk, kt, 65:130], rhs=ptr[pslot][0:nk, 0:nq], start=first, stop=last),
                                      r=(("vr", s), ("vr0", s), ("ptr", pslot)), w=(("ps", ob + 1),))
                    pend_q.append(pv)
                    if len(pend_q) > depth:
                        pend_q.pop(0)()
                    ntile += 1
                    if ntile == 5:
                        flush_fin()
                    first = False
                    k0 += nk
            while pend_q:
                pend_q.pop(0)()
            flush_fin()
            fin_recip(ob)
            pend_fin[0] = (lambda u=u, ob=ob, h=h: finalize(u, ob, h))
        flush_fin()

    def out_proj(G, panels):
        bs, nblk = G["bs"], G["nblk"]
        for nh in range(2):
            total = sum(pn[2] for pn in panels[nh])
            cnt = [0] * nblk
            for (wkey, npart, nkc, lhs_fn) in panels[nh]:
                ws = load_w(wkey)
                for b in range(nblk):
                    for kc in range(nkc):
                        ap, keys = lhs_fn(kc, b)
                        st = (cnt[b] == 0)
                        cnt[b] += 1
                        sp_ = (cnt[b] == total)
                        P.add("pe", lambda e, ap=ap, ws=ws, kc=kc, b=b, st=st, sp_=sp_, npart=npart: e.matmul(ps[2 + b][0:bs, 0:512], lhsT=ap, rhs=wring[ws][0:npart, kc, 0:512], start=st, stop=sp_),
                              r=(("wr", ws),) + keys, w=(("ps", 2 + b),))
            for b in range(nblk):
                P.add("dve", lambda e, b=b, nh=nh: e.tensor_tensor(out=xres[0:bs, b, nh * 512:(nh + 1) * 512], in0=ps[2 + b][0:bs, 0:512], in1=xres[0:bs, b, nh * 512:(nh + 1) * 512], op=ALU.add),
                      r=(("ps", 2 + b), ("xres", b)), w=(("xres", b),))

    def ffn(G, L):
        bs, nblk, ntok = G["bs"], G["nblk"], G["ntok"]
        norm_T(G, 2 * L + 1)
        for pi, (c0, cn) in enumerate(HP):
            wa = load_w("w1_%d_%d" % (L, pi))
            wb = load_w("w3_%d_%d" % (L, pi))
            for j in range(cn // 128):
                hc = c0 // 128 + j
                b1 = nxt("ps", 6)
                for kc in range(8):
                    P.add("pe", lambda e, kc=kc, j=j, b1=b1, wa=wa: e.matmul(ps[b1][:, 0:ntok], lhsT=wring[wa][:, kc, j * 128:(j + 1) * 128], rhs=hT[:, kc, 0:ntok], start=(kc == 0), stop=(kc == 7)),
                          r=(("wr", wa),) + hT_keys(G), w=(("ps", b1),))
                b3 = nxt("ps", 6)
                for kc in range(8):
                    P.add("pe", lambda e, kc=kc, j=j, b3=b3, wb=wb: e.matmul(ps[b3][:, 0:ntok], lhsT=wring[wb][:, kc, j * 128:(j + 1) * 128], rhs=hT[:, kc, 0:ntok], start=(kc == 0), stop=(kc == 7)),
                          r=(("wr", wb),) + hT_keys(G), w=(("ps", b3),))
                ds = nxt("dt", 2)
                P.add("act", lambda e, b1=b1, ds=ds: e.activation(out=dtmp[ds][:, 0:ntok], in_=ps[b1][:, 0:ntok], func=AF.Silu), r=(("ps", b1),), w=(("dtmp", ds),))
                P.add("dve", lambda e, b3=b3, ds=ds, hc=hc: e.tensor_tensor(out=GT[:, hc, 0:ntok], in0=ps[b3][:, 0:ntok], in1=dtmp[ds][:, 0:ntok], op=ALU.mult),
                      r=(("ps", b3), ("dtmp", ds)), w=(("GT", hc),))
        panels = []
        for nh in range(2):
            pl = []
            for kp, (k0, kn) in enumerate(KP2):
                pl.append(("w2_%d_%d_%d" % (L, nh, kp), 128, kn,
                           lambda kc, b, k0=k0: (GT[:, k0 + kc, b * bs:(b + 1) * bs], (("GT", k0 + kc),))))
            panels.append(pl)
        out_proj(G, panels)

    def layer0(G):
        g, bs, nblk, ntok = G["g"], G["bs"], G["nblk"], G["ntok"]
        chk(3.1)
        norm_T(G, 0)
        chk(3.2)
        for m, base in (("A", 0), ("B", 3)):
            ws = load_w("abin%d" % base)
            proj_units(G, ws, range(8), lambda u, m=m: QT[m][0:64, u, 0:ntok], lambda u, m=m: (("QT", u),), scale=0.125)
            chk(3.4)
            ws = load_w("abin%d" % (base + 1))
            proj_units(G, ws, range(8), lambda u: KTs[0:64, u, 0:ntok], lambda u: (("KTs", u),))
            chk(3.5)
            kname = ("a_k" if m == "A" else "b_k")
            vname = ("a_v" if m == "A" else "b_v")
            if m == "B":
                if g == "p":
                    kname = kname if G["last"] else None
                    vname = vname if G["last"] else None
            k_tok_out(G, ws, kname)
            chk(3.6)
            ws = load_w("abin%d" % (base + 2))
            chk(3.62)
            import os
            if os.environ.get("NOVS", "0") == "1":
                k_tok_out(G, ws, vname)
            else:
                k_tok_out(G, ws, vname, also=lambda b, bnk, s: v_stage(G, b, s))
            chk(4)
            write_kv_scratch(G, m)
            chk(5)
            if m == "A":
                dma("sp", QT1[64:65, :, :], cin["cqA"][:, :, :], r=(), w=("QT_aug",))
            else:
                P.add("pool", lambda e: e.memset(QT1[64:65, :, :], 0.0), w=("QT_aug",))
            attention(G, m)
            chk(6)
        panels = []
        for nh in range(2):
            pl = []
            for kp in range(2):
                pl.append(("about%d_%d" % (nh, kp), 64, 8,
                           lambda kc, b, kp=kp: (OT[0:64, kp * 8 + kc, b * bs:(b + 1) * bs], (("GT", kp * 8 + kc),))))
            panels.append(pl)
        out_proj(G, panels)
        ffn(G, 0)

    def logf_block(G, b, bnk):
        g, bs = G["g"], G["bs"]
        P.add("dve", lambda e: e.tensor_tensor(out=lft[0:bs, :], in0=ps[bnk][0:bs, 0:8], in1=cfb[0:bs, :], op=ALU.add), r=(("ps", bnk), "cfb"), w=("lft",))
        P.add("act", lambda e: e.activation(out=lft[0:bs, :], in_=lft[0:bs, :], func=AF.Exp, scale=-1.0), r=("lft",), w=("lft",))
        P.add("act", lambda e: e.activation(out=lft[0:bs, :], in_=lft[0:bs, :], func=AF.Ln, bias=1.0), r=("lft",), w=("lft",))
        P.add("dve", lambda e: e.tensor_scalar(out=lfo[0:bs, :], in0=lft[0:bs, :], scalar1=-1.0, scalar2=None, op0=ALU.mult), r=("lft",), w=("lfo",))
        dma("pool", out_rows(G, "c_logf", b), lfo[0:bs, :], r=("lfo",), w=(("out", "c_logf", G["g"], G["orow"], b),))
        cum_block(G, G["q0"] // 128 + b, bs, b)

    def cum_block(G, ktabs, bs, b=None):
        g = G["g"]
        has_prev = ktabs > 0
        P.add("pe", lambda e: e.matmul(ps[6][0:bs, 0:8], lhsT=tri_f[0:bs, 0:bs], rhs=lft[0:bs, :], start=True, stop=not has_prev), r=("lft", "tri_f"), w=(("ps", 6),))
        if has_prev:
            P.add("pe", lambda e: e.matmul(ps[6][0:bs, 0:8], lhsT=sel_f[0:128, 0:bs], rhs=negF[g][0:128, ktabs - 1, :], start=False, stop=True),
                  r=(("negF", g), "sel_f"), w=(("ps", 6),))
        P.add("dve", lambda e: e.tensor_copy(out=negF[g][0:bs, ktabs, :], in_=ps[6][0:bs, 0:8]), r=(("ps", 6),), w=(("negF", g),))
        if b is not None:
            P.add("dve", lambda e: e.tensor_scalar(out=lfb[0:bs, :], in0=negF[g][0:bs, ktabs, :], scalar1=-1.0, scalar2=None, op0=ALU.mult), r=(("negF", g),), w=("lfb",))
            P.add("pe", lambda e: e.transpose(out=ptb[0:8, 0, :], in_=lfb[:, :], identity=ident_b[:, :]), r=("lfb", "ident_b"), w=(("ptb", 0),))
            P.add("act", lambda e: e.activation(out=FTb[0:8, b * bs:(b + 1) * bs], in_=ptb[0:8, 0, 0:bs], func=AF.Copy), r=(("ptb", 0),), w=("FTb",))

    def dbranch(G, wsx, wsg):
        g, bs, nblk, ntok = G["g"], G["bs"], G["nblk"], G["ntok"]
        xb = dxbuf[g]
        u_, r_, i_, a_, b_, x_, t_ = TB[0:7]
        for cb in range(4):
            bx = nxt("ps", 6)
            for kc in range(8):
                P.add("pe", lambda e, kc=kc, bx=bx, cb=cb: e.matmul(ps[bx][:, 0:ntok], lhsT=wring[wsx][:, kc, cb * 128:(cb + 1) * 128], rhs=hT[:, kc, 0:ntok], start=(kc == 0), stop=(kc == 7)),
                      r=(("wr", wsx),) + hT_keys(G), w=(("ps", bx),))
            bg = nxt("ps", 6)
            for kc in range(8):
                P.add("pe", lambda e, kc=kc, bg=bg, cb=cb: e.matmul(ps[bg][:, 0:ntok], lhsT=wring[wsg][:, kc, cb * 128:(cb + 1) * 128], rhs=hT[:, kc, 0:ntok], start=(kc == 0), stop=(kc == 7)),
                      r=(("wr", wsg),) + hT_keys(G), w=(("ps", bg),))
            kx = ("dxbuf", g, cb)
            P.add("dve", lambda e, bx=bx, cb=cb: e.tensor_copy(out=xb[:, cb, 3:3 + ntok], in_=ps[bx][:, 0:ntok]), r=(("ps", bx),), w=(kx,))
            P.add("dve", lambda e, cb=cb: e.tensor_scalar(out=u_[:, 0:ntok], in0=xb[:, cb, 3:3 + ntok], scalar1=convw[:, cb, 3:4], scalar2=convb[:, cb:cb + 1], op0=ALU.mult, op1=ALU.add),
                  r=(kx, "convw", "convb"), w=(("T", 0),))
            for j in range(3):
                P.add("dve", lambda e, cb=cb, j=j: e.scalar_tensor_tensor(out=u_[:, 0:ntok], in0=xb[:, cb, j:j + ntok], scalar=convw[:, cb, j:j + 1], in1=u_[:, 0:ntok], op0=ALU.mult, op1=ALU.add),
                      r=(kx, "convw", ("T", 0)), w=(("T", 0),))
            P.add("dve", lambda e, cb=cb: e.tensor_copy(out=xb[:, cb, 0:3], in_=xb[:, cb, ntok:ntok + 3]), r=(kx, ("T", 0)), w=(kx,))
            P.add("act", lambda e: e.activation(out=ub[:, 0:ntok], in_=u_[:, 0:ntok], func=AF.Copy), r=(("T", 0),), w=("ub",))
            ba = nxt("ps", 6)
            P.add("pe", lambda e, ba=ba, cb=cb: e.matmul(ps[ba][:, 0:ntok], lhsT=bda[:, cb, :], rhs=ub[:, 0:ntok], start=True, stop=True), r=("ub", "bda"), w=(("ps", ba),))
            bi = nxt("ps", 6)
            P.add("pe", lambda e, bi=bi, cb=cb: e.matmul(ps[bi][:, 0:ntok], lhsT=bdx[:, cb, :], rhs=ub[:, 0:ntok], start=True, stop=True), r=("ub", "bdx"), w=(("ps", bi),))
            P.add("act", lambda e, ba=ba, cb=cb: e.activation(out=r_[:, 0:ntok], in_=ps[ba][:, 0:ntok], func=AF.Sigmoid, bias=bat[:, cb:cb + 1]), r=(("ps", ba), "bat"), w=(("T", 1),))
            P.add("act", lambda e, bi=bi, cb=cb: e.activation(out=i_[:, 0:ntok], in_=ps[bi][:, 0:ntok], func=AF.Sigmoid, bias=bxt[:, cb:cb + 1]), r=(("ps", bi), "bxt"), w=(("T", 2),))
            P.add("act", lambda e, cb=cb: e.activation(out=a_[:, 0:ntok], in_=r_[:, 0:ntok], func=AF.Exp, scale=m8sp[:, cb:cb + 1]), r=(("T", 1), "m8sp"), w=(("T", 3),))
            P.add("act", lambda e, cb=cb: e.activation(out=b_[:, 0:ntok], in_=r_[:, 0:ntok], func=AF.Exp, scale=m16sp[:, cb:cb + 1]), r=(("T", 1), "m16sp"), w=(("T", 4),))
            P.add("dve", lambda e: e.tensor_scalar(out=b_[:, 0:ntok], in0=b_[:, 0:ntok], scalar1=-1.0, scalar2=1.0, op0=ALU.mult, op1=ALU.add), r=(("T", 4),), w=(("T", 4),))
            P.add("act", lambda e: e.activation(out=b_[:, 0:ntok], in_=b_[:, 0:ntok], func=AF.Ln), r=(("T", 4),), w=(("T", 4),))
            P.add("act", lambda e: e.activation(out=b_[:, 0:ntok], in_=b_[:, 0:ntok], func=AF.Exp, scale=0.5), r=(("T", 4),), w=(("T", 4),))
            P.add("dve", lambda e: e.tensor_tensor(out=i_[:, 0:ntok], in0=i_[:, 0:ntok], in1=u_[:, 0:ntok], op=ALU.mult), r=(("T", 2), ("T", 0)), w=(("T", 2),))
            P.add("dve", lambda e: e.tensor_tensor(out=b_[:, 0:ntok], in0=b_[:, 0:ntok], in1=i_[:, 0:ntok], op=ALU.mult), r=(("T", 2), ("T", 4)), w=(("T", 4),))
            kh = ("hst", g)
            P.add("dve", lambda e, cb=cb: e.tensor_tensor_scan(out=r_[:, 0:ntok], data0=a_[:, 0:ntok], data1=b_[:, 0:ntok], initial=hst[g][:, cb:cb + 1], op0=ALU.mult, op1=ALU.add),
                  r=(("T", 3), ("T", 4), kh), w=(("T", 1),))
            P.add("dve", lambda e, cb=cb: e.tensor_copy(out=hst[g][:, cb:cb + 1], in_=r_[:, ntok - 1:ntok]), r=(("T", 1),), w=(kh,))
            P.add("act", lambda e, bg=bg: e.activation(out=x_[:, 0:ntok], in_=ps[bg][:, 0:ntok], func=AF.Copy), r=(("ps", bg),), w=(("T", 5),))
            P.add("dve", lambda e: e.tensor_tensor(out=t_[:, 0:ntok], in0=x_[:, 0:ntok], in1=x_[:, 0:ntok], op=ALU.mult), r=(("T", 5),), w=(("T", 6),))
            P.add("dve", lambda e: e.tensor_scalar(out=t_[:, 0:ntok], in0=t_[:, 0:ntok], scalar1=0.044715, scalar2=1.0, op0=ALU.mult, op1=ALU.add), r=(("T", 6),), w=(("T", 6),))
            P.add("dve", lambda e: e.tensor_tensor(out=t_[:, 0:ntok], in0=t_[:, 0:ntok], in1=x_[:, 0:ntok], op=ALU.mult), r=(("T", 6), ("T", 5)), w=(("T", 6),))
            P.add("act", lambda e: e.activation(out=t_[:, 0:ntok], in_=t_[:, 0:ntok], func=AF.Sigmoid, scale=1.5957691216057308), r=(("T", 6),), w=(("T", 6),))
            P.add("dve", lambda e: e.tensor_tensor(out=t_[:, 0:ntok], in0=t_[:, 0:ntok], in1=x_[:, 0:ntok], op=ALU.mult), r=(("T", 6), ("T", 5)), w=(("T", 6),))
            P.add("dve", lambda e, cb=cb: e.tensor_tensor(out=odT[:, cb, 0:ntok], in0=t_[:, 0:ntok], in1=r_[:, 0:ntok], op=ALU.mult), r=(("T", 6), ("T", 1)), w=(("odT", cb),))
        if G["last"]:
            dma("pool", O[g + "d_conv"][:, :, :], xb[:, :, 0:3], r=tuple(("dxbuf", g, c) for c in range(4)), w=(("out", "d_conv"),))
            dma("pool", O[g + "d_h"][:, :], hst[g][:, :], r=(("hst", g),), w=(("out", "d_h"),))

    def layer1(G):
        g, bs, nblk, ntok = G["g"], G["bs"], G["nblk"], G["ntok"]
        norm_T(G, 2)
        ws = load_w("cdin0")
        proj_units(G, ws, range(8), lambda u: QT["C"][0:64, u, 0:ntok], lambda u: (("QT", u),), scale=0.125)
        ws = load_w("cdin1")
        proj_units(G, ws, range(8), lambda u: KTs[0:64, u, 0:ntok], lambda u: (("KTs", u),))
        k_tok_out(G, ws, "c_k")
        ws = load_w("cdin2")
        k_tok_out(G, ws, "c_v", also=lambda b, bnk, s: v_stage(G, b, s))
        write_kv_scratch(G, "C")
        ws = load_w("cdcf")
        proj_tok(G, ws, 8, lambda b, bnk: logf_block(G, b, bnk))
        dma("sp", QT["C"][64:65, :, 0:ntok], FTb[0:8, 0:ntok], r=("FTb",), w=("QT_aug",))
        attention(G, "C")
        chk(8)
        wsx = load_w("cdin3")
        wsg = load_w("cdin4")
        dbranch(G, wsx, wsg)
        panels = []
        for nh in range(2):
            pl = [("cdoutc%d" % nh, 64, 8, lambda kc, b: (OT[0:64, kc, b * bs:(b + 1) * bs], (("GT", kc),))),
                  ("cdoutd%d" % nh, 128, 4, lambda kc, b: (odT[:, kc, b * bs:(b + 1) * bs], (("odT", kc),)))]
            panels.append(pl)
        out_proj(G, panels)
        ffn(G, 1)

    def final_norm(G):
        g, bs, nblk = G["g"], G["bs"], G["nblk"]
        rstd_all(G)
        for b in range(nblk):
            for nh in range(2):
                ti = (2 * b + nh) % 8
                P.add("dve", lambda e, b=b, nh=nh, ti=ti: e.scalar_tensor_tensor(out=TB[ti][0:bs, :], in0=xres[0:bs, b, nh * 512:(nh + 1) * 512], scalar=rstd4[0:bs, b:b + 1], in1=gfin[0:bs, nh * 512:(nh + 1) * 512], op0=ALU.mult, op1=ALU.mult),
                      r=(("xres", b), "rstd4", "gfin"), w=(("T", ti),))
                dma("pool", out_rows(G, "y", b)[:, nh * 512:(nh + 1) * 512], TB[ti][0:bs, :], r=(("T", ti),), w=(("out", "y", G["g"], G["orow"], b, nh),))

    def run_tile(G):
        bs, nblk = G["bs"], G["nblk"]
        src = G["x"]
        for b in range(nblk):
            dma("sp", xres[0:bs, b, :], src[b * bs:(b + 1) * bs, :], r=(), w=(("xres", b),))
        layer0(G)
        chk(7)
        layer1(G)
        chk(9)
        final_norm(G)

    def ingest(m, csrc_k, csrc_v, ntok_c, pos0, roll=None):
        for t0 in range(0, ntok_c, TT):
            q0 = pos0 + t0
            Gc = {"g": "s", "bs": 128, "nblk": 4, "ntok": TT, "q0": q0}
            for b in range(4):
                tk = b % 2
                tv = 2 + b % 2
                r0 = t0 + b * 128
                dma("sp", TB[tk][:, :], csrc_k[r0:r0 + 128, :], r=(), w=(("T", tk),))
                dma("sp", TB[tv][:, :], csrc_v[r0:r0 + 128, :], r=(), w=(("T", tv),))
                import os
                rmode = os.environ.get("ROLLMODE", "1")
                if roll is not None and rmode != "0":
                    p0 = DEC if r0 == 0 else 0
                    for (Oo, tt) in ((roll[0], tk), (roll[1], tv)):
                        if rmode == "2":
                            if r0 == 0:
                                dma("pool", O["sa_k"][0:64, :], TB[tt][64:128, :], r=(("T", tt),), w=(("out", "roll"),))
                        elif rmode == "3":
                            dma("sp", Oo[r0 + p0 - DEC:r0 + 128 - DEC, :], TB[tt][p0:128, :], r=(("T", tt),), w=(("out", "roll"),))
                        else:
                            dma("pool", Oo[r0 + p0 - DEC:r0 + 128 - DEC, :], TB[tt][p0:128, :], r=(("T", tt),), w=(("out", "roll"),))
                P.add("dve", lambda e, tk=tk: e.tensor_copy(out=xn[:, 0:512], in_=TB[tk][:, :]), r=(("T", tk),), w=("xn",))
                for u in range(8):
                    P.add("pe", lambda e, u=u: e.transpose(out=ptb[0:64, u, :], in_=xn[:, u * 64:(u + 1) * 64], identity=ident_b[:, :]),
                          r=("xn", "ident_b"), w=(("ptb", u),))
                evac(KTs[0:64, :, b * 128:(b + 1) * 128], ptb[0:64, :, :], r=tuple(("ptb", u) for u in range(8)), w=tuple(("KTs", u) for u in range(8)))
                P.add("dve", lambda e, b=b, tv=tv: e.tensor_copy(out=Vs[:, b, :, 0:64], in_=TB[tv][:, :].rearrange("p (u d) -> p u d", d=64)), r=(("T", tv), "Vs1"), w=(("Vs", b),))
            write_kv_scratch(Gc, m)

    epsc = sb("epsc", [128, 1])
    def main_seq(chk):
        chk(0)
        prologue()
        chk(1)

        Gs = {"g": "s", "bs": DEC, "nblk": 1, "ntok": DEC, "q0": PAST, "orow": 0, "last": True, "x": x_s, "kmin": PAST - 512}
        ingest("A", cak, cav, PAST, 0)
        ingest("B", cbk, cbv, 512, PAST - 512, roll=(O["sb_k"], O["sb_v"]))
        ingest("C", cck, ccv, PAST, 0)
        chk(2)
        nb_c = PAST // 128
        lcs = sb("lcs", [128, nb_c, 8])
        dma("sp", lcs[:, :, :], cclf.rearrange("(b p) h -> p b h", p=128), r=(), w=("lcs",))
        for b in range(nb_c):
            P.add("dve", lambda e, b=b: e.tensor_scalar(out=lft[:, :], in0=lcs[:, b, :], scalar1=-1.0, scalar2=None, op0=ALU.mult), r=("lcs",), w=("lft",))
            cum_block(Gs, b, 128)
        chk(2.3)
        dma("sp", dxbuf["s"][:, :, 0:3], sdc[:, :, :], r=(), w=tuple(("dxbuf", "s", c) for c in range(4)))
        dma("sp", hst["s"][:, :], sdh[:, :], r=(), w=(("hst", "s"),))
        chk(2.6)
        Gs["orow_b"] = 448
        chk(3)
        import os
        if os.environ.get("SKIPS", "0") != "1":
            run_tile(Gs)
            chk(10)

        for cb in range(4):
            P.add("dve", lambda e, cb=cb: e.memset(dxbuf["p"][:, cb, 0:3], 0.0), w=(("dxbuf", "p", cb),))
        P.add("dve", lambda e: e.memset(hst["p"][:, :], 0.0), w=(("hst", "p"),))
        for t in range(NT):
            Gp = {"g": "p", "bs": 128, "nblk": 4, "ntok": TT, "q0": t * TT, "orow": t * TT, "last": t == NT - 1,
                  "x": x_p[t * TT:(t + 1) * TT, :], "kmin": 0, "orow_b": 0}
            run_tile(Gp)


    P.add("pool", lambda e: e.memset(epsc[:, :], EPS), w=("epsc",))
    P.add("pool", lambda e: e.memset(xn[:, :], 0.0), w=("xn",))
    P.add("pool", lambda e: e.memset(lfb[:, :], 0.0), w=("lfb",))
    try:
        main_seq(chk)
    except _Stop:
        pass

    P.emit()
    return nc, consts


_NAMES_B = ("b_k", "b_v")


_CACHE = {}


def _get_prog(SEQ, PAST):
    key = (SEQ, PAST)
    if key not in _CACHE:
        _CACHE[key] = build(SEQ, PAST)
    return _CACHE[key]


def kernel(**inp):
    x_prompt = np.asarray(inp["x_prompt"]); x_sample = np.asarray(inp["x_sample"])
    NB, SEQ, _ = x_prompt.shape
    NS = x_sample.shape[0]
    PAST = inp["cache_a_k"].shape[1]
    nc, consts = _get_prog(SEQ, PAST)
    f = lambda a: np.ascontiguousarray(np.asarray(a, dtype=np.float32))
    shared = {}
    for k in ("final_g", "ab_w_in", "ab_w_out", "a_lambda", "b_rel_bias", "cd_w_in",
              "cd_w_out", "c_f_bias", "d_w_a", "d_w_x", "ffn_w1", "ffn_w3", "ffn_w2"):
        shared[k] = f(inp[k])
    pc = lambda v: f(np.asarray(v, np.float32).reshape(4, 128).T)
    shared["d_b_a"] = pc(inp["d_b_a"]); shared["d_b_x"] = pc(inp["d_b_x"])
    shared["d_conv_b"] = pc(inp["d_conv_b"]); shared["d_lambda"] = pc(inp["d_lambda"])
    shared["d_conv_w"] = f(np.asarray(inp["d_conv_w"], np.float32).reshape(4, 4, 128).transpose(2, 1, 0))
    shared["a_subln_g"] = f(np.asarray(inp["a_subln_g"], np.float32).reshape(2, 64).T)
    gs = np.stack([np.asarray(inp["norm_mix_g"])[0], np.asarray(inp["norm_ffn_g"])[0],
                   np.asarray(inp["norm_mix_g"])[1], np.asarray(inp["norm_ffn_g"])[1]], 0).astype(np.float32)
    shared["gT_in"] = f(gs)
    for k, v in consts.items():
        shared["c_" + k] = v
    in_maps = []
    for c in range(8):
        m = dict(shared)
        m["x_p"] = f(x_prompt[c % NB]); m["x_s"] = f(x_sample[c % NS])
        s = c % NS
        m["cache_a_k"] = f(inp["cache_a_k"][s]).reshape(PAST, 512); m["cache_a_v"] = f(inp["cache_a_v"][s]).reshape(PAST, 512)
        m["cache_b_k"] = f(inp["cache_b_k"][s]).reshape(512, 512); m["cache_b_v"] = f(inp["cache_b_v"][s]).reshape(512, 512)
        m["cache_c_k"] = f(inp["cache_c_k"][s]).reshape(PAST, 512); m["cache_c_v"] = f(inp["cache_c_v"][s]).reshape(PAST, 512)
        m["cache_c_logf"] = f(inp["cache_c_logf"][s])
        m["state_d_conv"] = f(np.asarray(inp["state_d_conv"][s], np.float32).reshape(3, 4, 128).transpose(2, 1, 0))
        m["state_d_h"] = f(np.asarray(inp["state_d_h"][s], np.float32).reshape(4, 128).T)
        in_maps.append(m)
    res = run_bass_kernel_spmd(nc, in_maps, core_ids=list(range(8)))
    R = res.results

    def stack(name, cores, shape):
        if name.endswith("d_conv"):
            return np.stack([np.asarray(R[c][name], dtype=np.float32).reshape(128, 4, 3).transpose(2, 1, 0).reshape(3, 512) for c in cores], axis=0)
        if name.endswith("d_h"):
            return np.stack([np.asarray(R[c][name], dtype=np.float32).reshape(128, 4).T.reshape(512) for c in cores], axis=0)
        return np.stack([np.asarray(R[c][name], dtype=np.float32).reshape(shape) for c in cores], axis=0)
    pc = list(range(NB)); sc = list(range(NS))
    outs = (
        stack("p_y", pc, (SEQ, D)), stack("s_y", sc, (DEC, D)),
        stack("p_a_k", pc, (SEQ, 4, 128)), stack("p_a_v", pc, (SEQ, 4, 128)),
        stack("p_b_k", pc, (512, 8, 64)), stack("p_b_v", pc, (512, 8, 64)),
        stack("p_c_k", pc, (SEQ, 8, 64)), stack("p_c_v", pc, (SEQ, 8, 64)),
        stack("p_c_logf", pc, (SEQ, 8)), stack("p_d_conv", pc, (3, 512)), stack("p_d_h", pc, (512,)),
        stack("s_a_k", sc, (DEC, 4, 128)), stack("s_a_v", sc, (DEC, 4, 128)),
        stack("s_b_k", sc, (512, 8, 64)), stack("s_b_v", sc, (512, 8, 64)),
        stack("s_c_k", sc, (DEC, 8, 64)), stack("s_c_v", sc, (DEC, 8, 64)),
        stack("s_c_logf", sc, (DEC, 8)), stack("s_d_conv", sc, (3, 512)), stack("s_d_h", sc, (512,)),
    )
    return outs
```

```python
import math
import numpy as np
import ml_dtypes
import concourse.bass as bass
import concourse.mybir as mybir
from concourse.bass_utils import run_bass_kernel_spmd

F32 = mybir.dt.float32
BF16 = mybir.dt.bfloat16
AF = mybir.ActivationFunctionType
ALU = mybir.AluOpType
AX = mybir.AxisListType

D = 1024
HID = 2816
TT = 512
DEC = 64
CH = 1024
NEG = -1.0e30
EPS = 1e-6
NDS = 24


class Op:
    __slots__ = ("eng", "fn", "dma", "deps", "sig", "val", "sem", "prev")


class Prog:
    ENGS = ("pe", "act", "dve", "pool", "sp")

    def __init__(self, nc):
        self.nc = nc
        self.ops = {e: [] for e in self.ENGS}
        self.lastw = {}
        self.readers = {}

    def add(self, eng, fn, r=(), w=(), dma=False):
        op = Op()
        op.eng = eng; op.fn = fn; op.dma = dma; op.sig = False; op.val = 0; op.sem = None; op.prev = 0
        deps = {}
        for k in r:
            lw = self.lastw.get(k)
            if lw is not None:
                deps[lw] = True
        for k in w:
            lw = self.lastw.get(k)
            if lw is not None:
                deps[lw] = True
            for rd in self.readers.get(k, ()):
                deps.setdefault(rd, False)
        for k in r:
            lst = self.readers.setdefault(k, [])
            if not dma:
                lst[:] = [o for o in lst if o.dma or o.eng != eng]
            lst.append(op)
        for k in w:
            self.lastw[k] = op
            self.readers[k] = []
        fd = []
        for d, strong in deps.items():
            if d is op:
                continue
            if d.eng == eng and not d.dma and not dma:
                if eng == "pe" or not strong:
                    continue
            d.sig = True
            fd.append(d)
        op.deps = fd
        self.ops[eng].append(op)
        return op

    def emit(self):
        nc = self.nc
        esem = {e: nc.alloc_semaphore("sem_" + e) for e in ("pe", "act", "dve", "pool")}
        dsem = {q: [nc.alloc_semaphore("dq_%s_%d" % (q, i)) for i in range(NDS)] for q in ("sp", "pool", "act")}
        final = {}
        for e in self.ENGS:
            c = 0
            di = 0
            dcount = [0] * NDS
            for op in self.ops[e]:
                if op.dma:
                    s = di % NDS
                    di += 1
                    op.prev = 16 * dcount[s]
                    dcount[s] += 1
                    op.sem = dsem[e][s]
                    op.val = 16 * dcount[s]
                    final[op.sem] = op.val
                elif op.sig:
                    c += 1
                    op.sem = esem[e]
                    op.val = c

        def run(e, eng):
            waited = {}
            for op in self.ops[e]:
                need = {}
                for d in op.deps:
                    if need.get(d.sem, 0) < d.val:
                        need[d.sem] = d.val
                if op.dma and op.prev > 0 and need.get(op.sem, 0) < op.prev:
                    need[op.sem] = op.prev
                for s, v in need.items():
                    if waited.get(s, 0) < v:
                        eng.wait_ge(s, v)
                        waited[s] = v
                ins = op.fn(eng)
                if op.dma:
                    ins.then_inc(op.sem, 16)
                elif op.sig:
                    ins.then_inc(op.sem, 1)
            if e == "sp":
                for s, v in final.items():
                    if waited.get(s, 0) < v:
                        eng.wait_ge(s, v)

        with nc.Block() as block:
            @block.tensor
            def _(eng):
                run("pe", eng)

            @block.scalar
            def _(eng):
                run("act", eng)

            @block.vector
            def _(eng):
                run("dve", eng)

            @block.gpsimd
            def _(eng):
                run("pool", eng)

            @block.sync
            def _(eng):
                run("sp", eng)


def _bf16r(a):
    return np.asarray(a, np.float32).astype(ml_dtypes.bfloat16).astype(np.float32)


def make_consts(nda):
    c = {}
    c["ident_f"] = np.eye(128, dtype=np.float32)
    c["ident_b"] = np.eye(128, dtype=np.float32).astype(ml_dtypes.bfloat16)
    t = np.arange(128)
    c["tri_f"] = (t[:, None] <= t[None, :]).astype(np.float32)
    sel = np.zeros((128, 128), np.float32); sel[127, :] = 1.0
    c["sel_f"] = sel
    c["anti_f"] = np.ascontiguousarray(np.eye(128, dtype=np.float32)[::-1])
    i = np.arange(TT)
    ri = _bf16r(i.astype(np.float32))
    slopes = np.array([2.0 ** (-8.0 * (h + 1) / 4) for h in range(4)], np.float32)
    cq = np.zeros((1, 8, TT), np.float32)
    for h in range(4):
        for cc in range(2):
            cq[0, 2 * h + cc] = -slopes[h] * ri
    assert np.array_equal(_bf16r(cq), cq)
    c["cqA"] = cq.astype(ml_dtypes.bfloat16)
    j = np.arange(128)
    t0 = np.zeros((128, 4, TT), np.float32)
    cm = np.zeros((128, 4, TT), np.float32)
    for js in range(4):
        jp = 128 * js + j
        vis = (jp[:, None] // 64) <= (i[None, :] // 64)
        t0[:, js, :] = np.where(vis, ri[None, :] - np.abs(i[None, :] - jp[:, None]), -4.0e30)
        cm[:, js, :] = np.where(jp[:, None] <= i[None, :], 0.0, NEG)
    c["t0m"] = t0
    c["cmask"] = cm.astype(ml_dtypes.bfloat16)
    bk = np.zeros((128, 4, nda), np.float32)
    for h in range(4):
        bk[:, h, :] = slopes[h] * (j[:, None] - 128.0 * np.arange(nda)[None, :])
    c["bkA"] = bk
    bm = np.zeros((128, 8, TT), np.float32)
    dvals = [512, 384, 256, 128, 0, -128, -256, -384]
    for dk, dv in enumerate(dvals):
        kc_ = np.floor_divide(-dv + j, 64)
        qc_ = i // 64
        vis = (kc_[:, None] >= qc_[None, :] - 8) & (kc_[:, None] <= qc_[None, :])
        bm[:, dk, :] = np.where(vis, 0.0, NEG)
    c["bmask"] = bm.astype(ml_dtypes.bfloat16)
    return c, slopes, dvals


class _Stop(Exception):
    pass


def build(SEQ, PAST, stop=999):
    def chk(stage):
        if stage > stop:
            raise _Stop()
    NT = SEQ // TT
    NKT_P = SEQ // 128
    NKT_S = PAST // 128 + 1
    NDA = max(SEQ, PAST) // 128 + 2
    consts, slopes, dvals = make_consts(NDA)
    nc = bass.Bass("TRN2", target_bir_lowering=False)
    P = Prog(nc)

    def din(name, shape, dt=F32):
        return nc.dram_tensor(name, list(shape), dt, kind="ExternalInput").ap()

    def dout(name, shape):
        return nc.dram_tensor(name, list(shape), F32, kind="ExternalOutput").ap()

    def dscr(name, shape, dt=BF16):
        return nc.dram_tensor(name, list(shape), dt).ap()

    def sb(name, shape, dt=F32):
        return nc.alloc_sbuf_tensor(name, list(shape), dt)

    x_p = din("x_p", [SEQ, D]); x_s = din("x_s", [DEC, D])
    cak = din("cache_a_k", [PAST, 512]); cav = din("cache_a_v", [PAST, 512])
    cbk = din("cache_b_k", [512, 512]); cbv = din("cache_b_v", [512, 512])
    cck = din("cache_c_k", [PAST, 512]); ccv = din("cache_c_v", [PAST, 512])
    cclf = din("cache_c_logf", [PAST, 8])
    sdc = din("state_d_conv", [128, 4, 3]); sdh = din("state_d_h", [128, 4])
    gT_in = din("gT_in", [4, D]); fing = din("final_g", [D])
    ab_w_in = din("ab_w_in", [D, 3072]); ab_w_out = din("ab_w_out", [D, D])
    a_lambda = din("a_lambda", [4, 64]); a_subln = din("a_subln_g", [64, 2]); b_rel = din("b_rel_bias", [8, 257])
    cd_w_in = din("cd_w_in", [D, 2568]); cd_w_out = din("cd_w_out", [D, D]); c_f_bias = din("c_f_bias", [8])
    d_conv_w = din("d_conv_w", [128, 4, 4]); d_conv_b = din("d_conv_b", [128, 4])
    d_w_a = din("d_w_a", [8, 64, 64]); d_b_a = din("d_b_a", [128, 4]); d_w_x = din("d_w_x", [8, 64, 64])
    d_b_x = din("d_b_x", [128, 4]); d_lam = din("d_lambda", [128, 4])
    w1 = din("ffn_w1", [2, D, HID]); w3 = din("ffn_w3", [2, D, HID]); w2 = din("ffn_w2", [2, HID, D])
    cin = {}
    for k, v in consts.items():
        cin[k] = din("c_" + k, v.shape, BF16 if v.dtype == ml_dtypes.bfloat16 else F32)

    O = {}
    for g, T in (("p", SEQ), ("s", DEC)):
        O[g + "y"] = dout(g + "_y", [T, D])
        O[g + "a_k"] = dout(g + "_a_k", [T, 512]); O[g + "a_v"] = dout(g + "_a_v", [T, 512])
        O[g + "b_k"] = dout(g + "_b_k", [512, 512]); O[g + "b_v"] = dout(g + "_b_v", [512, 512])
        O[g + "c_k"] = dout(g + "_c_k", [T, 512]); O[g + "c_v"] = dout(g + "_c_v", [T, 512])
        O[g + "c_logf"] = dout(g + "_c_logf", [T, 8])
        O[g + "d_conv"] = dout(g + "_d_conv", [128, 4, 3]); O[g + "d_h"] = dout(g + "_d_h", [128, 4])

    WP = {}

    def wpanel(key, src_rows_fn, npart, nkc, ncols):
        t = dscr("wp_%s" % key, [npart, nkc, ncols])
        WP[key] = (t, npart, nkc, ncols)
        for kc in range(nkc):
            src = src_rows_fn(kc)
            P.add("pool", lambda e, o=t[:, kc, :], i=src: e.dma_start(out=o, in_=i), r=(), w=(("wp", key, kc),), dma=True)

    for pi in range(6):
        wpanel("abin%d" % pi, lambda kc, pi=pi: ab_w_in[kc * 128:(kc + 1) * 128, pi * 512:(pi + 1) * 512], 128, 8, 512)
    for nh in range(2):
        for kp in range(2):
            wpanel("about%d_%d" % (nh, kp),
                   lambda kc, nh=nh, kp=kp: ab_w_out[(kp * 8 + kc) * 64:(kp * 8 + kc + 1) * 64, nh * 512:(nh + 1) * 512], 64, 8, 512)
    for pi in range(3):
        wpanel("cdin%d" % pi, lambda kc, pi=pi: cd_w_in[kc * 128:(kc + 1) * 128, pi * 512:(pi + 1) * 512], 128, 8, 512)
    wpanel("cdcf", lambda kc: cd_w_in[kc * 128:(kc + 1) * 128, 1536:1544], 128, 8, 8)
    for pi in range(2):
        wpanel("cdin%d" % (3 + pi), lambda kc, pi=pi: cd_w_in[kc * 128:(kc + 1) * 128, 1544 + pi * 512:1544 + (pi + 1) * 512], 128, 8, 512)
    for nh in range(2):
        wpanel("cdoutc%d" % nh, lambda kc, nh=nh: cd_w_out[kc * 64:(kc + 1) * 64, nh * 512:(nh + 1) * 512], 64, 8, 512)
        wpanel("cdoutd%d" % nh, lambda kc, nh=nh: cd_w_out[512 + kc * 128:512 + (kc + 1) * 128, nh * 512:(nh + 1) * 512], 128, 4, 512)
    HP = [(0, 512), (512, 512), (1024, 512), (1536, 512), (2048, 512), (2560, 256)]
    KP2 = [(0, 8), (8, 8), (16, 6)]
    for L in range(2):
        for pi, (c0, cn) in enumerate(HP):
            wpanel("w1_%d_%d" % (L, pi), lambda kc, L=L, c0=c0, cn=cn: w1[L, kc * 128:(kc + 1) * 128, c0:c0 + cn], 128, 8, cn)
            wpanel("w3_%d_%d" % (L, pi), lambda kc, L=L, c0=c0, cn=cn: w3[L, kc * 128:(kc + 1) * 128, c0:c0 + cn], 128, 8, cn)
        for nh in range(2):
            for kp, (k0, kn) in enumerate(KP2):
                wpanel("w2_%d_%d_%d" % (L, nh, kp),
                       lambda kc, L=L, nh=nh, k0=k0: w2[L, (k0 + kc) * 128:(k0 + kc + 1) * 128, nh * 512:(nh + 1) * 512], 128, kn, 512)

    SCR = {}
    for g, Tt, nkt in (("p", SEQ, NKT_P), ("s", PAST + DEC, NKT_S)):
        for m in "ABC":
            SCR[(g, m, "k")] = dscr("kt_%s%s" % (g, m), [8, 64, Tt])
            if m == "A":
                SCR[(g, m, "v")] = dscr("v_%s%s" % (g, m), [4, 128, nkt, 130])
            else:
                SCR[(g, m, "v")] = dscr("v_%s%s" % (g, m), [8, 128, nkt, 65])
    text = dscr("text", [8, 1536], F32)
    bbias = dscr("bbias", [8, 128, 8, TT])

    ident_f = sb("ident_f", [128, 128]); ident_b = sb("ident_b", [128, 128], BF16)
    tri_f = sb("tri_f", [128, 128]); sel_f = sb("sel_f", [128, 128]); anti_f = sb("anti_f", [128, 128])
    ones_f = sb("ones_f", [128, 64]); ones_b = sb("ones_b", [128, 64], BF16)
    t0m = sb("t0m", [128, 4, TT]); cmask = sb("cmask", [128, 4, TT], BF16)
    bkA = sb("bkA", [128, 4, NDA])
    gbc = sb("gbc", [128, 4, D], BF16); gfin = sb("gfin", [128, D])
    cfb = sb("cfb", [128, 8])
    brt = sb("brt", [8, 257]); txs = sb("txs", [8, 1536])
    neglam = sb("neglam", [128, 1]); lamt = sb("lamt", [128, 4, 64]); lamp = sb("lamp", [128, 2, 64]); lams = sb("lams", [128, 2])
    subg = sb("subg", [64, 2])
    spt = sb("spt", [128, 4]); m8sp = sb("m8sp", [128, 4]); m16sp = sb("m16sp", [128, 4])
    convw = sb("convw", [128, 4, 4]); convb = sb("convb", [128, 4]); bat = sb("bat", [128, 4]); bxt = sb("bxt", [128, 4])
    bdf = sb("bdf", [128, 4, 128]); bda = sb("bda", [128, 4, 128], BF16); bdx = sb("bdx", [128, 4, 128], BF16)
    xres = sb("xres", [128, 4, D])
    xn = sb("xn", [128, D], BF16); sqj = sb("sqj", [128, D], BF16)
    ss4 = sb("ss4", [128, 4]); ss4n = sb("ss4n", [128, 4]); ln4 = sb("ln4", [128, 4]); rstd4 = sb("rstd4", [128, 4])
    hT = sb("hT", [128, 8, TT], BF16)
    QT1 = sb("QT1", [128, 8, TT], BF16)
    QT = {"A": QT1, "B": QT1, "C": QT1}
    KTs = sb("KTs", [64, 8, TT], BF16)
    Vs = sb("Vs", [128, 4, 8, 65], BF16)
    kvo = [sb("kvo%d" % i, [128, 512]) for i in range(2)]
    odT = sb("odT", [128, 4, TT], BF16)
    GT = sb("GT", [128, 22, TT], BF16)
    OT = GT
    NW = 3
    wring = [sb("wring%d" % i, [128, 8, 512], BF16) for i in range(NW)]
    NKR = 3
    kring = [sb("kring%d" % i, [128, CH], BF16) for i in range(NKR)]
    vring = [sb("vring%d" % i, [128, CH // 128, 130], BF16) for i in range(NKR)]
    NPT = 4
    ptr = [sb("ptr%d" % i, [128, TT], BF16) for i in range(NPT)]
    dtmp = [sb("dtmp%d" % i, [128, TT]) for i in range(2)]
    bbr = [sb("bbr%d" % i, [128, 8, TT], BF16) for i in range(1)]
    rl = sb("rl", [65, TT])
    TB = [sb("tb%d" % i, [128, TT]) for i in range(8)]
    sq2 = sb("sq2", [64, 2, TT], BF16)
    negF = {"p": sb("negF_p", [128, NKT_P, 8]), "s": sb("negF_s", [128, NKT_S, 8])}
    lft = sb("lft", [128, 8]); lfo = sb("lfo", [128, 8]); lfb = sb("lfb", [128, 8], BF16); FTb = sb("FTb", [8, TT], BF16)
    dxbuf = {"p": sb("dxbuf_p", [128, 4, TT + 3]), "s": sb("dxbuf_s", [128, 4, DEC + 3])}
    hst = {"p": sb("hst_p", [128, 4]), "s": sb("hst_s", [128, 4])}
    ub = sb("ub", [128, TT], BF16)

    ps = [nc.alloc_psum_tensor("ps%d" % i, [128, 512], F32) for i in range(7)]
    ptb = nc.alloc_psum_tensor("ptb", [128, 8, 128], BF16)

    rot = {"sps2": 0, "sps4": 0, "sps": 0, "w": 0, "k": 0, "pt": 0, "dt": 0, "bb": 0, "kvo": 0, "ps": 0}

    def nxt(name, n):
        v = rot[name]
        rot[name] = (v + 1) % n
        return v

    def dma(q, out, in_, r, w, **kw):
        return P.add(q, lambda e: e.dma_start(out=out, in_=in_, **kw), r=r, w=w, dma=True)

    def load_w(key):
        t, npart, nkc, ncols = WP[key]
        s = nxt("w", NW)
        dma("sp", wring[s][0:npart, 0:nkc, 0:ncols], t[:, :, :], r=tuple(("wp", key, kc) for kc in range(nkc)), w=(("wr", s),))
        return s

    def prologue():
        for k, t in (("ident_f", ident_f), ("ident_b", ident_b), ("tri_f", tri_f), ("sel_f", sel_f), ("anti_f", anti_f), ("t0m", t0m),
                     ("cmask", cmask), ("bkA", bkA)):
            sl = tuple(slice(None) for _ in consts[k].shape)
            dma("sp", t[sl], cin[k][sl], r=(), w=(k,))
        P.add("pool", lambda e: e.memset(ones_f[:, :], 1.0), w=("ones_f",))
        P.add("pool", lambda e: e.memset(ones_b[:, :], 1.0), w=("ones_b",))
        P.add("pool", lambda e: e.memset(QT1[64:128, :, :], 0.0), w=("QT_z", "QT_aug"))
        for i in range(NKR):
            P.add("pool", lambda e, i=i: e.memset(kring[i][64:128, :], 0.0), w=(("kr1", i),))
            P.add("pool", lambda e, i=i: e.memset(kring[i][64:65, :], 1.0), w=(("kr1", i),))
            P.add("pool", lambda e, i=i: e.memset(vring[i][:, :, :], 0.0), w=(("vr0", i), ("vr", i)))
        P.add("pool", lambda e: e.memset(Vs[:, :, :, 64:65], 1.0), w=("Vs1",))
        for n in range(4):
            for hh in range(2):
                dma("sp", TB[hh][:, :], gT_in[n, hh * 512:(hh + 1) * 512].partition_broadcast(128), r=(), w=(("T", hh),))
                P.add("dve", lambda e, n=n, hh=hh: e.tensor_copy(out=gbc[:, n, hh * 512:(hh + 1) * 512], in_=TB[hh][:, :]), r=(("T", hh),), w=("gbc",))
        dma("sp", gfin[:, :], fing.partition_broadcast(128), r=(), w=("gfin",))
        dma("sp", cfb[:, :], c_f_bias.partition_broadcast(128), r=(), w=("cfb",))
        dma("sp", lamt[:, :, :], a_lambda.partition_broadcast(128), r=(), w=("lamt",))
        P.add("dve", lambda e: e.tensor_tensor(out=lamp[:, 0, :], in0=lamt[:, 0, :], in1=lamt[:, 1, :], op=ALU.mult), r=("lamt",), w=("lamp0",))
        P.add("dve", lambda e: e.tensor_tensor(out=lamp[:, 1, :], in0=lamt[:, 2, :], in1=lamt[:, 3, :], op=ALU.mult), r=("lamt",), w=("lamp1",))
        P.add("dve", lambda e: e.reduce_sum(out=lams[:, :], in_=lamp[:, :, :], axis=AX.X), r=("lamp0", "lamp1"), w=("lams",))
        P.add("act", lambda e: e.activation(out=lams[:, :], in_=lams[:, :], func=AF.Exp), r=("lams",), w=("lams",))
        lam_init = 0.8 - 0.6 * math.exp(0.0)
        P.add("dve", lambda e: e.tensor_tensor(out=neglam[:, :], in0=lams[:, 1:2], in1=lams[:, 0:1], op=ALU.subtract), r=("lams",), w=("neglam",))
        P.add("dve", lambda e: e.tensor_scalar(out=neglam[:, :], in0=neglam[:, :], scalar1=-lam_init, scalar2=None, op0=ALU.add), r=("neglam",), w=("neglam",))
        dma("sp", subg[:, :], a_subln[:, :], r=(), w=("subg",))
        P.add("dve", lambda e: e.tensor_scalar(out=subg[:, :], in0=subg[:, :], scalar1=1.0 - lam_init, scalar2=None, op0=ALU.mult), r=("subg",), w=("subg",))
        dma("sp", spt[:, :], d_lam[:, :], r=(), w=("spt",))
        P.add("act", lambda e: e.activation(out=spt[:, :], in_=spt[:, :], func=AF.Exp, scale=-1.0), r=("spt",), w=("spt",))
        P.add("act", lambda e: e.activation(out=spt[:, :], in_=spt[:, :], func=AF.Ln, bias=1.0), r=("spt",), w=("spt",))
        P.add("dve", lambda e: e.tensor_scalar(out=m8sp[:, :], in0=spt[:, :], scalar1=-8.0, scalar2=None, op0=ALU.mult), r=("spt",), w=("m8sp",))
        P.add("dve", lambda e: e.tensor_scalar(out=m16sp[:, :], in0=spt[:, :], scalar1=-16.0, scalar2=None, op0=ALU.mult), r=("spt",), w=("m16sp",))
        dma("sp", convw[:, :, :], d_conv_w[:, :, :], r=(), w=("convw",))
        for t, src, k in ((convb, d_conv_b, "convb"), (bat, d_b_a, "bat"), (bxt, d_b_x, "bxt")):
            dma("sp", t[:, :], src[:, :], r=(), w=(k,))
        for wsrc, dst, k in ((d_w_a, bda, "bda"), (d_w_x, bdx, "bdx")):
            P.add("dve", lambda e: e.memset(bdf[:, :, :], 0.0), w=("bdf",))
            for nn in range(2):
                src = wsrc.rearrange("(c two) i j -> two i c j", two=2)[nn]
                dma("sp", bdf[nn * 64:(nn + 1) * 64, :, nn * 64:(nn + 1) * 64], src, r=(), w=("bdf",))
            P.add("dve", lambda e, dst=dst: e.tensor_copy(out=dst[:, :, :], in_=bdf[:, :, :]), r=("bdf",), w=(k,))
        dma("sp", brt[:, :], b_rel[:, :], r=(), w=("brt",))
        P.add("dve", lambda e: e.memset(txs[:, :], 0.0), w=("txs",))
        P.add("dve", lambda e: e.tensor_scalar(out=txs[:, 0:383], in0=txs[:, 0:383], scalar1=brt[:, 0:1], scalar2=None, op0=ALU.add), r=("brt", "txs"), w=("txs",))
        P.add("dve", lambda e: e.tensor_copy(out=txs[:, 383:640], in_=brt[:, 0:257]), r=("brt", "txs"), w=("txs",))
        P.add("dve", lambda e: e.tensor_scalar(out=txs[:, 640:1536], in0=txs[:, 640:1536], scalar1=brt[:, 256:257], scalar2=None, op0=ALU.add), r=("brt", "txs"), w=("txs",))
        dma("pool", text[:, :], txs[:, :], r=("txs",), w=("text",))
        for h in range(8):
            for dk, dv in enumerate(dvals):
                s = nxt("dt", 2)
                src = bass.AP(tensor=text.tensor, offset=h * 1536 + dv + 384, ap=[[1, 128], [1, TT]])
                dma("sp", dtmp[s][:, :], src, r=("text",), w=(("dtmp", s),))
                bnk = nxt("ps", 6)
                P.add("pe", lambda e, s=s, bnk=bnk: e.matmul(ps[bnk][:, :], lhsT=anti_f[:, :], rhs=dtmp[s][:, :], start=True, stop=True),
                      r=(("dtmp", s), "anti_f"), w=(("ps", bnk),))
                pslot = nxt("pt", NPT)
                dma_in = cin["bmask"]
                P.add("pool", lambda e, pslot=pslot, dk=dk: e.dma_start(out=ptr[pslot][:, :], in_=dma_in[:, dk, :]),
                      r=(), w=(("ptr", pslot),), dma=True)
                P.add("dve", lambda e, bnk=bnk, pslot=pslot: e.tensor_tensor(out=ptr[pslot][:, :], in0=ps[bnk][:, :], in1=ptr[pslot][:, :], op=ALU.add),
                      r=(("ps", bnk), ("ptr", pslot)), w=(("ptr", pslot),))
                dma("pool", bbias[h, :, dk, :], ptr[pslot][:, :], r=(("ptr", pslot),), w=("bbias",))

    def rstd_all(G):
        bs, nblk = G["bs"], G["nblk"]
        for b in range(nblk):
            P.add("dve", lambda e, b=b: e.tensor_tensor(out=sqj[0:bs, :], in0=xres[0:bs, b, :], in1=xres[0:bs, b, :], op=ALU.mult), r=(("xres", b),), w=("sqj",))
            P.add("dve", lambda e, b=b: e.reduce_sum(out=ss4[0:bs, b:b + 1], in_=sqj[0:bs, :], axis=AX.X), r=("sqj",), w=(("ss4", b),))
        P.add("dve", lambda e: e.tensor_scalar(out=ss4n[0:bs, 0:nblk], in0=ss4[0:bs, 0:nblk], scalar1=1.0 / D, scalar2=EPS, op0=ALU.mult, op1=ALU.add),
              r=tuple(("ss4", b) for b in range(nblk)), w=("ss4n",))
        P.add("act", lambda e: e.activation(out=ln4[0:bs, 0:nblk], in_=ss4n[0:bs, 0:nblk], func=AF.Ln), r=("ss4n",), w=("ln4",))
        P.add("act", lambda e: e.activation(out=rstd4[0:bs, 0:nblk], in_=ln4[0:bs, 0:nblk], func=AF.Exp, scale=-0.5), r=("ln4",), w=("rstd4",))

    def norm_T(G, nidx):
        bs, nblk, ntok = G["bs"], G["nblk"], G["ntok"]
        rstd_all(G)
        for b in range(nblk):
            P.add("dve", lambda e, b=b: e.scalar_tensor_tensor(out=xn[0:bs, :], in0=xres[0:bs, b, :], scalar=rstd4[0:bs, b:b + 1], in1=gbc[0:bs, nidx, :], op0=ALU.mult, op1=ALU.mult),
                  r=(("xres", b), "rstd4", "gbc"), w=("xn",))
            for kc in range(8):
                P.add("pe", lambda e, kc=kc: e.transpose(out=ptb[:, kc, :], in_=xn[:, kc * 128:(kc + 1) * 128], identity=ident_b[:, :]),
                      r=("xn", "ident_b"), w=(("ptb", kc),))
            evac(hT[:, :, b * bs:(b + 1) * bs], ptb[:, :, 0:bs], r=tuple(("ptb", kc) for kc in range(8)), w=(("hT", b, 0), ("hT", b, 1)))

    def hT_keys(G):
        return tuple(("hT", b, j) for b in range(G["nblk"]) for j in range(2))

    evac_tog = [0]

    def evac(out, in_, r, w, scale=None):
        evac_tog[0] ^= 1
        if evac_tog[0]:
            if scale is None:
                P.add("act", lambda e: e.activation(out=out, in_=in_, func=AF.Copy), r=r, w=w)
            else:
                P.add("act", lambda e: e.activation(out=out, in_=in_, func=AF.Copy, scale=float(scale)), r=r, w=w)
        else:
            if scale is None:
                P.add("dve", lambda e: e.tensor_copy(out=out, in_=in_), r=r, w=w)
            else:
                P.add("dve", lambda e: e.tensor_scalar(out=out, in0=in_, scalar1=float(scale), scalar2=None, op0=ALU.mult), r=r, w=w)

    def proj_units(G, ws, units, dst_fn, dst_keys_fn, scale=None):
        ntok = G["ntok"]
        for u in units:
            bnk = nxt("ps", 6)
            for kc in range(8):
                P.add("pe", lambda e, kc=kc, u=u, bnk=bnk: e.matmul(ps[bnk][0:64, 0:ntok], lhsT=wring[ws][:, kc, u * 64:(u + 1) * 64], rhs=hT[:, kc, 0:ntok],
                                                                    start=(kc == 0), stop=(kc == 7)),
                      r=(("wr", ws),) + hT_keys(G), w=(("ps", bnk),))
            evac(dst_fn(u), ps[bnk][0:64, 0:ntok], r=(("ps", bnk),), w=dst_keys_fn(u), scale=scale)

    def proj_tok(G, ws, ncols, cb):
        bs, nblk = G["bs"], G["nblk"]
        for b in range(nblk):
            bnk = nxt("ps", 6)
            for kc in range(8):
                P.add("pe", lambda e, kc=kc, b=b, bnk=bnk: e.matmul(ps[bnk][0:bs, 0:ncols], lhsT=hT[:, kc, b * bs:(b + 1) * bs], rhs=wring[ws][:, kc, 0:ncols],
                                                                    start=(kc == 0), stop=(kc == 7)),
                      r=(("wr", ws), ("hT", b, 0), ("hT", b, 1)), w=(("ps", bnk),))
            cb(b, bnk)

    def out_rows(G, name, b):
        bs = G["bs"]
        r0 = (G["orow_b"] if name in ("b_k", "b_v") else G["orow"]) + b * bs
        return O[G["g"] + name][r0:r0 + bs, :]

    def k_tok_out(G, ws, name, also=None):
        bs = G["bs"]
        if name is None and also is None:
            return

        def cb(b, bnk):
            s = nxt("kvo", 2)
            evac(kvo[s][0:bs, :], ps[bnk][0:bs, 0:512], r=(("ps", bnk),), w=(("kvo", s),))
            if name is not None:
                dma("pool", out_rows(G, name, b), kvo[s][0:bs, :], r=(("kvo", s),), w=(("out", name, G["g"], G["orow"], b),))
            if also is not None:
                also(b, bnk, s)
        proj_tok(G, ws, 512, cb)

    def v_stage(G, b, s):
        bs = G["bs"]
        P.add("dve", lambda e: e.tensor_copy(out=Vs[0:bs, b, :, 0:64], in_=kvo[s][0:bs, :].rearrange("p (u d) -> p u d", d=64)),
              r=(("kvo", s), "Vs1"), w=(("Vs", b),))

    def write_kv_scratch(G, m):
        g, bs, nblk, ntok, q0 = G["g"], G["bs"], G["nblk"], G["ntok"], G["q0"]
        kd = SCR[(g, m, "k")]
        dma("pool", kd[:, :, q0:q0 + ntok].rearrange("u d t -> d u t"), KTs[:, :, 0:ntok], r=tuple(("KTs", u) for u in range(8)), w=((g, m, "k", q0 // TT),))
        vd = SCR[(g, m, "v")]
        kt0 = q0 // 128
        vkeys = tuple(("Vs", b) for b in range(nblk))
        if m == "A":
            for h in range(4):
                src = Vs[0:bs, 0:nblk, 2 * h:2 * h + 2, :].rearrange("p b two x -> p b (two x)")
                dma("pool", vd[h, 0:bs, kt0:kt0 + nblk, :], src, r=vkeys, w=((g, m, "v", q0 // TT, h),))
        else:
            for u in range(8):
                dma("pool", vd[u, 0:bs, kt0:kt0 + nblk, :], Vs[0:bs, 0:nblk, u, :], r=vkeys, w=((g, m, "v", q0 // TT, u),))

    oset = [0]

    def attention(G, m):
        g, ntok, q0 = G["g"], G["ntok"], G["q0"]
        nq = ntok
        kd, vd = SCR[(g, m, "k")], SCR[(g, m, "v")]
        nhalf = 2 if m == "A" else 1
        xw = 65 * nhalf
        kstart = 0 if m != "B" else max(G["kmin"], q0 - 512)
        kend = q0 + ntok
        pend_fin = [None]

        def fin_recip(ob):
            P.add("dve", lambda e: e.reciprocal(out=rl[64:65, 0:nq], in_=ps[ob][64:65, 0:nq]), r=(("ps", ob),), w=("rl",))

        def finalize(u, ob, h):
            P.add("pe", lambda e: e.matmul(ps[6][0:64, 0:nq], lhsT=ones_f[64:65, 0:64], rhs=rl[64:65, 0:nq], start=True, stop=True),
                  r=("rl", "ones_f"), w=(("ps", 6),))
            P.add("act", lambda e: e.activation(out=TB[7][0:64, 0:nq], in_=ps[6][0:64, 0:nq], func=AF.Copy), r=(("ps", 6),), w=(("T", 7),))
            for hf in range(nhalf):
                bnk = ob + hf
                if m == "A":
                    cc = u % 2
                    P.add("dve", lambda e, bnk=bnk, cc=cc, hf=hf: e.tensor_tensor(out=TB[cc * 2 + hf][0:64, 0:nq], in0=ps[bnk][0:64, 0:nq], in1=TB[7][0:64, 0:nq], op=ALU.mult),
                          r=(("ps", bnk), ("T", 7)), w=(("T", cc * 2 + hf),))
                else:
                    chunk = 8 + u if m == "B" else u
                    P.add("dve", lambda e, bnk=bnk, chunk=chunk: e.tensor_tensor(out=OT[0:64, chunk, 0:nq], in0=ps[bnk][0:64, 0:nq], in1=TB[7][0:64, 0:nq], op=ALU.mult),
                          r=(("ps", bnk), ("T", 7)), w=(("GT", chunk),))
            if m == "A" and u % 2 == 1:
                for hf in range(2):
                    P.add("dve", lambda e, hf=hf: e.scalar_tensor_tensor(out=TB[4 + hf][0:64, 0:nq], in0=TB[2 + hf][0:64, 0:nq], scalar=neglam[0:64, 0:1], in1=TB[hf][0:64, 0:nq], op0=ALU.mult, op1=ALU.add),
                          r=(("T", hf), ("T", 2 + hf), "neglam"), w=(("T", 4 + hf),))
                    P.add("act", lambda e, hf=hf: e.activation(out=sq2[0:64, hf, 0:nq], in_=TB[4 + hf][0:64, 0:nq], func=AF.Square), r=(("T", 4 + hf),), w=(("sq2", hf),))
                for hf in range(2):
                    P.add("pe", lambda e, hf=hf: e.matmul(ps[6][0:64, 0:nq], lhsT=ones_b[0:64, 0:64], rhs=sq2[0:64, hf, 0:nq], start=(hf == 0), stop=(hf == 1)),
                          r=(("sq2", hf), "ones_b"), w=(("ps", 6),))
                P.add("dve", lambda e: e.tensor_scalar(out=TB[6][0:64, 0:nq], in0=ps[6][0:64, 0:nq], scalar1=1.0 / 128, scalar2=EPS, op0=ALU.mult, op1=ALU.add), r=(("ps", 6),), w=(("T", 6),))
                P.add("act", lambda e: e.activation(out=TB[6][0:64, 0:nq], in_=TB[6][0:64, 0:nq], func=AF.Ln), r=(("T", 6),), w=(("T", 6),))
                P.add("act", lambda e: e.activation(out=TB[6][0:64, 0:nq], in_=TB[6][0:64, 0:nq], func=AF.Exp, scale=-0.5), r=(("T", 6),), w=(("T", 6),))
                for hf in range(2):
                    P.add("dve", lambda e, hf=hf, h=h: e.scalar_tensor_tensor(out=OT[0:64, 2 * h + hf, 0:nq], in0=TB[4 + hf][0:64, 0:nq], scalar=subg[0:64, hf:hf + 1], in1=TB[6][0:64, 0:nq], op0=ALU.mult, op1=ALU.mult),
                          r=(("T", 4 + hf), ("T", 6), "subg"), w=(("GT", 2 * h + hf),))

        def flush_fin():
            if pend_fin[0] is not None:
                f = pend_fin[0]
                pend_fin[0] = None
                f()

        for u in range(8):
            ob = 2 + 2 * oset[0]
            oset[0] ^= 1
            h = u // 2 if m == "A" else u
            vh = h if m == "A" else u
            if m == "B":
                s_bb = 0
                for k0_ in range(kstart, kend, 128):
                    dk_ = (512 - (q0 - k0_)) // 128
                    dma("sp", bbr[0][:, dk_, :], bbias[u, :, dk_, :], r=("bbias",), w=(("bbr", dk_),))
            first = True
            c0 = (kstart // CH) * CH
            chunks = []
            while c0 < kend:
                chunks.append((max(c0, kstart), min(c0 + CH, kend)))
                c0 += CH
            pend_q = []
            sbanks = (0, 1) if nhalf == 2 else (0, 1, 3, 5)
            depth = len(sbanks) // 2
            ntile = 0
            for (lo, hi) in chunks:
                s = nxt("k", NKR)
                dkeys = tuple((g, m, "k", t) for t in range(lo // TT, (hi - 1) // TT + 1))
                dma("sp", kring[s][0:64, 0:hi - lo], kd[u, :, lo:hi], r=dkeys, w=(("kr", s),))
                kt_lo, kt_hi = lo // 128, (hi + 127) // 128
                vkeys = tuple((g, m, "v", t, vh) for t in range(lo // TT, (hi - 1) // TT + 1))
                dma("sp", vring[s][:, 0:kt_hi - kt_lo, 0:xw], vd[vh, :, kt_lo:kt_hi, :], r=vkeys, w=(("vr", s),))
                k0 = lo
                while k0 < hi:
                    nk = min(128, hi - k0)
                    off = k0 - lo
                    kt = k0 // 128 - kt_lo
                    last = (k0 + nk >= kend)
                    sbk = sbanks[nxt("sps%d" % len(sbanks), len(sbanks))]
                    P.add("pe", lambda e, s=s, off=off, nk=nk, sbk=sbk, u=u: e.matmul(ps[sbk][0:nk, 0:nq], lhsT=kring[s][0:128, off:off + nk], rhs=QT[m][0:128, u, 0:nq], start=True, stop=True),
                          r=(("kr", s), ("kr1", s), ("QT", u), "QT_aug", "QT_z"), w=(("ps", sbk),))
                    diag = k0 >= q0
                    bias = 0.0
                    src = ps[sbk][0:nk, 0:nq]
                    srck = (("ps", sbk),)
                    if m == "A":
                        if diag:
                            js = (k0 - q0) // 128
                            ds = nxt("dt", 2)
                            P.add("dve", lambda e, js=js, ds=ds, nk=nk, sbk=sbk, h=h: e.scalar_tensor_tensor(out=dtmp[ds][0:nk, 0:nq], in0=t0m[0:nk, js, 0:nq], scalar=float(slopes[h]), in1=ps[sbk][0:nk, 0:nq], op0=ALU.mult, op1=ALU.add),
                                  r=(("ps", sbk), "t0m"), w=(("dtmp", ds),))
                            src = dtmp[ds][0:nk, 0:nq]; srck = (("dtmp", ds),)
                        else:
                            d128 = (q0 - k0) // 128
                            bias = bkA[0:nk, h, d128:d128 + 1]
                    elif m == "C":
                        ktabs = k0 // 128
                        bias = negF[g][0:nk, ktabs, u:u + 1]
                        if diag:
                            js = (k0 - q0) // 128
                            ds = nxt("dt", 2)
                            P.add("dve", lambda e, js=js, ds=ds, nk=nk, sbk=sbk: e.tensor_tensor(out=dtmp[ds][0:nk, 0:nq], in0=ps[sbk][0:nk, 0:nq], in1=cmask[0:nk, js, 0:nq], op=ALU.add),
                                  r=(("ps", sbk), "cmask"), w=(("dtmp", ds),))
                            src = dtmp[ds][0:nk, 0:nq]; srck = (("dtmp", ds),)
                    else:
                        dk = (512 - (q0 - k0)) // 128
                        ds = nxt("dt", 2)
                        P.add("dve", lambda e, dk=dk, ds=ds, nk=nk, sbk=sbk, s_bb=s_bb: e.tensor_tensor(out=dtmp[ds][0:nk, 0:nq], in0=ps[sbk][0:nk, 0:nq], in1=bbr[s_bb][0:nk, dk, 0:nq], op=ALU.add),
                              r=(("ps", sbk), ("bbr", dk)), w=(("dtmp", ds),))
                        src = dtmp[ds][0:nk, 0:nq]; srck = (("dtmp", ds),)
                    pslot = nxt("pt", NPT)
                    bkeys = ("bkA", ("negF", g)) if not isinstance(bias, float) else ()
                    P.add("act", lambda e, pslot=pslot, src=src, bias=bias, nk=nk: e.activation(out=ptr[pslot][0:nk, 0:nq], in_=src, func=AF.Exp, bias=bias),
                          r=srck + bkeys, w=(("ptr", pslot),))

                    def pv(s=s, kt=kt, nk=nk, pslot=pslot, first=first, last=last, ob=ob):
                        for hf in range(nhalf):
                            if hf == 0:
                                P.add("pe", lambda e: e.matmul(ps[ob][0:128, 0:nq], lhsT=vring[s][0:nk, kt, 0:128], rhs=ptr[pslot][0:nk, 0:nq], start=first, stop=last),
                                      r=(("vr", s), ("vr0", s), ("ptr", pslot)), w=(("ps", ob),))
                            else:
                                P.add("pe", lambda e: e.matmul(ps[ob + 1][0:65, 0:nq], lhsT=vring[s][0:nk, kt, 65:130], rhs=ptr[pslot][0:nk, 0:nq], start=first, stop=last),
                                      r=(("vr", s), ("vr0", s), ("ptr", pslot)), w=(("ps", ob + 1),))
                    pend_q.append(pv)
                    if len(pend_q) > depth:
                        pend_q.pop(0)()
                    ntile += 1
                    if ntile == 5:
                        flush_fin()
                    first = False
                    k0 += nk
            while pend_q:
                pend_q.pop(0)()
            flush_fin()
            fin_recip(ob)
            pend_fin[0] = (lambda u=u, ob=ob, h=h: finalize(u, ob, h))
        flush_fin()

    def out_proj(G, panels):
        bs, nblk = G["bs"], G["nblk"]
        for nh in range(2):
            total = sum(pn[2] for pn in panels[nh])
            cnt = [0] * nblk
            for (wkey, npart, nkc, lhs_fn) in panels[nh]:
                ws = load_w(wkey)
                for b in range(nblk):
                    for kc in range(nkc):
                        ap, keys = lhs_fn(kc, b)
                        st = (cnt[b] == 0)
                        cnt[b] += 1
                        sp_ = (cnt[b] == total)
                        P.add("pe", lambda e, ap=ap, ws=ws, kc=kc, b=b, st=st, sp_=sp_, npart=npart: e.matmul(ps[2 + b][0:bs, 0:512], lhsT=ap, rhs=wring[ws][0:npart, kc, 0:512], start=st, stop=sp_),
                              r=(("wr", ws),) + keys, w=(("ps", 2 + b),))
            for b in range(nblk):
                P.add("dve", lambda e, b=b, nh=nh: e.tensor_tensor(out=xres[0:bs, b, nh * 512:(nh + 1) * 512], in0=ps[2 + b][0:bs, 0:512], in1=xres[0:bs, b, nh * 512:(nh + 1) * 512], op=ALU.add),
                      r=(("ps", 2 + b), ("xres", b)), w=(("xres", b),))

    def ffn(G, L):
        bs, nblk, ntok = G["bs"], G["nblk"], G["ntok"]
        norm_T(G, 2 * L + 1)
        for pi, (c0, cn) in enumerate(HP):
            wa = load_w("w1_%d_%d" % (L, pi))
            wb = load_w("w3_%d_%d" % (L, pi))
            for j in range(cn // 128):
                hc = c0 // 128 + j
                b1 = nxt("ps", 6)
                for kc in range(8):
                    P.add("pe", lambda e, kc=kc, j=j, b1=b1, wa=wa: e.matmul(ps[b1][:, 0:ntok], lhsT=wring[wa][:, kc, j * 128:(j + 1) * 128], rhs=hT[:, kc, 0:ntok], start=(kc == 0), stop=(kc == 7)),
                          r=(("wr", wa),) + hT_keys(G), w=(("ps", b1),))
                b3 = nxt("ps", 6)
                for kc in range(8):
                    P.add("pe", lambda e, kc=kc, j=j, b3=b3, wb=wb: e.matmul(ps[b3][:, 0:ntok], lhsT=wring[wb][:, kc, j * 128:(j + 1) * 128], rhs=hT[:, kc, 0:ntok], start=(kc == 0), stop=(kc == 7)),
                          r=(("wr", wb),) + hT_keys(G), w=(("ps", b3),))
                ds = nxt("dt", 2)
                P.add("act", lambda e, b1=b1, ds=ds: e.activation(out=dtmp[ds][:, 0:ntok], in_=ps[b1][:, 0:ntok], func=AF.Silu), r=(("ps", b1),), w=(("dtmp", ds),))
                P.add("dve", lambda e, b3=b3, ds=ds, hc=hc: e.tensor_tensor(out=GT[:, hc, 0:ntok], in0=ps[b3][:, 0:ntok], in1=dtmp[ds][:, 0:ntok], op=ALU.mult),
                      r=(("ps", b3), ("dtmp", ds)), w=(("GT", hc),))
        panels = []
        for nh in range(2):
            pl = []
            for kp, (k0, kn) in enumerate(KP2):
                pl.append(("w2_%d_%d_%d" % (L, nh, kp), 128, kn,
                           lambda kc, b, k0=k0: (GT[:, k0 + kc, b * bs:(b + 1) * bs], (("GT", k0 + kc),))))
            panels.append(pl)
        out_proj(G, panels)

    def layer0(G):
        g, bs, nblk, ntok = G["g"], G["bs"], G["nblk"], G["ntok"]
        chk(3.1)
        norm_T(G, 0)
        chk(3.2)
        for m, base in (("A", 0), ("B", 3)):
            ws = load_w("abin%d" % (base + 1))
            proj_units(G, ws, range(8), lambda u: KTs[0:64, u, 0:ntok], lambda u: (("KTs", u),))
            kname = ("a_k" if m == "A" else "b_k")
            vname = ("a_v" if m == "A" else "b_v")
            if m == "B" and g == "p":
                kname = kname if G["last"] else None
                vname = vname if G["last"] else None
            k_tok_out(G, ws, kname)
            ws = load_w("abin%d" % (base + 2))
            k_tok_out(G, ws, vname, also=lambda b, bnk, s: v_stage(G, b, s))
            write_kv_scratch(G, m)
        for m, base in (("A", 0), ("B", 3)):
            ws = load_w("abin%d" % base)
            proj_units(G, ws, range(8), lambda u, m=m: QT[m][0:64, u, 0:ntok], lambda u, m=m: (("QT", u),), scale=0.125)
            if m == "A":
                dma("sp", QT1[64:65, :, :], cin["cqA"][:, :, :], r=(), w=("QT_aug",))
            else:
                P.add("pool", lambda e: e.memset(QT1[64:65, :, :], 0.0), w=("QT_aug",))
            attention(G, m)
        panels = []
        for nh in range(2):
            pl = []
            for kp in range(2):
                pl.append(("about%d_%d" % (nh, kp), 64, 8,
                           lambda kc, b, kp=kp: (OT[0:64, kp * 8 + kc, b * bs:(b + 1) * bs], (("GT", kp * 8 + kc),))))
            panels.append(pl)
        out_proj(G, panels)
        ffn(G, 0)

    def logf_block(G, b, bnk):
        g, bs = G["g"], G["bs"]
        P.add("dve", lambda e: e.tensor_tensor(out=lft[0:bs, :], in0=ps[bnk][0:bs, 0:8], in1=cfb[0:bs, :], op=ALU.add), r=(("ps", bnk), "cfb"), w=("lft",))
        P.add("act", lambda e: e.activation(out=lft[0:bs, :], in_=lft[0:bs, :], func=AF.Exp, scale=-1.0), r=("lft",), w=("lft",))
        P.add("act", lambda e: e.activation(out=lft[0:bs, :], in_=lft[0:bs, :], func=AF.Ln, bias=1.0), r=("lft",), w=("lft",))
        P.add("dve", lambda e: e.tensor_scalar(out=lfo[0:bs, :], in0=lft[0:bs, :], scalar1=-1.0, scalar2=None, op0=ALU.mult), r=("lft",), w=("lfo",))
        dma("pool", out_rows(G, "c_logf", b), lfo[0:bs, :], r=("lfo",), w=(("out", "c_logf", G["g"], G["orow"], b),))
        cum_block(G, G["q0"] // 128 + b, bs, b)

    def cum_block(G, ktabs, bs, b=None):
        g = G["g"]
        has_prev = ktabs > 0
        P.add("pe", lambda e: e.matmul(ps[6][0:bs, 0:8], lhsT=tri_f[0:bs, 0:bs], rhs=lft[0:bs, :], start=True, stop=not has_prev), r=("lft", "tri_f"), w=(("ps", 6),))
        if has_prev:
            P.add("pe", lambda e: e.matmul(ps[6][0:bs, 0:8], lhsT=sel_f[0:128, 0:bs], rhs=negF[g][0:128, ktabs - 1, :], start=False, stop=True),
                  r=(("negF", g), "sel_f"), w=(("ps", 6),))
        P.add("dve", lambda e: e.tensor_copy(out=negF[g][0:bs, ktabs, :], in_=ps[6][0:bs, 0:8]), r=(("ps", 6),), w=(("negF", g),))
        if b is not None:
            P.add("dve", lambda e: e.tensor_scalar(out=lfb[0:bs, :], in0=negF[g][0:bs, ktabs, :], scalar1=-1.0, scalar2=None, op0=ALU.mult), r=(("negF", g),), w=("lfb",))
            P.add("pe", lambda e: e.transpose(out=ptb[0:8, 0, :], in_=lfb[:, :], identity=ident_b[:, :]), r=("lfb", "ident_b"), w=(("ptb", 0),))
            P.add("act", lambda e: e.activation(out=FTb[0:8, b * bs:(b + 1) * bs], in_=ptb[0:8, 0, 0:bs], func=AF.Copy), r=(("ptb", 0),), w=("FTb",))

    def dbranch(G, wsx, wsg):
        g, bs, nblk, ntok = G["g"], G["bs"], G["nblk"], G["ntok"]
        xb = dxbuf[g]
        u_, r_, i_, a_, b_, x_, t_ = TB[0:7]
        for cb in range(4):
            bx = nxt("ps", 6)
            for kc in range(8):
                P.add("pe", lambda e, kc=kc, bx=bx, cb=cb: e.matmul(ps[bx][:, 0:ntok], lhsT=wring[wsx][:, kc, cb * 128:(cb + 1) * 128], rhs=hT[:, kc, 0:ntok], start=(kc == 0), stop=(kc == 7)),
                      r=(("wr", wsx),) + hT_keys(G), w=(("ps", bx),))
            bg = nxt("ps", 6)
            for kc in range(8):
                P.add("pe", lambda e, kc=kc, bg=bg, cb=cb: e.matmul(ps[bg][:, 0:ntok], lhsT=wring[wsg][:, kc, cb * 128:(cb + 1) * 128], rhs=hT[:, kc, 0:ntok], start=(kc == 0), stop=(kc == 7)),
                      r=(("wr", wsg),) + hT_keys(G), w=(("ps", bg),))
            kx = ("dxbuf", g, cb)
            P.add("dve", lambda e, bx=bx, cb=cb: e.tensor_copy(out=xb[:, cb, 3:3 + ntok], in_=ps[bx][:, 0:ntok]), r=(("ps", bx),), w=(kx,))
            P.add("dve", lambda e, cb=cb: e.tensor_scalar(out=u_[:, 0:ntok], in0=xb[:, cb, 3:3 + ntok], scalar1=convw[:, cb, 3:4], scalar2=convb[:, cb:cb + 1], op0=ALU.mult, op1=ALU.add),
                  r=(kx, "convw", "convb"), w=(("T", 0),))
            for j in range(3):
                P.add("dve", lambda e, cb=cb, j=j: e.scalar_tensor_tensor(out=u_[:, 0:ntok], in0=xb[:, cb, j:j + ntok], scalar=convw[:, cb, j:j + 1], in1=u_[:, 0:ntok], op0=ALU.mult, op1=ALU.add),
                      r=(kx, "convw", ("T", 0)), w=(("T", 0),))
            P.add("dve", lambda e, cb=cb: e.tensor_copy(out=xb[:, cb, 0:3], in_=xb[:, cb, ntok:ntok + 3]), r=(kx, ("T", 0)), w=(kx,))
            P.add("act", lambda e: e.activation(out=ub[:, 0:ntok], in_=u_[:, 0:ntok], func=AF.Copy), r=(("T", 0),), w=("ub",))
            ba = nxt("ps", 6)
            P.add("pe", lambda e, ba=ba, cb=cb: e.matmul(ps[ba][:, 0:ntok], lhsT=bda[:, cb, :], rhs=ub[:, 0:ntok], start=True, stop=True), r=("ub", "bda"), w=(("ps", ba),))
            bi = nxt("ps", 6)
            P.add("pe", lambda e, bi=bi, cb=cb: e.matmul(ps[bi][:, 0:ntok], lhsT=bdx[:, cb, :], rhs=ub[:, 0:ntok], start=True, stop=True), r=("ub", "bdx"), w=(("ps", bi),))
            P.add("act", lambda e, ba=ba, cb=cb: e.activation(out=r_[:, 0:ntok], in_=ps[ba][:, 0:ntok], func=AF.Sigmoid, bias=bat[:, cb:cb + 1]), r=(("ps", ba), "bat"), w=(("T", 1),))
            P.add("act", lambda e, bi=bi, cb=cb: e.activation(out=i_[:, 0:ntok], in_=ps[bi][:, 0:ntok], func=AF.Sigmoid, bias=bxt[:, cb:cb + 1]), r=(("ps", bi), "bxt"), w=(("T", 2),))
            P.add("act", lambda e, cb=cb: e.activation(out=a_[:, 0:ntok], in_=r_[:, 0:ntok], func=AF.Exp, scale=m8sp[:, cb:cb + 1]), r=(("T", 1), "m8sp"), w=(("T", 3),))
            P.add("act", lambda e, cb=cb: e.activation(out=b_[:, 0:ntok], in_=r_[:, 0:ntok], func=AF.Exp, scale=m16sp[:, cb:cb + 1]), r=(("T", 1), "m16sp"), w=(("T", 4),))
            P.add("dve", lambda e: e.tensor_scalar(out=b_[:, 0:ntok], in0=b_[:, 0:ntok], scalar1=-1.0, scalar2=1.0, op0=ALU.mult, op1=ALU.add), r=(("T", 4),), w=(("T", 4),))
            P.add("act", lambda e: e.activation(out=b_[:, 0:ntok], in_=b_[:, 0:ntok], func=AF.Ln), r=(("T", 4),), w=(("T", 4),))
            P.add("act", lambda e: e.activation(out=b_[:, 0:ntok], in_=b_[:, 0:ntok], func=AF.Exp, scale=0.5), r=(("T", 4),), w=(("T", 4),))
            P.add("dve", lambda e: e.tensor_tensor(out=i_[:, 0:ntok], in0=i_[:, 0:ntok], in1=u_[:, 0:ntok], op=ALU.mult), r=(("T", 2), ("T", 0)), w=(("T", 2),))
            P.add("dve", lambda e: e.tensor_tensor(out=b_[:, 0:ntok], in0=b_[:, 0:ntok], in1=i_[:, 0:ntok], op=ALU.mult), r=(("T", 2), ("T", 4)), w=(("T", 4),))
            kh = ("hst", g)
            P.add("dve", lambda e, cb=cb: e.tensor_tensor_scan(out=r_[:, 0:ntok], data0=a_[:, 0:ntok], data1=b_[:, 0:ntok], initial=hst[g][:, cb:cb + 1], op0=ALU.mult, op1=ALU.add),
                  r=(("T", 3), ("T", 4), kh), w=(("T", 1),))
            P.add("dve", lambda e, cb=cb: e.tensor_copy(out=hst[g][:, cb:cb + 1], in_=r_[:, ntok - 1:ntok]), r=(("T", 1),), w=(kh,))
            P.add("act", lambda e, bg=bg: e.activation(out=x_[:, 0:ntok], in_=ps[bg][:, 0:ntok], func=AF.Copy), r=(("ps", bg),), w=(("T", 5),))
            P.add("dve", lambda e: e.tensor_tensor(out=t_[:, 0:ntok], in0=x_[:, 0:ntok], in1=x_[:, 0:ntok], op=ALU.mult), r=(("T", 5),), w=(("T", 6),))
            P.add("dve", lambda e: e.tensor_scalar(out=t_[:, 0:ntok], in0=t_[:, 0:ntok], scalar1=0.044715, scalar2=1.0, op0=ALU.mult, op1=ALU.add), r=(("T", 6),), w=(("T", 6),))
            P.add("dve", lambda e: e.tensor_tensor(out=t_[:, 0:ntok], in0=t_[:, 0:ntok], in1=x_[:, 0:ntok], op=ALU.mult), r=(("T", 6), ("T", 5)), w=(("T", 6),))
            P.add("act", lambda e: e.activation(out=t_[:, 0:ntok], in_=t_[:, 0:ntok], func=AF.Sigmoid, scale=1.5957691216057308), r=(("T", 6),), w=(("T", 6),))
            P.add("dve", lambda e: e.tensor_tensor(out=t_[:, 0:ntok], in0=t_[:, 0:ntok], in1=x_[:, 0:ntok], op=ALU.mult), r=(("T", 6), ("T", 5)), w=(("T", 6),))
            P.add("dve", lambda e, cb=cb: e.tensor_tensor(out=odT[:, cb, 0:ntok], in0=t_[:, 0:ntok], in1=r_[:, 0:ntok], op=ALU.mult), r=(("T", 6), ("T", 1)), w=(("odT", cb),))
        if G["last"]:
            dma("pool", O[g + "d_conv"][:, :, :], xb[:, :, 0:3], r=tuple(("dxbuf", g, c) for c in range(4)), w=(("out", "d_conv"),))
            dma("pool", O[g + "d_h"][:, :], hst[g][:, :], r=(("hst", g),), w=(("out", "d_h"),))

    def layer1(G):
        g, bs, nblk, ntok = G["g"], G["bs"], G["nblk"], G["ntok"]
        norm_T(G, 2)
        ws = load_w("cdin1")
        proj_units(G, ws, range(8), lambda u: KTs[0:64, u, 0:ntok], lambda u: (("KTs", u),))
        k_tok_out(G, ws, "c_k")
        ws = load_w("cdin2")
        k_tok_out(G, ws, "c_v", also=lambda b, bnk, s: v_stage(G, b, s))
        write_kv_scratch(G, "C")
        ws = load_w("cdin0")
        proj_units(G, ws, range(8), lambda u: QT["C"][0:64, u, 0:ntok], lambda u: (("QT", u),), scale=0.125)
        ws = load_w("cdcf")
        proj_tok(G, ws, 8, lambda b, bnk: logf_block(G, b, bnk))
        dma("sp", QT["C"][64:65, :, 0:ntok], FTb[0:8, 0:ntok], r=("FTb",), w=("QT_aug",))
        attention(G, "C")
        chk(8)
        wsx = load_w("cdin3")
        wsg = load_w("cdin4")
        dbranch(G, wsx, wsg)
        panels = []
        for nh in range(2):
            pl = [("cdoutc%d" % nh, 64, 8, lambda kc, b: (OT[0:64, kc, b * bs:(b + 1) * bs], (("GT", kc),))),
                  ("cdoutd%d" % nh, 128, 4, lambda kc, b: (odT[:, kc, b * bs:(b + 1) * bs], (("odT", kc),)))]
            panels.append(pl)
        out_proj(G, panels)
        ffn(G, 1)

    def final_norm(G):
        g, bs, nblk = G["g"], G["bs"], G["nblk"]
        rstd_all(G)
        for b in range(nblk):
            for nh in range(2):
                ti = (2 * b + nh) % 8
                P.add("dve", lambda e, b=b, nh=nh, ti=ti: e.scalar_tensor_tensor(out=TB[ti][0:bs, :], in0=xres[0:bs, b, nh * 512:(nh + 1) * 512], scalar=rstd4[0:bs, b:b + 1], in1=gfin[0:bs, nh * 512:(nh + 1) * 512], op0=ALU.mult, op1=ALU.mult),
                      r=(("xres", b), "rstd4", "gfin"), w=(("T", ti),))
                dma("pool", out_rows(G, "y", b)[:, nh * 512:(nh + 1) * 512], TB[ti][0:bs, :], r=(("T", ti),), w=(("out", "y", G["g"], G["orow"], b, nh),))

    def run_tile(G):
        bs, nblk = G["bs"], G["nblk"]
        src = G["x"]
        for b in range(nblk):
            dma("sp", xres[0:bs, b, :], src[b * bs:(b + 1) * bs, :], r=(), w=(("xres", b),))
        layer0(G)
        chk(7)
        layer1(G)
        chk(9)
        final_norm(G)

    def ingest(m, csrc_k, csrc_v, ntok_c, pos0, roll=None):
        for t0 in range(0, ntok_c, TT):
            q0 = pos0 + t0
            Gc = {"g": "s", "bs": 128, "nblk": 4, "ntok": TT, "q0": q0}
            for b in range(4):
                tk = b % 2
                tv = 2 + b % 2
                r0 = t0 + b * 128
                dma("sp", TB[tk][:, :], csrc_k[r0:r0 + 128, :], r=(), w=(("T", tk),))
                dma("sp", TB[tv][:, :], csrc_v[r0:r0 + 128, :], r=(), w=(("T", tv),))
                import os
                rmode = os.environ.get("ROLLMODE", "1")
                if roll is not None and rmode != "0":
                    p0 = DEC if r0 == 0 else 0
                    for (Oo, tt) in ((roll[0], tk), (roll[1], tv)):
                        if rmode == "2":
                            if r0 == 0:
                                dma("pool", O["sa_k"][0:64, :], TB[tt][64:128, :], r=(("T", tt),), w=(("out", "roll"),))
                        elif rmode == "3":
                            dma("sp", Oo[r0 + p0 - DEC:r0 + 128 - DEC, :], TB[tt][p0:128, :], r=(("T", tt),), w=(("out", "roll"),))
                        else:
                            dma("pool", Oo[r0 + p0 - DEC:r0 + 128 - DEC, :], TB[tt][p0:128, :], r=(("T", tt),), w=(("out", "roll"),))
                P.add("dve", lambda e, tk=tk: e.tensor_copy(out=xn[:, 0:512], in_=TB[tk][:, :]), r=(("T", tk),), w=("xn",))
                for u in range(8):
                    P.add("pe", lambda e, u=u: e.transpose(out=ptb[0:64, u, :], in_=xn[:, u * 64:(u + 1) * 64], identity=ident_b[:, :]),
                          r=("xn", "ident_b"), w=(("ptb", u),))
                evac(KTs[0:64, :, b * 128:(b + 1) * 128], ptb[0:64, :, :], r=tuple(("ptb", u) for u in range(8)), w=tuple(("KTs", u) for u in range(8)))
                P.add("dve", lambda e, b=b, tv=tv: e.tensor_copy(out=Vs[:, b, :, 0:64], in_=TB[tv][:, :].rearrange("p (u d) -> p u d", d=64)), r=(("T", tv), "Vs1"), w=(("Vs", b),))
            write_kv_scratch(Gc, m)

    epsc = sb("epsc", [128, 1])
    def main_seq(chk):
        chk(0)
        prologue()
        chk(1)

        Gs = {"g": "s", "bs": DEC, "nblk": 1, "ntok": DEC, "q0": PAST, "orow": 0, "last": True, "x": x_s, "kmin": PAST - 512}
        ingest("A", cak, cav, PAST, 0)
        ingest("B", cbk, cbv, 512, PAST - 512, roll=(O["sb_k"], O["sb_v"]))
        ingest("C", cck, ccv, PAST, 0)
        chk(2)
        nb_c = PAST // 128
        lcs = sb("lcs", [128, nb_c, 8])
        dma("sp", lcs[:, :, :], cclf.rearrange("(b p) h -> p b h", p=128), r=(), w=("lcs",))
        for b in range(nb_c):
            P.add("dve", lambda e, b=b: e.tensor_scalar(out=lft[:, :], in0=lcs[:, b, :], scalar1=-1.0, scalar2=None, op0=ALU.mult), r=("lcs",), w=("lft",))
            cum_block(Gs, b, 128)
        chk(2.3)
        dma("sp", dxbuf["s"][:, :, 0:3], sdc[:, :, :], r=(), w=tuple(("dxbuf", "s", c) for c in range(4)))
        dma("sp", hst["s"][:, :], sdh[:, :], r=(), w=(("hst", "s"),))
        chk(2.6)
        Gs["orow_b"] = 448
        chk(3)
        import os
        if os.environ.get("SKIPS", "0") != "1":
            run_tile(Gs)
            chk(10)

        for cb in range(4):
            P.add("dve", lambda e, cb=cb: e.memset(dxbuf["p"][:, cb, 0:3], 0.0), w=(("dxbuf", "p", cb),))
        P.add("dve", lambda e: e.memset(hst["p"][:, :], 0.0), w=(("hst", "p"),))
        for t in range(NT):
            Gp = {"g": "p", "bs": 128, "nblk": 4, "ntok": TT, "q0": t * TT, "orow": t * TT, "last": t == NT - 1,
                  "x": x_p[t * TT:(t + 1) * TT, :], "kmin": 0, "orow_b": 0}
            run_tile(Gp)


    P.add("pool", lambda e: e.memset(epsc[:, :], EPS), w=("epsc",))
    P.add("pool", lambda e: e.memset(xn[:, :], 0.0), w=("xn",))
    P.add("pool", lambda e: e.memset(lfb[:, :], 0.0), w=("lfb",))
    try:
        main_seq(chk)
    except _Stop:
        pass

    P.emit()
    return nc, consts


_NAMES_B = ("b_k", "b_v")


_CACHE = {}


def _get_prog(SEQ, PAST):
    key = (SEQ, PAST)
    if key not in _CACHE:
        _CACHE[key] = build(SEQ, PAST)
    return _CACHE[key]


def kernel(**inp):
    x_prompt = np.asarray(inp["x_prompt"]); x_sample = np.asarray(inp["x_sample"])
    NB, SEQ, _ = x_prompt.shape
    NS = x_sample.shape[0]
    PAST = inp["cache_a_k"].shape[1]
    nc, consts = _get_prog(SEQ, PAST)
    f = lambda a: np.ascontiguousarray(np.asarray(a, dtype=np.float32))
    shared = {}
    for k in ("final_g", "ab_w_in", "ab_w_out", "a_lambda", "b_rel_bias", "cd_w_in",
              "cd_w_out", "c_f_bias", "d_w_a", "d_w_x", "ffn_w1", "ffn_w3", "ffn_w2"):
        shared[k] = f(inp[k])
    pc = lambda v: f(np.asarray(v, np.float32).reshape(4, 128).T)
    shared["d_b_a"] = pc(inp["d_b_a"]); shared["d_b_x"] = pc(inp["d_b_x"])
    shared["d_conv_b"] = pc(inp["d_conv_b"]); shared["d_lambda"] = pc(inp["d_lambda"])
    shared["d_conv_w"] = f(np.asarray(inp["d_conv_w"], np.float32).reshape(4, 4, 128).transpose(2, 1, 0))
    shared["a_subln_g"] = f(np.asarray(inp["a_subln_g"], np.float32).reshape(2, 64).T)
    gs = np.stack([np.asarray(inp["norm_mix_g"])[0], np.asarray(inp["norm_ffn_g"])[0],
                   np.asarray(inp["norm_mix_g"])[1], np.asarray(inp["norm_ffn_g"])[1]], 0).astype(np.float32)
    shared["gT_in"] = f(gs)
    for k, v in consts.items():
        shared["c_" + k] = v
    in_maps = []
    for c in range(8):
        m = dict(shared)
        m["x_p"] = f(x_prompt[c % NB]); m["x_s"] = f(x_sample[c % NS])
        s = c % NS
        m["cache_a_k"] = f(inp["cache_a_k"][s]).reshape(PAST, 512); m["cache_a_v"] = f(inp["cache_a_v"][s]).reshape(PAST, 512)
        m["cache_b_k"] = f(inp["cache_b_k"][s]).reshape(512, 512); m["cache_b_v"] = f(inp["cache_b_v"][s]).reshape(512, 512)
        m["cache_c_k"] = f(inp["cache_c_k"][s]).reshape(PAST, 512); m["cache_c_v"] = f(inp["cache_c_v"][s]).reshape(PAST, 512)
        m["cache_c_logf"] = f(inp["cache_c_logf"][s])
        m["state_d_conv"] = f(np.asarray(inp["state_d_conv"][s], np.float32).reshape(3, 4, 128).transpose(2, 1, 0))
        m["state_d_h"] = f(np.asarray(inp["state_d_h"][s], np.float32).reshape(4, 128).T)
        in_maps.append(m)
    res = run_bass_kernel_spmd(nc, in_maps, core_ids=list(range(8)))
    R = res.results

    def stack(name, cores, shape):
        if name.endswith("d_conv"):
            return np.stack([np.asarray(R[c][name], dtype=np.float32).reshape(128, 4, 3).transpose(2, 1, 0).reshape(3, 512) for c in cores], axis=0)
        if name.endswith("d_h"):
            return np.stack([np.asarray(R[c][name], dtype=np.float32).reshape(128, 4).T.reshape(512) for c in cores], axis=0)
        return np.stack([np.asarray(R[c][name], dtype=np.float32).reshape(shape) for c in cores], axis=0)
    pc = list(range(NB)); sc = list(range(NS))
    outs = (
        stack("p_y", pc, (SEQ, D)), stack("s_y", sc, (DEC, D)),
        stack("p_a_k", pc, (SEQ, 4, 128)), stack("p_a_v", pc, (SEQ, 4, 128)),
        stack("p_b_k", pc, (512, 8, 64)), stack("p_b_v", pc, (512, 8, 64)),
        stack("p_c_k", pc, (SEQ, 8, 64)), stack("p_c_v", pc, (SEQ, 8, 64)),
        stack("p_c_logf", pc, (SEQ, 8)), stack("p_d_conv", pc, (3, 512)), stack("p_d_h", pc, (512,)),
        stack("s_a_k", sc, (DEC, 4, 128)), stack("s_a_v", sc, (DEC, 4, 128)),
        stack("s_b_k", sc, (512, 8, 64)), stack("s_b_v", sc, (512, 8, 64)),
        stack("s_c_k", sc, (DEC, 8, 64)), stack("s_c_v", sc, (DEC, 8, 64)),
        stack("s_c_logf", sc, (DEC, 8)), stack("s_d_conv", sc, (3, 512)), stack("s_d_h", sc, (512,)),
    )
    return outs
```
